# Optimizing a Trainium2 kernel written in Bass

```python
import math
import jax, jax.numpy as jnp
from jax import lax
import numpy as np

D_MODEL = 1024
BATCH = 4
SEQ = 4096
DEPTH = 2
DEC_BATCH = 32
DEC_SEQ = 8
PAST_LEN = 8192
PAGE_SIZE = 128

N_HEADS = 8
HEAD_DIM = 64
ROT_DIM = HEAD_DIM // 4
ROPE_THETA = 500000.0
CONV_WIDTH = 3
D_FF = -(-8 * D_MODEL // (3 * 256)) * 256
Q_BLOCK = 128
LN_EPS = 1e-5
SUBLN_EPS = 1e-5
ALPHA = (2 * DEPTH) ** 0.25
BETA = (8 * DEPTH) ** -0.25
SCALE = HEAD_DIM ** -0.5
ATTN_LAYER = 1

kernel_name = "hybrid_shortconv_diffattn_decoder_step"


def _lambda_init(layer):
    return 0.8 - 0.6 * math.exp(-0.3 * layer)


def _layernorm(x, g, b):
    xf = x.astype(jnp.float32)
    mu = jnp.mean(xf, axis=-1, keepdims=True)
    xc = xf - mu
    var = jnp.mean(xc * xc, axis=-1, keepdims=True)
    return (xc * lax.rsqrt(var + LN_EPS) * g.astype(jnp.float32) + b.astype(jnp.float32)).astype(x.dtype)


def _short_conv(x, buf, w_in, w_taps, w_out):
    S = x.shape[1]
    gb, gc, h = jnp.split(x @ w_in, 3, axis=-1)
    u = gc * h
    u_pad = jnp.concatenate([buf.astype(u.dtype), u], axis=1)
    conv = (w_taps[0] * u_pad[:, 0:S] + w_taps[1] * u_pad[:, 1:S + 1]
            + w_taps[2] * u_pad[:, 2:S + 2])
    y = (gb * conv) @ w_out
    return y, u_pad[:, -(CONV_WIDTH - 1):]


def _rope(x, pos):
    half = ROT_DIM // 2
    inv = jnp.power(ROPE_THETA, -jnp.arange(0, ROT_DIM, 2, dtype=jnp.float32) / ROT_DIM)
    ang = pos.astype(jnp.float32)[:, None] * inv[None, :]
    cos = jnp.cos(ang)[None, :, None, :]
    sin = jnp.sin(ang)[None, :, None, :]
    xr = x[..., :ROT_DIM].astype(jnp.float32)
    x1, x2 = xr[..., :half], xr[..., half:]
    rot = jnp.concatenate([x1 * cos - x2 * sin, x2 * cos + x1 * sin], axis=-1).astype(x.dtype)
    return jnp.concatenate([rot, x[..., ROT_DIM:]], axis=-1)


def _qkv(x, w_qkv, pos):
    B, S, _ = x.shape
    q, k, v = jnp.split(x @ w_qkv, 3, axis=-1)
    q = q.reshape(B, S, 2 * N_HEADS, HEAD_DIM)
    k = k.reshape(B, S, 2 * N_HEADS, HEAD_DIM)
    v = v.reshape(B, S, N_HEADS, 2 * HEAD_DIM)
    return _rope(q, pos), _rope(k, pos), v


def _diff_core(q, k, v, mask, lam):
    s = jnp.einsum('bqhd,bkhd->bhqk', q, k).astype(jnp.float32) * SCALE
    s = jnp.where(mask[None, None], s, jnp.finfo(jnp.float32).min)
    p = jax.nn.softmax(s, axis=-1)
    B, _, Sq, Sk = p.shape
    p = p.reshape(B, N_HEADS, 2, Sq, Sk)
    a = p[:, :, 0] - lam * p[:, :, 1]
    return jnp.einsum('bhqk,bkhe->bqhe', a, v.astype(jnp.float32))


def _prompt_attn(q, k, v, lam):
    B, S = q.shape[:2]
    nb = S // Q_BLOCK
    qb = q.reshape(B, nb, Q_BLOCK, 2 * N_HEADS, HEAD_DIM).swapaxes(0, 1)
    k_pos = jnp.arange(S)

    def one_block(args):
        q_blk, b_idx = args
        q_pos = b_idx * Q_BLOCK + jnp.arange(Q_BLOCK)
        return _diff_core(q_blk, k, v, k_pos[None, :] <= q_pos[:, None], lam)

    o = lax.map(one_block, (qb, jnp.arange(nb)))
    return o.swapaxes(0, 1).reshape(B, S, N_HEADS, 2 * HEAD_DIM)


def _sample_attn(q, k, v, cache_k, cache_v, page_table, lam):
    DB, n_pages = page_table.shape
    past_len = n_pages * cache_k.shape[1]
    DS = q.shape[1]
    past_k = cache_k[page_table].reshape(DB, past_len, 2 * N_HEADS, HEAD_DIM)
    past_v = cache_v[page_table].reshape(DB, past_len, N_HEADS, 2 * HEAD_DIM)
    k_all = jnp.concatenate([past_k, k.astype(past_k.dtype)], axis=1)
    v_all = jnp.concatenate([past_v, v.astype(past_v.dtype)], axis=1)
    k_pos = jnp.arange(past_len + DS)
    q_pos = past_len + jnp.arange(DS)
    return _diff_core(q, k_all, v_all, k_pos[None, :] <= q_pos[:, None], lam)


def _attn_out(o, subln_g, lam_init, w_out, dtype):
    o = o * lax.rsqrt(jnp.mean(o * o, axis=-1, keepdims=True) + SUBLN_EPS)
    o = o * subln_g.astype(jnp.float32) * (1.0 - lam_init)
    B, S = o.shape[:2]
    return o.reshape(B, S, N_HEADS * 2 * HEAD_DIM).astype(dtype) @ w_out


def _swiglu(x, w_in, w_out):
    gate, up = jnp.split(x @ w_in, 2, axis=-1)
    return (jax.nn.silu(gate) * up) @ w_out


def setup_inputs(seed: int = 0) -> dict:
    key = jax.random.key(seed)
    ks = jax.random.split(key, 24)
    D, F, f32 = D_MODEL, D_FF, jnp.float32
    n_pages = PAST_LEN // PAGE_SIZE
    n_used = DEC_BATCH * n_pages
    n_pool = n_used + max(1, n_used // 4)
    nrm = lambda k, shp: jax.random.normal(k, shp, f32)
    page_table = jax.random.permutation(ks[5], n_pool)[:n_used].astype(jnp.int32).reshape(DEC_BATCH, n_pages)
    return {
        "x_prompt": nrm(ks[0], (BATCH, SEQ, D)),
        "x_sample": nrm(ks[1], (DEC_BATCH, DEC_SEQ, D)),
        "state_conv": nrm(ks[2], (DEC_BATCH, CONV_WIDTH - 1, D)),
        "cache_k": nrm(ks[3], (n_pool, PAGE_SIZE, 2 * N_HEADS, HEAD_DIM)),
        "cache_v": nrm(ks[4], (n_pool, PAGE_SIZE, N_HEADS, 2 * HEAD_DIM)),
        "page_table": page_table,
        "w_conv_in": nrm(ks[6], (D, 3 * D)) * D ** -0.5,
        "w_conv": nrm(ks[7], (CONV_WIDTH, D)) * CONV_WIDTH ** -0.5,
        "w_conv_out": nrm(ks[8], (D, D)) * (D ** -0.5 * BETA),
        "w_qkv": nrm(ks[9], (D, 3 * D)) * D ** -0.5,
        "lambda_q1": nrm(ks[10], (HEAD_DIM,)) * 0.1,
        "lambda_k1": nrm(ks[11], (HEAD_DIM,)) * 0.1,
        "lambda_q2": nrm(ks[12], (HEAD_DIM,)) * 0.1,
        "lambda_k2": nrm(ks[13], (HEAD_DIM,)) * 0.1,
        "subln_g": 1.0 + 0.02 * nrm(ks[14], (2 * HEAD_DIM,)),
        "w_attn_out": nrm(ks[15], (D, D)) * (D ** -0.5 * BETA),
        "ln_mix_g": 1.0 + 0.02 * nrm(ks[16], (DEPTH, D)),
        "ln_mix_b": 0.02 * nrm(ks[17], (DEPTH, D)),
        "w_ffn_in": nrm(ks[18], (DEPTH, D, 2 * F)) * D ** -0.5,
        "w_ffn_out": nrm(ks[19], (DEPTH, F, D)) * (F ** -0.5 * BETA),
        "ln_ffn_g": 1.0 + 0.02 * nrm(ks[20], (DEPTH, D)),
        "ln_ffn_b": 0.02 * nrm(ks[21], (DEPTH, D)),
    }


def reference(x_prompt, x_sample, state_conv, cache_k, cache_v, page_table,
              w_conv_in, w_conv, w_conv_out, w_qkv,
              lambda_q1, lambda_k1, lambda_q2, lambda_k2, subln_g, w_attn_out,
              ln_mix_g, ln_mix_b, w_ffn_in, w_ffn_out, ln_ffn_g, ln_ffn_b):
    xp, xs = x_prompt, x_sample
    S, DS = xp.shape[1], xs.shape[1]
    past_len = page_table.shape[1] * cache_k.shape[1]
    pos_p = jnp.arange(S)
    pos_s = past_len + jnp.arange(DS)
    for i in range(DEPTH):
        if i % 2 == 0:
            buf_p = jnp.zeros((xp.shape[0], CONV_WIDTH - 1, xp.shape[2]), xp.dtype)
            mp, conv_p = _short_conv(xp, buf_p, w_conv_in, w_conv, w_conv_out)
            ms, conv_s = _short_conv(xs, state_conv, w_conv_in, w_conv, w_conv_out)
        else:
            lam_init = _lambda_init(i)
            lam = (jnp.exp(jnp.sum(lambda_q1.astype(jnp.float32) * lambda_k1.astype(jnp.float32)))
                   - jnp.exp(jnp.sum(lambda_q2.astype(jnp.float32) * lambda_k2.astype(jnp.float32)))
                   + lam_init)
            qp, k_p, v_p = _qkv(xp, w_qkv, pos_p)
            op = _prompt_attn(qp, k_p, v_p, lam)
            mp = _attn_out(op, subln_g, lam_init, w_attn_out, xp.dtype)
            qs, k_s, v_s = _qkv(xs, w_qkv, pos_s)
            os_ = _sample_attn(qs, k_s, v_s, cache_k, cache_v, page_table, lam)
            ms = _attn_out(os_, subln_g, lam_init, w_attn_out, xs.dtype)
        xp = _layernorm(ALPHA * xp + mp, ln_mix_g[i], ln_mix_b[i])
        xs = _layernorm(ALPHA * xs + ms, ln_mix_g[i], ln_mix_b[i])
        xp = _layernorm(ALPHA * xp + _swiglu(xp, w_ffn_in[i], w_ffn_out[i]), ln_ffn_g[i], ln_ffn_b[i])
        xs = _layernorm(ALPHA * xs + _swiglu(xs, w_ffn_in[i], w_ffn_out[i]), ln_ffn_g[i], ln_ffn_b[i])
    return (xp, xs, conv_p, k_p, v_p, conv_s, k_s, v_s)
```

```python
import math
import numpy as np
import ml_dtypes
from contextlib import ExitStack
import concourse.bass as bass
import concourse.mybir as mybir
from concourse.bass_utils import run_bass_kernel_spmd

F32 = mybir.dt.float32
BF16 = mybir.dt.bfloat16
I32 = mybir.dt.int32
AF = mybir.ActivationFunctionType
ALU = mybir.AluOpType
AX = mybir.AxisListType

D = 1024
KC = 8
DFF = 2816
FC = 22
ALPHA = (2 * 2) ** 0.25
SCALE = 64 ** -0.5
LN_EPS = 1e-5
SUB_EPS = 1e-5
LAM_INIT = 0.8 - 0.6 * math.exp(-0.3 * 1)
NEG = -30000.0


class Buf:
    __slots__ = ("name", "w", "rs", "dsem")

    def __init__(self, name):
        self.name = name
        self.w = None
        self.rs = {}
        self.dsem = None


class Sem:
    __slots__ = ("h", "cnt")

    def __init__(self, h):
        self.h = h
        self.cnt = 0


class Eng:
    def __init__(self, name, h, sem):
        self.name = name
        self.h = h
        self.sem = sem
        self.seen = {}
        self.pending = False


class TB:
    def __init__(self, t, name):
        self.t = t
        self.b = Buf(name)


class FW:
    def __init__(self, nc, stack):
        self.nc = nc
        self.stack = stack
        self.nsem = 0
        self.pe = self._eng("pe", nc.tensor)
        self.act = self._eng("act", nc.scalar)
        self.dve = self._eng("dve", nc.vector)
        self.pool = self._eng("pool", nc.gpsimd)
        self.sp = self._eng("sp", nc.sync)
        self.engs = [self.pe, self.act, self.dve, self.pool, self.sp]
        self.out_events = {}
        self.n_inst = 0
        self.dry = False

    def new_sem(self, name):
        self.nsem += 1
        return Sem(self.stack.enter_context(self.nc.semaphore(name)))

    def _eng(self, name, h):
        return Eng(name, h, self.new_sem("s_" + name))

    def _deps(self, E, reads, writes):
        need = {}
        for b in reads:
            if b.w is not None:
                s, v = b.w
                if need.get(s, 0) < v:
                    need[s] = v
        for b in writes:
            if b.w is not None:
                s, v = b.w
                if need.get(s, 0) < v:
                    need[s] = v
            for s, v in b.rs.items():
                if need.get(s, 0) < v:
                    need[s] = v
        for s, v in need.items():
            if s is E.sem and (v > s.cnt or E is self.pe):
                continue
            if E.seen.get(s, 0) >= v:
                continue
            E.h.wait_ge(s.h, v)
            E.seen[s] = v

    def _record(self, ev, reads, writes):
        s, v = ev
        for b in writes:
            b.w = ev
            b.rs = {}
        for b in reads:
            if b.rs.get(s, 0) < v:
                b.rs[s] = v

    def op(self, E, fn, reads=(), writes=(), signal=True):
        if self.dry:
            return None
        self._deps(E, reads, writes)
        ins = fn(E.h)
        self.n_inst += 1
        if signal:
            E.sem.cnt += 1
            ins.then_inc(E.sem.h, 1)
            ev = (E.sem, E.sem.cnt)
            E.pending = False
        else:
            ev = (E.sem, E.sem.cnt + 1)
            E.pending = True
        self._record(ev, reads, writes)
        return ins

    def dma(self, Q, fn, reads=(), writes=(), sembuf=None, is_out=False):
        if self.dry:
            return None
        self._deps(Q, reads, writes)
        if sembuf.dsem is None:
            sembuf.dsem = self.new_sem("d_" + sembuf.name)
        s = sembuf.dsem
        ins = fn(Q.h)
        self.n_inst += 1
        s.cnt += 16
        ins.then_inc(s.h, 16)
        ev = (s, s.cnt)
        self._record(ev, reads, writes)
        if is_out:
            self.out_events[s] = s.cnt
        return ins

    def finish(self):
        assert not self.pe.pending
        E = self.sp
        for s, v in self.out_events.items():
            if E.seen.get(s, 0) < v:
                E.h.wait_ge(s.h, v)
                E.seen[s] = v
        for X in self.engs:
            if X is E or X.sem.cnt == 0:
                continue
            if E.seen.get(X.sem, 0) < X.sem.cnt:
                E.h.wait_ge(X.sem.h, X.sem.cnt)


def view(ap, name):
    v = TB.__new__(TB)
    v.t = ap
    v.b = Buf(name)
    return v


def merge_bufs(dst, srcs):
    for s_ in srcs:
        evs = dict(s_.rs)
        if s_.w is not None:
            s, v = s_.w
            if evs.get(s, 0) < v:
                evs[s] = v
        for s, v in evs.items():
            if dst.rs.get(s, 0) < v:
                dst.rs[s] = v


def build(NG, NPG, NPOOL):
    NT = 2 * NG
    NTOK = NT * 512
    NB = NT * 4
    NPT = 4 * NPG
    nc = bass.Bass("TRN2", target_bir_lowering=False)
    dt = lambda n, s, d, k: nc.dram_tensor(n, s, d, kind=k).ap()
    IN, OUT, SCR = "ExternalInput", "ExternalOutput", "Internal"
    xT = dt("xT", [D, NTOK], F32, IN)
    xh = dt("xh", [D, NT * 2], F32, IN)
    xsT = dt("xsT", [D, 32], F32, IN)
    sconvT = dt("sconvT", [D, 8], F32, IN)
    cache_k = dt("cache_k", [NPOOL * 128, D], F32, IN)
    cache_v = dt("cache_v", [NPOOL * 128, D], F32, IN)
    ptab = dt("ptab", [1, NPT], I32, IN)
    w_ci = dt("w_ci", [D, 3 * D], F32, IN)
    w_co = dt("w_co", [D, D], F32, IN)
    w_qkv = dt("w_qkv", [D, 3 * D], F32, IN)
    w_ao = dt("w_ao", [D, D], F32, IN)
    w_fi = dt("w_fi", [2, D, 2 * DFF], F32, IN)
    w_fo = dt("w_fo", [2, DFF, D], F32, IN)
    vecs = dt("vecs", [D, 11], F32, IN)
    lamv = dt("lamv", [1, 256], F32, IN)
    grep = dt("grep", [128, 128], F32, IN)
    ropeP = dt("ropeP", [128, NB, 16], F32, IN)
    ropeS = dt("ropeS", [32, 16], F32, IN)
    cst_f = dt("cst_f", [128, 908], F32, IN)
    cst_b = dt("cst_b", [128, 384], BF16, IN)
    yT = dt("yT", [D, NG * 512], F32, OUT)
    ysT = dt("ysT", [D, 32], F32, OUT)
    kp = dt("kp", [NG * 512, D], F32, OUT)
    vp = dt("vp", [NG * 512, D], F32, OUT)
    convp = dt("convp", [D, 2], F32, OUT)
    convs = dt("convs", [D, 8], F32, OUT)
    ks = dt("ks", [32, D], F32, OUT)
    vs = dt("vs", [32, D], F32, OUT)
    s_ci = dt("s_ci", [D, 3 * D], BF16, SCR)
    s_co = dt("s_co", [D, D], BF16, SCR)
    s_qkv = dt("s_qkv", [D, 3 * D], BF16, SCR)
    s_ao = dt("s_ao", [D, D], BF16, SCR)
    s_fi = dt("s_fi", [2, D, 2 * DFF], BF16, SCR)
    s_fo = dt("s_fo", [2, DFF, D], BF16, SCR)
    s_kT = dt("s_kT", [D, NTOK], BF16, SCR)
    s_v = dt("s_v", [NTOK, 8 * 129], BF16, SCR)
    s_x2 = dt("s_x2", [D, NG * 512], F32, SCR)

    with ExitStack() as st:
        fw = FW(nc, st)
        PE, ACT, DVE, POOL, SP = fw.pe, fw.act, fw.dve, fw.pool, fw.sp

        def T(name, shape, dtype):
            return TB(st.enter_context(nc.sbuf_tensor(name, shape, dtype)), name)

        def PS(name, shape, dtype):
            return TB(st.enter_context(nc.psum_tensor(name, shape, dtype)), name)

        c_f = T("c_f", [128, 908], F32)
        c_b = T("c_b", [128, 384], BF16)
        vec = T("vec", [128, KC, 11], F32)
        g_rep = T("g_rep", [128, 128], F32)
        rP = T("rP", [128, NB, 16], F32)
        rS = T("rS", [32, 16], F32)
        lamb = T("lamb", [128, 2], F32)
        Cmat = T("Cmat", [128, 64], F32)
        pt_i = T("pt_i", [128, NPT], I32)
        pt_x = T("pt_x", [128, NPT], I32)
        xf = T("xf", [128, KC, 512], F32)
        xb = T("xb", [128, KC, 512], BF16)
        gbuf = T("gbuf", [128, KC, 512], BF16)
        x1b = T("x1b", [128, KC, 512], BF16)
        hb = [T(f"hb{i}", [128, 512], BF16) for i in range(FC)]
        t1 = T("t1", [128, 512], F32)
        t2 = T("t2", [128, 512], F32)
        lam_sb = t1
        ln_m = t2
        ln_q = t1
        qz = [T(f"qz{i}", [128, 2, 512], BF16) for i in range(2)]
        pt_f = view(t2.t[:, 0:NPT], "pt_f")
        pt_f.b = t2.b
        upad = T("upad", [128, 520], F32)
        uhal = T("uhal", [128, 8], F32)
        cvo = T("cvo", [128, KC, 8], F32)
        ln_r = T("ln_r", [128, 512], F32)
        ln_n = T("ln_n", [128, 512], F32)
        ln_b1 = [T(f"ln_b1_{i}", [128, 512], BF16) for i in range(2)]
        ln_b2 = [T(f"ln_b2_{i}", [128, 512], BF16) for i in range(2)]
        tmf = [T(f"tmf{i}", [128, D], F32) for i in range(2)]
        tmb = [T(f"tmb{i}", [128, D], BF16) for i in range(2)]
        rt = [T(f"rt{i}", [128, 16, 8], F32) for i in range(4)]
        kTt = gbuf
        t3 = T("t3", [128, 512], F32)
        vbt = [T(f"vbt{i}", [128, 8, 129], BF16) for i in range(2)]
        wsl = [T(f"wsl{i}", [128, 4096], BF16) for i in range(5)]
        NKB = 4 * NG
        KTW = max(NKB * 128, 2048)
        kt = [T(f"kt{i}", [128, 2, KTW], BF16) for i in range(2)]
        vt = [T(f"vt{i}", [128, 2 * NKB, 129], BF16) for i in range(2)]
        pT = [T(f"pT{i}", [128, 512], BF16) for i in range(4)]
        sm = T("sm", [128, 16], F32)
        o_t = T("o_t", [128, 128], F32)
        o_o = T("o_o", [128, 128], F32)
        o_n = [T(f"o_n{i}", [128, 128], BF16) for i in range(2)]
        qblk = T("qblk", [128, KC, 128], BF16)
        kpg = [view(kt[0].t[:, 0, 0:1024], "kpg0"), view(kt[0].t[:, 0, 1024:2048], "kpg1"), view(kt[0].t[:, 1, 0:1024], "kpg2")]
        vpg = [view(kt[1].t[:, 0, 0:1024], "vpg0"), view(kt[1].t[:, 0, 1024:2048], "vpg1"), view(kt[1].t[:, 1, 0:1024], "vpg2")]
        kTp = [view(kt[0].t[:, 1, 1024:2048], "kTp0"), view(kt[1].t[:, 1, 1024:2048], "kTp1")]
        osb = T("osb", [128, 3, 387], F32)
        xs2 = T("xs2", [128, KC, 32], F32)
        pTs = [T(f"pTs{i}", [128, 128], BF16) for i in range(2)]
        kTs = T("kTs", [128, KC, 32], BF16)
        qTs = T("qTs", [128, KC, 32], BF16)
        vsb = T("vsb", [32, D], BF16)
        od = tmf[0]
        orr = T("orr", [128, 128], F32)
        attS = T("attS", [128, KC, 32], BF16)
        zer = T("zer", [128, 512], BF16)
        mmb = [PS(f"mm{i}", [128, 512], F32) for i in range(3)]
        sbank = PS("sbank", [128, 512], F32)
        trb = PS("trb", [128, 1024], BF16)
        trbF = view(trb.t[:, 0:8].bitcast(F32), "trbF")
        trbF.b = trb.b
        ob = [PS(f"ob{i}", [128, 512], F32) for i in range(3)]

        state = {"mm": 0, "ws": 0, "k": 0}

        tk = {"gen": None}

        def tick(n=1):
            g = tk["gen"]
            if g is None:
                return
            for _ in range(n):
                try:
                    next(g)
                except StopIteration:
                    tk["gen"] = None
                    return

        def next_mm():
            b = mmb[state["mm"] % 3]
            state["mm"] += 1
            return b

        identF = c_f.t[:, 0:128]
        C1 = c_f.t[:, 128:192]
        C2 = c_f.t[:, 192:256]
        mask3 = c_f.t[:, 256:264]
        flag = c_f.t[:, 265:266]
        eps_ln = c_f.t[:, 266:267]
        eps_sub = c_f.t[:, 267:268]
        masknew = c_f.t[0:32, 268:268 + 512]
        identB = c_b.t[:, 0:128]
        onesB = c_b.t[:, 128:256]
        tri = c_b.t[:, 256:384]

        fw.op(POOL, lambda h: h.memset(zer.t[:], 0.0), writes=[zer.b])
        for vb_ in vbt:
            fw.op(POOL, lambda h, vb_=vb_: h.memset(vb_.t[:], 1.0), writes=[vb_.b])
        for qz_ in qz:
            fw.op(POOL, lambda h, qz_=qz_: h.memset(qz_.t[:], 0.0), writes=[qz_.b])
        def ld(tb, src, q=SP):
            fw.dma(q, lambda h: h.dma_start(out=tb.t[:], in_=src), writes=[tb.b], sembuf=tb.b)
        ld(c_f, cst_f)
        ld(c_b, cst_b)
        fw.dma(SP, lambda h: h.dma_start(out=vec.t[:], in_=vecs.rearrange("(c p) k -> p c k", p=128)), writes=[vec.b], sembuf=vec.b)
        ld(g_rep, grep)
        ld(rP, ropeP)
        ld(rS, ropeS)
        fw.dma(SP, lambda h: h.dma_start(out=lam_sb.t[0:1, 0:256], in_=lamv), writes=[lam_sb.b], sembuf=lam_sb.b)
        fw.dma(SP, lambda h: h.dma_start(out=pt_i.t[:], in_=ptab.partition_broadcast(128)), writes=[pt_i.b], sembuf=pt_i.b)
        wb = {}
        wlist = [("ci", w_ci, s_ci), ("co", w_co, s_co), ("fi0", w_fi[0], s_fi[0]), ("fo0", w_fo[0], s_fo[0]),
                 ("qkv", w_qkv, s_qkv), ("ao", w_ao, s_ao), ("fi1", w_fi[1], s_fi[1]), ("fo1", w_fo[1], s_fo[1])]
        for name, src, dst in wlist:
            wb[name] = Buf("w_" + name)

        def convert(names):
            for name, src, dst in wlist:
                if name not in names:
                    continue
                b = wb[name]
                nk = src.shape[0] // 128
                for k0 in range(0, nk, 4):
                    k1 = min(nk, k0 + 4)
                    fw.dma(POOL, lambda h, src=src, dst=dst, k0=k0, k1=k1: h.dma_start(
                        out=dst[k0 * 128:k1 * 128, :], in_=src[k0 * 128:k1 * 128, :]), writes=[b], sembuf=b)
        convert(["ci", "co", "fi0", "fo0", "qkv", "ao", "fi1", "fo1"])
        fw.op(DVE, lambda h: h.tensor_scalar(out=g_rep.t[:], in0=g_rep.t[:], scalar1=float(1.0 - LAM_INIT), scalar2=None, op0=ALU.mult),
              reads=[g_rep.b], writes=[g_rep.b])
        L = lam_sb.t
        fw.op(DVE, lambda h: h.tensor_tensor(out=L[0:1, 0:64], in0=L[0:1, 0:64], in1=L[0:1, 64:128], op=ALU.mult), reads=[lam_sb.b], writes=[lam_sb.b])
        fw.op(DVE, lambda h: h.tensor_tensor(out=L[0:1, 128:192], in0=L[0:1, 128:192], in1=L[0:1, 192:256], op=ALU.mult), reads=[lam_sb.b], writes=[lam_sb.b])
        fw.op(DVE, lambda h: h.reduce_sum(out=L[0:1, 256:257], in_=L[0:1, 0:64], axis=AX.X), reads=[lam_sb.b], writes=[lam_sb.b])
        fw.op(DVE, lambda h: h.reduce_sum(out=L[0:1, 257:258], in_=L[0:1, 128:192], axis=AX.X), reads=[lam_sb.b], writes=[lam_sb.b])
        fw.op(ACT, lambda h: h.activation(out=L[0:1, 256:258], in_=L[0:1, 256:258], func=AF.Exp), reads=[lam_sb.b], writes=[lam_sb.b])
        fw.op(DVE, lambda h: h.tensor_tensor(out=L[0:1, 258:259], in0=L[0:1, 256:257], in1=L[0:1, 257:258], op=ALU.subtract), reads=[lam_sb.b], writes=[lam_sb.b])
        fw.op(DVE, lambda h: h.tensor_scalar(out=L[0:1, 258:259], in0=L[0:1, 258:259], scalar1=float(LAM_INIT), scalar2=None, op0=ALU.add), reads=[lam_sb.b], writes=[lam_sb.b])
        fw.op(DVE, lambda h: h.tensor_scalar(out=L[0:1, 259:260], in0=L[0:1, 258:259], scalar1=-1.0, scalar2=None, op0=ALU.mult), reads=[lam_sb.b], writes=[lam_sb.b])
        b0 = next_mm()
        fw.op(PE, lambda h: h.matmul(b0.t[:, 0:2], lhsT=c_f.t[0:1, 780:908], rhs=L[0:1, 258:260], start=True, stop=True),
              reads=[c_f.b, lam_sb.b], writes=[b0.b])
        fw.op(DVE, lambda h: h.tensor_copy(out=lamb.t[:], in_=b0.t[:, 0:2]), reads=[b0.b], writes=[lamb.b])
        fw.op(DVE, lambda h: h.scalar_tensor_tensor(out=Cmat.t[:], in0=C2, scalar=lamb.t[:, 1:2], in1=C1, op0=ALU.mult, op1=ALU.add),
              reads=[c_f.b, lamb.b], writes=[Cmat.b])
        fw.op(DVE, lambda h: h.tensor_copy(out=pt_f.t[:], in_=pt_i.t[:]), reads=[pt_i.b], writes=[pt_f.b])
        fw.op(DVE, lambda h: h.tensor_scalar(out=pt_f.t[:], in0=pt_f.t[:], scalar1=128.0, scalar2=c_f.t[:, 264:265], op0=ALU.mult, op1=ALU.add),
              reads=[pt_f.b, c_f.b], writes=[pt_f.b])
        fw.op(DVE, lambda h: h.tensor_copy(out=pt_x.t[:], in_=pt_f.t[:]), reads=[pt_f.b], writes=[pt_x.b])

        plan = []
        pst = {"i": 0, "issued": 0}
        NS, LOOK = len(wsl), 2

        def issue_w(j):
            scr, wbuf, k0, nk, c0, ncols = plan[j]
            slot = wsl[j % NS]
            fw.dma(SP, lambda h: h.dma_start(
                out=slot.t[:, 0:nk * ncols].rearrange("p (k n) -> p k n", n=ncols),
                in_=scr[k0 * 128:(k0 + nk) * 128, c0:c0 + ncols].rearrange("(k p) n -> p k n", p=128)),
                reads=[wbuf], writes=[slot.b], sembuf=slot.b)

        def load_w(scr, wbuf, k0, nk, c0, ncols):
            if fw.dry:
                plan.append((scr, wbuf, k0, nk, c0, ncols))
                return wsl[0]
            i = pst["i"]
            pst["i"] += 1
            assert plan[i][2:] == (k0, nk, c0, ncols)
            while pst["issued"] < min(len(plan), i + 1 + LOOK):
                issue_w(pst["issued"])
                pst["issued"] += 1
            return wsl[i % NS]

        def linear_fm(inp, in_bufs, nkc, scr, wbuf, c0, noc_total, n, evac):
            for og in range(0, noc_total, 3):
                noc = min(3, noc_total - og)
                banks = [next_mm() for _ in range(noc)]
                for k0 in range(0, nkc, 8):
                    nk = min(8, nkc - k0)
                    slot = load_w(scr, wbuf, k0, nk, c0 + og * 128, noc * 128)
                    for j in range(noc):
                        for kk in range(nk):
                            kc = k0 + kk
                            fw.op(PE, lambda h, j=j, kk=kk, kc=kc, slot=slot, noc=noc: h.matmul(
                                banks[j].t[:, 0:n], lhsT=slot.t[:, kk * noc * 128 + j * 128: kk * noc * 128 + (j + 1) * 128],
                                rhs=inp(kc), start=(kc == 0), stop=(kc == nkc - 1)),
                                reads=[slot.b] + in_bufs, writes=[banks[j].b], signal=(kk == nk - 1))
                    tick()
                for j in range(noc):
                    evac(og + j, banks[j])

        def linear_tm(inp, in_bufs, nrows, ntb, scr, wbuf, c0, evac):
            slot = load_w(scr, wbuf, 0, KC, c0, 512)
            for tb in range(ntb):
                bank = next_mm()
                for kc in range(KC):
                    fw.op(PE, lambda h, kc=kc, tb=tb, bank=bank: h.matmul(
                        bank.t[0:nrows, :], lhsT=inp(kc, tb), rhs=slot.t[:, kc * 512:(kc + 1) * 512],
                        start=(kc == 0), stop=(kc == KC - 1)),
                        reads=[slot.b] + in_bufs, writes=[bank.b], signal=(kc == KC - 1))
                evac(tb, bank)

        def layernorm(y, n, gcol, bcol, outb, out_dram=None):
            s1, s2 = next_mm(), next_mm()
            for c in range(KC):
                a = ln_b1[c % 2]
                q = ln_b2[c % 2]
                fw.op(ACT, lambda h, c=c, a=a: h.activation(out=a.t[:, 0:n], in_=y.t[:, c, 0:n], func=AF.Copy), reads=[y.b], writes=[a.b])
                fw.op(ACT, lambda h, c=c, q=q: h.activation(out=q.t[:, 0:n], in_=y.t[:, c, 0:n], func=AF.Square), reads=[y.b], writes=[q.b])
                fw.op(PE, lambda h, c=c, a=a: h.matmul(s1.t[:, 0:n], lhsT=onesB, rhs=a.t[:, 0:n], start=(c == 0), stop=(c == KC - 1)),
                      reads=[a.b, c_b.b], writes=[s1.b])
                fw.op(PE, lambda h, c=c, q=q: h.matmul(s2.t[:, 0:n], lhsT=onesB, rhs=q.t[:, 0:n], start=(c == 0), stop=(c == KC - 1)),
                      reads=[q.b, c_b.b], writes=[s2.b])
            tick()
            m, q, r, nm = ln_m, ln_q, ln_r, ln_n
            fw.op(DVE, lambda h: h.tensor_scalar(out=m.t[:, 0:n], in0=s1.t[:, 0:n], scalar1=1.0 / D, scalar2=None, op0=ALU.mult), reads=[s1.b], writes=[m.b])
            fw.op(DVE, lambda h: h.tensor_tensor(out=q.t[:, 0:n], in0=m.t[:, 0:n], in1=m.t[:, 0:n], op=ALU.mult), reads=[m.b], writes=[q.b])
            fw.op(DVE, lambda h: h.scalar_tensor_tensor(out=q.t[:, 0:n], in0=s2.t[:, 0:n], scalar=1.0 / D, in1=q.t[:, 0:n], op0=ALU.mult, op1=ALU.subtract),
                  reads=[s2.b, q.b], writes=[q.b])
            fw.op(ACT, lambda h: h.activation(out=r.t[:, 0:n], in_=q.t[:, 0:n], func=AF.Sqrt, bias=eps_ln, scale=1.0), reads=[q.b, c_f.b], writes=[r.b])
            fw.op(DVE, lambda h: h.reciprocal(out=r.t[:, 0:n], in_=r.t[:, 0:n]), reads=[r.b], writes=[r.b])
            fw.op(DVE, lambda h: h.scalar_tensor_tensor(out=nm.t[:, 0:n], in0=m.t[:, 0:n], scalar=-1.0, in1=r.t[:, 0:n], op0=ALU.mult, op1=ALU.mult),
                  reads=[m.b, r.b], writes=[nm.b])
            for c in range(KC):
                fw.op(DVE, lambda h, c=c: h.tensor_tensor(out=y.t[:, c, 0:n], in0=y.t[:, c, 0:n], in1=r.t[:, 0:n], op=ALU.mult), reads=[y.b, r.b], writes=[y.b])
                fw.op(DVE, lambda h, c=c: h.tensor_tensor(out=y.t[:, c, 0:n], in0=y.t[:, c, 0:n], in1=nm.t[:, 0:n], op=ALU.add), reads=[y.b, nm.b], writes=[y.b])
                fw.op(ACT, lambda h, c=c: h.activation(out=y.t[:, c, 0:n], in_=y.t[:, c, 0:n], func=AF.Identity,
                                                       scale=vec.t[:, c, gcol:gcol + 1], bias=vec.t[:, c, bcol:bcol + 1]),
                      reads=[y.b, vec.b], writes=[y.b])
                fw.op(ACT, lambda h, c=c: h.activation(out=outb.t[:, c, 0:n], in_=y.t[:, c, 0:n], func=AF.Copy), reads=[y.b], writes=[outb.b])
                if c % 2 == 1:
                    tick()

        def ffn(layer, inb, y, n):
            wfi, wfo = wb[f"fi{layer}"], wb[f"fo{layer}"]
            for j0 in range(0, FC, 2):
                sg = load_w(s_fi[layer], wfi, 0, KC, j0 * 128, 256)
                su = load_w(s_fi[layer], wfi, 0, KC, DFF + j0 * 128, 256)
                for jj in range(2):
                    j = j0 + jj
                    bg, bu = next_mm(), next_mm()
                    for kc in range(KC):
                        fw.op(PE, lambda h, kc=kc, jj=jj, bg=bg: h.matmul(bg.t[:, 0:n], lhsT=sg.t[:, kc * 256 + jj * 128: kc * 256 + (jj + 1) * 128],
                              rhs=inb.t[:, kc, 0:n], start=(kc == 0), stop=(kc == KC - 1)), reads=[sg.b, inb.b], writes=[bg.b], signal=(kc == KC - 1))
                    for kc in range(KC):
                        fw.op(PE, lambda h, kc=kc, jj=jj, bu=bu: h.matmul(bu.t[:, 0:n], lhsT=su.t[:, kc * 256 + jj * 128: kc * 256 + (jj + 1) * 128],
                              rhs=inb.t[:, kc, 0:n], start=(kc == 0), stop=(kc == KC - 1)), reads=[su.b, inb.b], writes=[bu.b], signal=(kc == KC - 1))
                    tt = t1 if j % 2 == 0 else t2
                    fw.op(ACT, lambda h, bg=bg, tt=tt: h.activation(out=tt.t[:, 0:n], in_=bg.t[:, 0:n], func=AF.Silu), reads=[bg.b], writes=[tt.b])
                    fw.op(DVE, lambda h, bu=bu, tt=tt, j=j: h.tensor_tensor(out=hb[j].t[:, 0:n], in0=bu.t[:, 0:n], in1=tt.t[:, 0:n], op=ALU.mult),
                          reads=[bu.b, tt.b], writes=[hb[j].b])
                    tick()

            def ev(oc, bank):
                fw.op(DVE, lambda h: h.scalar_tensor_tensor(out=y.t[:, oc, 0:n], in0=y.t[:, oc, 0:n], scalar=float(ALPHA), in1=bank.t[:, 0:n],
                                                            op0=ALU.mult, op1=ALU.add), reads=[y.b, bank.b], writes=[y.b])
            linear_fm(lambda kc: hb[kc].t[:, 0:n], [h_.b for h_ in hb], FC, s_fo[layer], wfo, 0, KC, n, ev)

        def rope(tm, nrows, cos, sin):
            v = tm.t[0:nrows, :].rearrange("p (h d) -> p h d", d=64)
            x1 = v[:, :, 0:8]
            x2 = v[:, :, 8:16]
            cb = cos.unsqueeze(1).to_broadcast([nrows, 16, 8])
            sb = sin.unsqueeze(1).to_broadcast([nrows, 16, 8])
            a, b, c_, d_ = [r.t[0:nrows] for r in rt]
            rb = [r.b for r in rt]
            fw.op(DVE, lambda h: h.tensor_tensor(out=a, in0=x1, in1=cb, op=ALU.mult), reads=[tm.b, rP.b, rS.b], writes=[rb[0]])
            fw.op(DVE, lambda h: h.tensor_tensor(out=b, in0=x2, in1=sb, op=ALU.mult), reads=[tm.b, rP.b, rS.b], writes=[rb[1]])
            fw.op(DVE, lambda h: h.tensor_tensor(out=c_, in0=x2, in1=cb, op=ALU.mult), reads=[tm.b, rP.b, rS.b], writes=[rb[2]])
            fw.op(DVE, lambda h: h.tensor_tensor(out=d_, in0=x1, in1=sb, op=ALU.mult), reads=[tm.b, rP.b, rS.b], writes=[rb[3]])
            fw.op(DVE, lambda h: h.tensor_tensor(out=x1, in0=a, in1=b, op=ALU.subtract), reads=[rb[0], rb[1]], writes=[tm.b])
            fw.op(DVE, lambda h: h.tensor_tensor(out=x2, in0=c_, in1=d_, op=ALU.add), reads=[rb[2], rb[3]], writes=[tm.b])

        def qk_proj(inb, nrows, ntb, col0, cosf, sinf, dstT, out_dram):
            tms = {}

            def ev_half(half):
                def ev(tb, bank):
                    tm = tmf[tb % 2]
                    fw.op(ACT, lambda h: h.activation(out=tm.t[0:nrows, half * 512:(half + 1) * 512], in_=bank.t[0:nrows, :], func=AF.Copy),
                          reads=[bank.b], writes=[tm.b])
                    if half == 1:
                        rope(tm, nrows, cosf(tb), sinf(tb))
                        if out_dram is not None:
                            fw.dma(SP, lambda h: h.dma_start(out=out_dram(tb), in_=tm.t[0:nrows, :]), reads=[tm.b], sembuf=tm.b, is_out=True)
                        tbf = tmb[tb % 2]
                        fw.op(DVE, lambda h: h.tensor_copy(out=tbf.t[0:nrows, :], in_=tm.t[0:nrows, :]), reads=[tm.b], writes=[tbf.b])
                        for c in range(KC):
                            fw.op(PE, lambda h, c=c: h.transpose(trb.t[:, c * 128:c * 128 + nrows], tbf.t[0:nrows, c * 128:(c + 1) * 128], identB[0:nrows, 0:nrows]),
                                  reads=[tbf.b, c_b.b], writes=[trb.b])
                        fw.op(ACT, lambda h: h.activation(out=dstT.t[:, :, tb * nrows:(tb + 1) * nrows],
                                                          in_=trb.t[:, :].rearrange("p (c t) -> p c t", t=128)[:, :, 0:nrows], func=AF.Copy),
                              reads=[trb.b], writes=[dstT.b])
                return ev
            for tb0 in range(0, ntb, 2):
                nt_ = min(2, ntb - tb0)
                for half in range(2):
                    slot = load_w(s_qkv, wb["qkv"], 0, KC, col0 + half * 512, 512)
                    for tb in range(tb0, tb0 + nt_):
                        bank = next_mm()
                        for kc in range(KC):
                            fw.op(PE, lambda h, kc=kc, tb=tb, bank=bank, slot=slot: h.matmul(
                                bank.t[0:nrows, :], lhsT=inb.t[:, kc, tb * nrows:(tb + 1) * nrows], rhs=slot.t[:, kc * 512:(kc + 1) * 512],
                                start=(kc == 0), stop=(kc == KC - 1)), reads=[slot.b, inb.b], writes=[bank.b], signal=(kc == KC - 1))
                        ev_half(half)(tb, bank)
                        tick()

        def v_proj(inb, nrows, ntb, out_dram, vdst):
            for tb0 in range(0, ntb, 2):
                nt_ = min(2, ntb - tb0)
                for half in range(2):
                    slot = load_w(s_qkv, wb["qkv"], 0, KC, 2048 + half * 512, 512)
                    for tb in range(tb0, tb0 + nt_):
                        bank = next_mm()
                        for kc in range(KC):
                            fw.op(PE, lambda h, kc=kc, tb=tb, bank=bank, slot=slot: h.matmul(
                                bank.t[0:nrows, :], lhsT=inb.t[:, kc, tb * nrows:(tb + 1) * nrows], rhs=slot.t[:, kc * 512:(kc + 1) * 512],
                                start=(kc == 0), stop=(kc == KC - 1)), reads=[slot.b, inb.b], writes=[bank.b], signal=(kc == KC - 1))
                        tm = tmf[tb % 2]
                        fw.op(ACT, lambda h, tm=tm, bank=bank, half=half: h.activation(out=tm.t[0:nrows, half * 512:(half + 1) * 512], in_=bank.t[0:nrows, :], func=AF.Copy),
                              reads=[bank.b], writes=[tm.b])
                        if half == 1:
                            if out_dram is not None:
                                fw.dma(SP, lambda h, tm=tm, tb=tb: h.dma_start(out=out_dram(tb), in_=tm.t[0:nrows, :]), reads=[tm.b], sembuf=tm.b, is_out=True)
                            vdst(tb, tm)
                        tick()

        def layer0(n, nb, seg, x_src, halo_src, state_src, conv_out):
            fw.dma(SP, lambda h: h.dma_start(out=xf.t[:, :, 0:n], in_=x_src), writes=[xf.b], sembuf=xf.b)
            for c in range(KC):
                fw.op(ACT if c % 2 else DVE, (lambda h, c=c: h.activation(out=xb.t[:, c, 0:n], in_=xf.t[:, c, 0:n], func=AF.Copy)) if c % 2 else
                      (lambda h, c=c: h.tensor_copy(out=xb.t[:, c, 0:n], in_=xf.t[:, c, 0:n])), reads=[xf.b], writes=[xb.b])
            if halo_src is not None:
                fw.dma(SP, lambda h: h.dma_start(out=t2.t[:, 0:16].rearrange("p (c t) -> p c t", t=2), in_=halo_src), writes=[t2.b], sembuf=t2.b)
                fw.op(DVE, lambda h: h.tensor_copy(out=ln_b1[0].t[:, 0:16], in_=t2.t[:, 0:16]), reads=[t2.b], writes=[ln_b1[0].b])
            up3 = upad.t[:, 0:nb * (seg + 2)].rearrange("p (b s) -> p b s", s=seg + 2)
            for c in range(KC):
                if c % 4 == 0:
                    sl = [load_w(s_ci, wb["ci"], 0, KC, part * 1024 + c * 128, 512) for part in range(3)]
                j = c % 4
                banks = [next_mm() for _ in range(3)]
                for part in (1, 2, 0):
                    for kc in range(KC):
                        fw.op(PE, lambda h, kc=kc, part=part, j=j, sl=sl, banks=banks: h.matmul(
                            banks[part].t[:, 0:n], lhsT=sl[part].t[:, kc * 512 + j * 128: kc * 512 + (j + 1) * 128], rhs=xb.t[:, kc, 0:n],
                            start=(kc == 0), stop=(kc == KC - 1)), reads=[sl[part].b, xb.b], writes=[banks[part].b], signal=(kc == KC - 1))
                bgb, bgc, bh = banks
                fw.op(ACT, lambda h, bgc=bgc: h.activation(out=t1.t[:, 0:n], in_=bgc.t[:, 0:n], func=AF.Copy), reads=[bgc.b], writes=[t1.b])
                fw.op(ACT, lambda h, bgb=bgb: h.activation(out=t3.t[:, 0:n], in_=bgb.t[:, 0:n], func=AF.Copy), reads=[bgb.b], writes=[t3.b])
                fw.op(DVE, lambda h, bh=bh: h.tensor_tensor(out=up3[:, :, 2:seg + 2], in0=t1.t[:, 0:n].rearrange("p (b s) -> p b s", s=seg),
                                                            in1=bh.t[:, 0:n].rearrange("p (b s) -> p b s", s=seg), op=ALU.mult),
                      reads=[t1.b, bh.b], writes=[upad.b])
                if halo_src is not None:
                    bk = trbF
                    for part in (1, 2):
                        for kc in range(KC):
                            fw.op(PE, lambda h, kc=kc, part=part, j=j, sl=sl, bk=bk: h.matmul(
                                bk.t[:, (part - 1) * 2:(part - 1) * 2 + 2], lhsT=sl[part].t[:, kc * 512 + j * 128: kc * 512 + (j + 1) * 128],
                                rhs=ln_b1[0].t[:, kc * 2:kc * 2 + 2], start=(kc == 0), stop=(kc == KC - 1)),
                                reads=[sl[part].b, ln_b1[0].b], writes=[bk.b], signal=(kc == KC - 1))
                    fw.op(ACT, lambda h, bk=bk: h.activation(out=uhal.t[:, 0:2], in_=bk.t[:, 0:2], func=AF.Copy), reads=[bk.b], writes=[uhal.b])
                    fw.op(DVE, lambda h, bk=bk: h.tensor_tensor(out=upad.t[:, 0:2], in0=uhal.t[:, 0:2], in1=bk.t[:, 2:4], op=ALU.mult),
                          reads=[uhal.b, bk.b], writes=[upad.b])
                else:
                    fw.dma(SP, lambda h, c=c: h.dma_start(out=up3[:, :, 0:2], in_=state_src[c * 128:(c + 1) * 128, :].rearrange("p (b t) -> p b t", t=2)),
                           writes=[upad.b], sembuf=upad.b)
                tv = t2.t[:, 0:n].rearrange("p (b s) -> p b s", s=seg)
                fw.op(DVE, lambda h, c=c: h.tensor_scalar(out=tv, in0=up3[:, :, 0:seg], scalar1=vec.t[:, c, 0:1], scalar2=None, op0=ALU.mult),
                      reads=[upad.b, vec.b], writes=[t2.b])
                fw.op(DVE, lambda h, c=c: h.scalar_tensor_tensor(out=tv, in0=up3[:, :, 1:seg + 1], scalar=vec.t[:, c, 1:2], in1=tv, op0=ALU.mult, op1=ALU.add),
                      reads=[upad.b, vec.b, t2.b], writes=[t2.b])
                fw.op(DVE, lambda h, c=c: h.scalar_tensor_tensor(out=tv, in0=up3[:, :, 2:seg + 2], scalar=vec.t[:, c, 2:3], in1=tv, op0=ALU.mult, op1=ALU.add),
                      reads=[upad.b, vec.b, t2.b], writes=[t2.b])
                fw.op(DVE, lambda h, c=c: h.tensor_tensor(out=gbuf.t[:, c, 0:n], in0=t3.t[:, 0:n], in1=t2.t[:, 0:n], op=ALU.mult),
                      reads=[t3.b, t2.b], writes=[gbuf.b])
                tick()
                if conv_out is not None:
                    fw.op(DVE, lambda h, c=c: h.tensor_copy(out=cvo.t[:, c, 0:nb * 2].rearrange("p (b t) -> p b t", t=2), in_=up3[:, :, seg:seg + 2]),
                          reads=[upad.b], writes=[cvo.b])
            if conv_out is not None:
                fw.dma(SP, lambda h: h.dma_start(out=conv_out.rearrange("(c p) t -> p c t", p=128), in_=cvo.t[:, :, 0:nb * 2]), reads=[cvo.b], sembuf=cvo.b, is_out=True)

            def ev(oc, bank):
                fw.op(DVE, lambda h: h.scalar_tensor_tensor(out=xf.t[:, oc, 0:n], in0=xf.t[:, oc, 0:n], scalar=float(ALPHA), in1=bank.t[:, 0:n],
                                                            op0=ALU.mult, op1=ALU.add), reads=[xf.b, bank.b], writes=[xf.b])
            linear_fm(lambda kc: gbuf.t[:, kc, 0:n], [gbuf.b], KC, s_co, wb["co"], 0, KC, n, ev)
            layernorm(xf, n, 3, 4, x1b)
            ffn(0, x1b, xf, n)
            layernorm(xf, n, 5, 6, xb)

        def layer1_tail(attT, n, out_dram):
            def ev(oc, bank):
                fw.op(DVE, lambda h: h.scalar_tensor_tensor(out=xf.t[:, oc, 0:n], in0=xf.t[:, oc, 0:n], scalar=float(ALPHA), in1=bank.t[:, 0:n],
                                                            op0=ALU.mult, op1=ALU.add), reads=[xf.b, bank.b], writes=[xf.b])
            linear_fm(lambda kc: attT.t[:, kc, 0:n], [attT.b], KC, s_ao, wb["ao"], 0, KC, n, ev)
            layernorm(xf, n, 7, 8, x1b)
            ffn(1, x1b, xf, n)
            layernorm(xf, n, 9, 10, xb)
            fw.dma(SP, lambda h: h.dma_start(out=out_dram, in_=xf.t[:, :, 0:n]), reads=[xf.b], sembuf=xf.b, is_out=True)

        def emit_all():
            def phaseA_tile(ti):
                own = ti < NG
                conv_o = convp if ti == NG - 1 else None
                layer0(512, 1, 512, xT[:, ti * 512:(ti + 1) * 512].rearrange("(c p) t -> p c t", p=128),
                       xh[:, ti * 2:(ti + 1) * 2].rearrange("(c p) t -> p c t", p=128), None, conv_o)
                if own:
                    fw.dma(SP, lambda h, ti=ti: h.dma_start(out=s_x2[:, ti * 512:(ti + 1) * 512].rearrange("(c p) t -> p c t", p=128), in_=xf.t[:]),
                           reads=[xf.b], writes=[Buf("x2s%d" % ti)], sembuf=xf.b)
                qk_proj(xb, 128, 4, 1024, lambda tb, ti=ti: rP.t[:, ti * 4 + tb, 0:8], lambda tb, ti=ti: rP.t[:, ti * 4 + tb, 8:16], kTt,
                        (lambda tb, ti=ti: kp[ti * 512 + tb * 128: ti * 512 + (tb + 1) * 128, :]) if own else None)
                bkT = Buf("skT")
                fw.dma(SP, lambda h, ti=ti: h.dma_start(out=s_kT[:, ti * 512:(ti + 1) * 512].rearrange("(c p) t -> p c t", p=128), in_=kTt.t[:]),
                       reads=[kTt.b], writes=[bkT], sembuf=kTt.b)

                def vdst_p(tb, tm, ti=ti):
                    vb_ = vbt[tb % 2]
                    fw.op(ACT, lambda h: h.activation(out=vb_.t[:, :, 0:128], in_=tm.t[:, :].rearrange("p (g e) -> p g e", e=128), func=AF.Copy), reads=[tm.b], writes=[vb_.b])
                    r0 = ti * 512 + tb * 128
                    fw.dma(SP, lambda h: h.dma_start(out=s_v[r0:r0 + 128, :], in_=vb_.t[:].rearrange("p g e -> p (g e)")), reads=[vb_.b], writes=[Buf("sv")], sembuf=vb_.b)
                v_proj(xb, 128, 4, (lambda tb, ti=ti: vp[ti * 512 + tb * 128: ti * 512 + (tb + 1) * 128, :]) if own else None, vdst_p)
            phaseA_tile(0)
            layer0(32, 4, 8, xsT.rearrange("(c p) t -> p c t", p=128), None, sconvT, convs)
            fw.op(DVE, lambda h: h.tensor_copy(out=xs2.t[:], in_=xf.t[:, :, 0:32]), reads=[xf.b], writes=[xs2.b])
            qk_proj(xb, 32, 1, 0, lambda tb: rS.t[:, 0:8], lambda tb: rS.t[:, 8:16], qTs, None)
            qk_proj(xb, 32, 1, 1024, lambda tb: rS.t[:, 0:8], lambda tb: rS.t[:, 8:16], kTs, lambda tb: ks)

            def vdst_s(tb, tm):
                fw.op(POOL, lambda h: h.tensor_copy(out=vsb.t[:], in_=tm.t[0:32, :]), reads=[tm.b], writes=[vsb.b])
            v_proj(xb, 32, 1, lambda tb: vs, vdst_s)

            osum = ob[2]

            def sample_gen():
                for b in range(4):
                    fw.op(POOL, lambda h: h.memset(qblk.t[:], 0.0), writes=[qblk.b])
                    for c in range(KC):
                        for hh in range(2):
                            col = (2 * c + hh) * 8
                            fw.op(ACT, lambda h, c=c, hh=hh, col=col, b=b: h.activation(out=qblk.t[hh * 64:(hh + 1) * 64, c, col:col + 8],
                                  in_=qTs.t[hh * 64:(hh + 1) * 64, c, b * 8:(b + 1) * 8], func=AF.Copy), reads=[qTs.b], writes=[qblk.b])
                    yield
                    for t in range(-3, NPG + 2):
                        p = t
                        if 0 <= p <= NPG:
                            last = (p == NPG)
                            if not last:
                                kT_ = kTp[p % 2]
                                nk = 128
                                lhs = lambda c, kT_=kT_: kT_.t[:, c * 128:(c + 1) * 128]
                                lb = [kT_.b]
                            else:
                                nk = 32
                                lhs = lambda c: kTs.t[:, c, 0:32]
                                lb = [kTs.b]
                            for c in range(KC):
                                fw.op(PE, lambda h, c=c, lhs=lhs, nk=nk: h.matmul(sbank.t[0:nk, 0:128], lhsT=lhs(c), rhs=qblk.t[:, c, :],
                                      start=(c == 0), stop=(c == KC - 1)), reads=lb + [qblk.b], writes=[sbank.b], signal=(c == KC - 1))
                            pT_ = pTs[p % 2]
                            fw.op(ACT, lambda h, pT_=pT_, nk=nk: h.activation(out=pT_.t[0:nk, :], in_=sbank.t[0:nk, 0:128], func=AF.Exp, scale=float(SCALE)),
                                  reads=[sbank.b], writes=[pT_.b])
                            if last:
                                fw.op(DVE, lambda h, pT_=pT_, b=b: h.tensor_tensor(out=pT_.t[0:32, :], in0=pT_.t[0:32, :], in1=masknew[:, b * 128:(b + 1) * 128], op=ALU.mult),
                                      reads=[pT_.b, c_f.b], writes=[pT_.b])
                        p = t - 1
                        if 0 <= p <= NPG:
                            last = (p == NPG)
                            pT_ = pTs[p % 2]
                            nk = 32 if last else 128
                            vrhs = vsb if last else vpg[p % 3]
                            for half in range(2):
                                fw.op(PE, lambda h, half=half, pT_=pT_, vrhs=vrhs, nk=nk, p=p, last=last: h.matmul(ob[half].t[:, :], lhsT=pT_.t[0:nk, :],
                                      rhs=vrhs.t[0:nk, half * 512:(half + 1) * 512], start=(p == 0), stop=last), reads=[pT_.b, vrhs.b], writes=[ob[half].b], signal=False)
                            fw.op(PE, lambda h, pT_=pT_, nk=nk, p=p, last=last: h.matmul(osum.t[:, 0:1], lhsT=pT_.t[0:nk, :], rhs=onesB[0:nk, 0:1],
                                  start=(p == 0), stop=last), reads=[pT_.b, c_b.b], writes=[osum.b], signal=True)
                        p = t + 1
                        if 0 <= p < NPG:
                            kpg_ = kpg[p % 3]
                            kT_ = kTp[p % 2]
                            for c in range(KC):
                                fw.op(PE, lambda h, c=c, kpg_=kpg_: h.transpose(trb.t[:, c * 128:(c + 1) * 128], kpg_.t[:, c * 128:(c + 1) * 128], identB),
                                      reads=[kpg_.b, c_b.b], writes=[trb.b])
                            fw.op(DVE, lambda h, kT_=kT_: h.tensor_copy(out=kT_.t[:], in_=trb.t[:]), reads=[trb.b], writes=[kT_.b])
                        p = t + 1
                        if 0 <= p < NPG:
                            vpg_ = vpg[p % 3]
                            col = b * NPG + p
                            fw.dma(POOL, lambda h, vpg_=vpg_, col=col: h.indirect_dma_start(out=vpg_.t[:], out_offset=None, in_=cache_v,
                                   in_offset=bass.IndirectOffsetOnAxis(ap=pt_x.t[:, col:col + 1], axis=0)), reads=[pt_x.b], writes=[vpg_.b], sembuf=vpg_.b)
                        p = t + 3
                        if 0 <= p < NPG:
                            kpg_ = kpg[p % 3]
                            col = b * NPG + p
                            fw.dma(POOL, lambda h, kpg_=kpg_, col=col: h.indirect_dma_start(out=kpg_.t[:], out_offset=None, in_=cache_k,
                                   in_offset=bass.IndirectOffsetOnAxis(ap=pt_x.t[:, col:col + 1], axis=0)), reads=[pt_x.b], writes=[kpg_.b], sembuf=kpg_.b)
                        yield
                    for half in range(2):
                        fw.op(DVE, lambda h, half=half: h.tensor_tensor(out=od.t[:, half * 512:(half + 1) * 512].rearrange("p (g e) -> p g e", e=128),
                              in0=ob[half].t[:, :].rearrange("p (g e) -> p g e", e=128),
                              in1=mask3[:, half * 4:(half + 1) * 4].unsqueeze(2).to_broadcast([128, 4, 128]), op=ALU.mult),
                              reads=[ob[half].b, c_f.b], writes=[od.b])
                    fw.op(DVE, lambda h: h.tensor_reduce(out=orr.t[:], in_=od.t[:].rearrange("p (g e) -> p e g", e=128), op=ALU.add, axis=AX.X), reads=[od.b], writes=[orr.b])
                    fw.op(DVE, lambda h: h.reciprocal(out=sm.t[:, 0:1], in_=osum.t[:, 0:1]), reads=[osum.b], writes=[sm.b])
                    fw.op(DVE, lambda h: h.tensor_scalar(out=orr.t[:], in0=orr.t[:], scalar1=sm.t[:, 0:1], scalar2=None, op0=ALU.mult), reads=[orr.b, sm.b], writes=[orr.b])
                    yield
                    fw.op(PE, lambda h: h.matmul(sbank.t[0:64, 0:128], lhsT=Cmat.t[:], rhs=orr.t[:], start=True, stop=True), reads=[Cmat.b, orr.b], writes=[sbank.b])
                    fw.op(ACT, lambda h: h.activation(out=o_t.t[0:64, :], in_=sbank.t[0:64, 0:128], func=AF.Square, accum_out=sm.t[0:64, 1:2]),
                          reads=[sbank.b], writes=[o_t.b, sm.b])
                    fw.op(ACT, lambda h: h.activation(out=sm.t[0:64, 1:2], in_=sm.t[0:64, 1:2], func=AF.Sqrt, bias=eps_sub[0:64], scale=1.0 / 128), reads=[sm.b, c_f.b], writes=[sm.b])
                    fw.op(DVE, lambda h: h.reciprocal(out=sm.t[0:64, 1:2], in_=sm.t[0:64, 1:2]), reads=[sm.b], writes=[sm.b])
                    on_ = o_n[b % 2]
                    fw.op(DVE, lambda h, on_=on_: h.scalar_tensor_tensor(out=on_.t[0:64, :], in0=sbank.t[0:64, 0:128], scalar=sm.t[0:64, 1:2], in1=g_rep.t[0:64, :],
                          op0=ALU.mult, op1=ALU.mult), reads=[sbank.b, sm.b, g_rep.b], writes=[on_.b])
                    yield
                    fw.op(PE, lambda h, on_=on_: h.transpose(trb.t[:, 0:64], on_.t[0:64, :], identB[0:64, 0:64]), reads=[on_.b, c_b.b], writes=[trb.b])
                    fw.op(ACT, lambda h, b=b: h.activation(out=attS.t[:, :, b * 8:(b + 1) * 8], in_=trb.t[:, 0:64].rearrange("p (g q) -> p g q", q=8), func=AF.Copy),
                          reads=[trb.b], writes=[attS.b])
                    yield

            tk["gen"] = sample_gen()
            tick(4)

            for ti in range(1, NT):
                phaseA_tile(ti)
            while tk["gen"] is not None:
                tick()
            fw.op(DVE, lambda h: h.tensor_copy(out=xf.t[:, :, 0:32], in_=xs2.t[:]), reads=[xs2.b], writes=[xf.b])
            layer1_tail(attS, 32, ysT.rearrange("(c p) t -> p c t", p=128))
            if not fw.dry:
                for tb_ in [kTt] + vbt + [xf]:
                    s = tb_.b.dsem
                    if s is not None and SP.seen.get(s, 0) < s.cnt:
                        SP.h.wait_ge(s.h, s.cnt)
                        SP.seen[s] = s.cnt
                merge_bufs(kt[0].b, [x.b for x in (kpg[0], kpg[1], kpg[2], kTp[0])])
                merge_bufs(kt[1].b, [x.b for x in (vpg[0], vpg[1], vpg[2], kTp[1])])

            QT = kTt
            attT = x1b
            for I in range(NG):
                fw.dma(SP, lambda h, I=I: h.dma_start(out=xf.t[:], in_=s_x2[:, I * 512:(I + 1) * 512].rearrange("(c p) t -> p c t", p=128)), writes=[xf.b], sembuf=xf.b)
                for c in range(KC):
                    fw.op(ACT if c % 2 else DVE, (lambda h, c=c: h.activation(out=xb.t[:, c, :], in_=xf.t[:, c, :], func=AF.Copy)) if c % 2 else
                          (lambda h, c=c: h.tensor_copy(out=xb.t[:, c, :], in_=xf.t[:, c, :])), reads=[xf.b], writes=[xb.b])
                qk_proj(xb, 128, 4, 0, lambda tb, I=I: rP.t[:, I * 4 + tb, 0:8], lambda tb, I=I: rP.t[:, I * 4 + tb, 8:16], QT, None)
                nkb = 4 * (I + 1)
                def load_kv(hd, I=I, nkb=nkb):
                    kt_, vt_ = kt[hd % 2], vt[hd % 2]
                    for side in range(2):
                        t0 = side * NG * 512
                        fw.dma(SP, lambda h, side=side, t0=t0, kt_=kt_, hd=hd: h.dma_start(out=kt_.t[:, side, 0:nkb * 128], in_=s_kT[hd * 128:(hd + 1) * 128, t0:t0 + nkb * 128]),
                               writes=[kt_.b], sembuf=kt_.b)
                        fw.dma(SP, lambda h, side=side, t0=t0, vt_=vt_, hd=hd: h.dma_start(out=vt_.t[:, side * NKB:side * NKB + nkb, :],
                               in_=s_v[t0:t0 + nkb * 128, hd * 129:(hd + 1) * 129].rearrange("(k p) e -> p k e", p=128)), writes=[vt_.b], sembuf=vt_.b)

                def oacc(s, qs):
                    if qs < 3:
                        return ob[s], ob[s].t[:, qs * 129:(qs + 1) * 129]
                    return ob[2], ob[2].t[:, s * 129:(s + 1) * 129]

                def osb_ap(s, qs):
                    if qs < 3:
                        return osb.t[:, s, qs * 129:(qs + 1) * 129]
                    return osb.t[:, 2, s * 129:(s + 1) * 129]

                def blocks_of(hd, I=I):
                    kt_, vt_ = kt[hd % 2], vt[hd % 2]
                    blocks = []
                    for side in range(2):
                        for kb in range(4 * I):
                            blocks.append((side, kb, "full", 0))
                    for kb in range(4):
                        blocks.append((1, 4 * I + kb, "pdiag", 0))
                    for kb in range(4):
                        blocks.append((0, 4 * I + kb, "odiag", kb))
                    for ob_ in ob:
                        fw.op(PE, lambda h, ob_=ob_: h.matmul(ob_.t[:, 0:512], lhsT=zer.t[:, 0:128], rhs=zer.t[:, 0:512], start=True, stop=False),
                              reads=[zer.b], writes=[ob_.b])
                    qz_ = qz[hd % 2]
                    fw.op(POOL, lambda h, qz_=qz_, hd=hd: h.tensor_copy(out=qz_.t[0:64, 0, :], in_=QT.t[0:64, hd, :]), reads=[QT.b], writes=[qz_.b])
                    fw.op(POOL, lambda h, qz_=qz_, hd=hd: h.tensor_copy(out=qz_.t[64:128, 1, :], in_=QT.t[64:128, hd, :]), reads=[QT.b], writes=[qz_.b])
                    steps = [(side, kb, kind, j0, s) for (side, kb, kind, j0) in blocks for s in range(2)]
                    pts = {}
                    SKEW = 2

                    def qk_exp(i):
                        side, kb, kind, j0, s = steps[i]
                        q0 = j0 * 128
                        bank = next_mm()
                        fw.op(PE, lambda h: h.matmul(
                            bank.t[:, q0:512], lhsT=kt_.t[:, side, kb * 128:(kb + 1) * 128], rhs=qz_.t[:, s, q0:512],
                            start=True, stop=True), reads=[kt_.b, qz_.b], writes=[bank.b])
                        pT_ = pT[state["k"] % 4]
                        state["k"] += 1
                        pts[i] = pT_
                        if kind == "pdiag":
                            fw.op(ACT, lambda h: h.activation(out=pT_.t[:, :], in_=bank.t[:, :], func=AF.Exp, scale=float(SCALE), bias=flag),
                                  reads=[bank.b, c_f.b], writes=[pT_.b])
                        else:
                            fw.op(ACT, lambda h: h.activation(out=pT_.t[:, q0:512], in_=bank.t[:, q0:512], func=AF.Exp, scale=float(SCALE)),
                                  reads=[bank.b], writes=[pT_.b])
                        if kind == "odiag":
                            fw.op(POOL, lambda h: h.tensor_tensor(out=pT_.t[:, q0:q0 + 128], in0=pT_.t[:, q0:q0 + 128], in1=tri, op=ALU.mult),
                                  reads=[pT_.b, c_b.b], writes=[pT_.b])

                    def pv(i):
                        side, kb, kind, j0, s = steps[i]
                        pT_ = pts.pop(i)
                        for qs in range(j0, 4):
                            ab, aap = oacc(s, qs)
                            stop_ = (kind == "odiag" and j0 == qs)
                            fw.op(PE, lambda h, aap=aap, qs=qs, stop_=stop_: h.matmul(
                                aap, lhsT=pT_.t[:, qs * 128:(qs + 1) * 128], rhs=vt_.t[:, side * NKB + kb, :], start=False, stop=stop_),
                                reads=[pT_.b, vt_.b], writes=[ab.b], signal=(qs == 3))

                    for i in range(len(steps) + SKEW):
                        if i < len(steps):
                            qk_exp(i)
                        if i - SKEW >= 0:
                            pv(i - SKEW)

                def evac_o():
                    fw.op(DVE, lambda h: h.tensor_copy(out=osb.t[:, 0, :], in_=ob[0].t[:, 0:387]), reads=[ob[0].b], writes=[osb.b])
                    fw.op(ACT, lambda h: h.activation(out=osb.t[:, 1, :], in_=ob[1].t[:, 0:387], func=AF.Copy), reads=[ob[1].b], writes=[osb.b])
                    fw.op(DVE, lambda h: h.tensor_copy(out=osb.t[:, 2, 0:258], in_=ob[2].t[:, 0:258]), reads=[ob[2].b], writes=[osb.b])

                def finalize(hd):
                    for qs in range(4):
                        a0 = osb_ap(0, qs)
                        a1 = osb_ap(1, qs)
                        fw.op(DVE, lambda h, a0=a0: h.reciprocal(out=sm.t[:, 2:3], in_=a0[:, 128:129]), reads=[osb.b], writes=[sm.b])
                        fw.op(DVE, lambda h, a1=a1: h.reciprocal(out=sm.t[:, 3:4], in_=a1[:, 128:129]), reads=[osb.b], writes=[sm.b])
                        fw.op(DVE, lambda h: h.tensor_tensor(out=sm.t[:, 3:4], in0=sm.t[:, 3:4], in1=lamb.t[:, 1:2], op=ALU.mult), reads=[sm.b, lamb.b], writes=[sm.b])
                        fw.op(DVE, lambda h, a0=a0: h.tensor_scalar(out=o_t.t[:], in0=a0[:, 0:128], scalar1=sm.t[:, 2:3], scalar2=None, op0=ALU.mult),
                              reads=[osb.b, sm.b], writes=[o_t.b])
                        fw.op(DVE, lambda h, a1=a1: h.scalar_tensor_tensor(out=o_o.t[:], in0=a1[:, 0:128], scalar=sm.t[:, 3:4], in1=o_t.t[:], op0=ALU.mult, op1=ALU.add),
                              reads=[osb.b, sm.b, o_t.b], writes=[o_o.b])
                        fw.op(ACT, lambda h: h.activation(out=o_t.t[:], in_=o_o.t[:], func=AF.Square, accum_out=sm.t[:, 4:5]), reads=[o_o.b], writes=[o_t.b, sm.b])
                        fw.op(ACT, lambda h: h.activation(out=sm.t[:, 4:5], in_=sm.t[:, 4:5], func=AF.Sqrt, bias=eps_sub, scale=1.0 / 128), reads=[sm.b, c_f.b], writes=[sm.b])
                        fw.op(DVE, lambda h: h.reciprocal(out=sm.t[:, 4:5], in_=sm.t[:, 4:5]), reads=[sm.b], writes=[sm.b])
                        on_ = o_n[qs % 2]
                        fw.op(DVE, lambda h, on_=on_: h.scalar_tensor_tensor(out=on_.t[:], in0=o_o.t[:], scalar=sm.t[:, 4:5], in1=g_rep.t[:], op0=ALU.mult, op1=ALU.mult),
                              reads=[o_o.b, sm.b, g_rep.b], writes=[on_.b])
                        fw.op(PE, lambda h, on_=on_: h.transpose(trb.t[:, 0:128], on_.t[:], identB), reads=[on_.b, c_b.b], writes=[trb.b])
                        fw.op(ACT, lambda h, qs=qs, hd=hd: h.activation(out=attT.t[:, hd, qs * 128:(qs + 1) * 128], in_=trb.t[:, 0:128], func=AF.Copy),
                              reads=[trb.b], writes=[attT.b])

                load_kv(0)
                for hd in range(8):
                    if hd + 1 < 8:
                        load_kv(hd + 1)
                    blocks_of(hd)
                    if hd > 0:
                        finalize(hd - 1)
                    evac_o()
                finalize(7)
                layer1_tail(attT, 512, yT[:, I * 512:(I + 1) * 512].rearrange("(c p) t -> p c t", p=128))

        fw.dry = True
        emit_all()
        fw.dry = False
        state["mm"] = 0
        state["ws"] = 0
        state["k"] = 0
        emit_all()
        assert pst["i"] == len(plan)
        fw.finish()
        print("bass program: n_inst", fw.n_inst, "nsem", fw.nsem)
    return nc


def _consts(parity):
    cf = np.zeros((128, 908), np.float32)
    cf[:, 780:908] = 1.0
    cf[:, 0:128] = np.eye(128, dtype=np.float32)
    for r in range(128):
        hp, q = r // 8, r % 8
        c = (hp // 2) * 8 + q
        if hp % 2 == 0:
            cf[r, 128 + c] = 1.0
        else:
            cf[r, 192 + c] = 1.0
        cf[r, 256 + hp // 2] = 1.0
    cf[:, 264] = np.arange(128, dtype=np.float32)
    cf[:, 265] = NEG if parity == 0 else 0.0
    cf[:, 266] = LN_EPS
    cf[:, 267] = SUB_EPS
    for b in range(4):
        for r in range(32):
            bb, t = r // 8, r % 8
            for c in range(128):
                q = c % 8
                cf[r, 268 + b * 128 + c] = 1.0 if (bb == b and t <= q) else 0.0
    cb = np.zeros((128, 384), np.float32)
    cb[:, 0:128] = np.eye(128)
    cb[:, 128:256] = 1.0
    k = np.arange(128)[:, None]
    q = np.arange(128)[None, :]
    cb[:, 256:384] = (k <= q).astype(np.float32)
    return cf, cb.astype(ml_dtypes.bfloat16)


def _rope_tab(pos):
    inv = np.power(np.float32(500000.0), -np.arange(0, 16, 2, dtype=np.float32) / np.float32(16)).astype(np.float32)
    ang = pos.astype(np.float32)[:, None] * inv[None, :]
    return np.concatenate([np.cos(ang), np.sin(ang)], axis=-1).astype(np.float32)


_CACHE = {}


def run(inputs, NG, NPG):
    f = lambda k: np.ascontiguousarray(np.asarray(inputs[k]))
    x_prompt, x_sample, state_conv = f("x_prompt"), f("x_sample"), f("state_conv")
    cache_k, cache_v, page_table = f("cache_k"), f("cache_v"), f("page_table")
    B, S, _ = x_prompt.shape
    NPOOL = cache_k.shape[0]
    assert S == 2 * NG * 512 and page_table.shape[1] == NPG and B == 4
    key = (NG, NPG, NPOOL)
    if key not in _CACHE:
        _CACHE[key] = build(NG, NPG, NPOOL)
    nc = _CACHE[key]
    ck = cache_k.reshape(NPOOL * 128, D)
    cv = cache_v.reshape(NPOOL * 128, D)
    vecs = np.concatenate([f("w_conv").T,
                           np.stack([f("ln_mix_g")[0], f("ln_mix_b")[0], f("ln_ffn_g")[0], f("ln_ffn_b")[0],
                                     f("ln_mix_g")[1], f("ln_mix_b")[1], f("ln_ffn_g")[1], f("ln_ffn_b")[1]], axis=1)], axis=1).astype(np.float32)
    lamv = np.concatenate([f("lambda_q1"), f("lambda_k1"), f("lambda_q2"), f("lambda_k2")]).reshape(1, 256).astype(np.float32)
    grep = np.broadcast_to(f("subln_g")[None, :], (128, 128)).astype(np.float32).copy()
    past_len = NPG * 128
    ropeS = _rope_tab(past_len + (np.arange(32) % 8))
    in_maps = []
    orders = []
    for core in range(8):
        b, par = core // 2, core % 2
        own = [2 * I + par for I in range(NG)]
        oth = [2 * I + (1 - par) for I in range(NG)]
        gran = own + oth
        orders.append(gran)
        tok = np.concatenate([np.arange(g * 512, (g + 1) * 512) for g in gran])
        xT = np.ascontiguousarray(x_prompt[b][tok].T)
        xh = np.zeros((D, 2 * len(gran)), np.float32)
        for i, g in enumerate(gran):
            if g > 0:
                xh[:, 2 * i:2 * i + 2] = x_prompt[b][g * 512 - 2:g * 512].T
        rp = _rope_tab(tok).reshape(len(gran) * 4, 128, 16).transpose(1, 0, 2)
        cf, cb = _consts(par)
        sb = slice(core * 4, core * 4 + 4)
        in_maps.append({
            "xT": xT, "xh": xh,
            "xsT": np.ascontiguousarray(x_sample[sb].reshape(32, D).T),
            "sconvT": np.ascontiguousarray(state_conv[sb].reshape(8, D).T),
            "cache_k": ck, "cache_v": cv,
            "ptab": np.ascontiguousarray(page_table[sb].reshape(1, 4 * NPG).astype(np.int32)),
            "w_ci": f("w_conv_in"), "w_co": f("w_conv_out"), "w_qkv": f("w_qkv"), "w_ao": f("w_attn_out"),
            "w_fi": f("w_ffn_in"), "w_fo": f("w_ffn_out"), "vecs": vecs, "lamv": lamv, "grep": grep,
            "ropeP": np.ascontiguousarray(rp), "ropeS": ropeS, "cst_f": cf, "cst_b": cb,
        })
    res = run_bass_kernel_spmd(nc, in_maps, core_ids=list(range(8))).results
    y_p = np.zeros((B, S, D), np.float32)
    k_p = np.zeros((B, S, 16, 64), np.float32)
    v_p = np.zeros((B, S, 8, 128), np.float32)
    conv_p = np.zeros((B, 2, D), np.float32)
    y_s = np.zeros((32, 8, D), np.float32)
    conv_s = np.zeros((32, 2, D), np.float32)
    k_s = np.zeros((32, 8, 16, 64), np.float32)
    v_s = np.zeros((32, 8, 8, 128), np.float32)
    for core in range(8):
        b, par = core // 2, core % 2
        r = res[core]
        for I in range(NG):
            g = 2 * I + par
            y_p[b, g * 512:(g + 1) * 512] = r["yT"][:, I * 512:(I + 1) * 512].T
            k_p[b, g * 512:(g + 1) * 512] = r["kp"][I * 512:(I + 1) * 512].reshape(512, 16, 64)
            v_p[b, g * 512:(g + 1) * 512] = r["vp"][I * 512:(I + 1) * 512].reshape(512, 8, 128)
        if par == 1:
            conv_p[b] = r["convp"].T
        sb = slice(core * 4, core * 4 + 4)
        y_s[sb] = r["ysT"].T.reshape(4, 8, D)
        conv_s[sb] = r["convs"].T.reshape(4, 2, D)
        k_s[sb] = r["ks"].reshape(4, 8, 16, 64)
        v_s[sb] = r["vs"].reshape(4, 8, 8, 128)
    return (y_p, y_s, conv_p, k_p, v_p, conv_s, k_s, v_s)


def kernel(**inputs):
    return run(inputs, 4, 64)
```

```python
import math
import numpy as np
import ml_dtypes
from contextlib import ExitStack
import concourse.bass as bass
import concourse.mybir as mybir
from concourse.bass_utils import run_bass_kernel_spmd

F32 = mybir.dt.float32
BF16 = mybir.dt.bfloat16
I32 = mybir.dt.int32
AF = mybir.ActivationFunctionType
ALU = mybir.AluOpType
AX = mybir.AxisListType

D = 1024
KC = 8
DFF = 2816
FC = 22
ALPHA = (2 * 2) ** 0.25
SCALE = 64 ** -0.5
LN_EPS = 1e-5
SUB_EPS = 1e-5
LAM_INIT = 0.8 - 0.6 * math.exp(-0.3 * 1)
NEG = -30000.0


class Buf:
    __slots__ = ("name", "w", "rs", "dsem")

    def __init__(self, name):
        self.name = name
        self.w = None
        self.rs = {}
        self.dsem = None


class Sem:
    __slots__ = ("h", "cnt")

    def __init__(self, h):
        self.h = h
        self.cnt = 0


class Eng:
    def __init__(self, name, h, sem):
        self.name = name
        self.h = h
        self.sem = sem
        self.seen = {}
        self.pending = False


class TB:
    def __init__(self, t, name):
        self.t = t
        self.b = Buf(name)


class FW:
    def __init__(self, nc, stack):
        self.nc = nc
        self.stack = stack
        self.nsem = 0
        self.pe = self._eng("pe", nc.tensor)
        self.act = self._eng("act", nc.scalar)
        self.dve = self._eng("dve", nc.vector)
        self.pool = self._eng("pool", nc.gpsimd)
        self.sp = self._eng("sp", nc.sync)
        self.engs = [self.pe, self.act, self.dve, self.pool, self.sp]
        self.out_events = {}
        self.n_inst = 0
        self.dry = False

    def new_sem(self, name):
        self.nsem += 1
        return Sem(self.stack.enter_context(self.nc.semaphore(name)))

    def _eng(self, name, h):
        return Eng(name, h, self.new_sem("s_" + name))

    def _deps(self, E, reads, writes):
        need = {}
        for b in reads:
            if b.w is not None:
                s, v = b.w
                if need.get(s, 0) < v:
                    need[s] = v
        for b in writes:
            if b.w is not None:
                s, v = b.w
                if need.get(s, 0) < v:
                    need[s] = v
            for s, v in b.rs.items():
                if need.get(s, 0) < v:
                    need[s] = v
        for s, v in need.items():
            if s is E.sem and (v > s.cnt or E is self.pe):
                continue
            if E.seen.get(s, 0) >= v:
                continue
            E.h.wait_ge(s.h, v)
            E.seen[s] = v

    def _record(self, ev, reads, writes):
        s, v = ev
        for b in writes:
            b.w = ev
            b.rs = {}
        for b in reads:
            if b.rs.get(s, 0) < v:
                b.rs[s] = v

    def op(self, E, fn, reads=(), writes=(), signal=True):
        if self.dry:
            return None
        self._deps(E, reads, writes)
        ins = fn(E.h)
        self.n_inst += 1
        if signal:
            E.sem.cnt += 1
            ins.then_inc(E.sem.h, 1)
            ev = (E.sem, E.sem.cnt)
            E.pending = False
        else:
            ev = (E.sem, E.sem.cnt + 1)
            E.pending = True
        self._record(ev, reads, writes)
        return ins

    def dma(self, Q, fn, reads=(), writes=(), sembuf=None, is_out=False):
        if self.dry:
            return None
        self._deps(Q, reads, writes)
        if sembuf.dsem is None:
            sembuf.dsem = self.new_sem("d_" + sembuf.name)
        s = sembuf.dsem
        ins = fn(Q.h)
        self.n_inst += 1
        s.cnt += 16
        ins.then_inc(s.h, 16)
        ev = (s, s.cnt)
        self._record(ev, reads, writes)
        if is_out:
            self.out_events[s] = s.cnt
        return ins

    def finish(self):
        assert not self.pe.pending
        E = self.sp
        for s, v in self.out_events.items():
            if E.seen.get(s, 0) < v:
                E.h.wait_ge(s.h, v)
                E.seen[s] = v
        for X in self.engs:
            if X is E or X.sem.cnt == 0:
                continue
            if E.seen.get(X.sem, 0) < X.sem.cnt:
                E.h.wait_ge(X.sem.h, X.sem.cnt)


def view(ap, name):
    v = TB.__new__(TB)
    v.t = ap
    v.b = Buf(name)
    return v


def merge_bufs(dst, srcs):
    for s_ in srcs:
        evs = dict(s_.rs)
        if s_.w is not None:
            s, v = s_.w
            if evs.get(s, 0) < v:
                evs[s] = v
        for s, v in evs.items():
            if dst.rs.get(s, 0) < v:
                dst.rs[s] = v


def build(NG, NPG, NPOOL):
    NT = 2 * NG
    NTOK = NT * 512
    NB = NT * 4
    NPT = 4 * NPG
    nc = bass.Bass("TRN2", target_bir_lowering=False)
    dt = lambda n, s, d, k: nc.dram_tensor(n, s, d, kind=k).ap()
    IN, OUT, SCR = "ExternalInput", "ExternalOutput", "Internal"
    xT = dt("xT", [D, NTOK], F32, IN)
    xh = dt("xh", [D, NT * 2], F32, IN)
    xsT = dt("xsT", [D, 32], F32, IN)
    sconvT = dt("sconvT", [D, 8], F32, IN)
    cache_k = dt("cache_k", [NPOOL * 128, D], F32, IN)
    cache_v = dt("cache_v", [NPOOL * 128, D], F32, IN)
    ptab = dt("ptab", [1, NPT], I32, IN)
    w_ci = dt("w_ci", [D, 3 * D], F32, IN)
    w_co = dt("w_co", [D, D], F32, IN)
    w_qkv = dt("w_qkv", [D, 3 * D], F32, IN)
    w_ao = dt("w_ao", [D, D], F32, IN)
    w_fi = dt("w_fi", [2, D, 2 * DFF], F32, IN)
    w_fo = dt("w_fo", [2, DFF, D], F32, IN)
    vecs = dt("vecs", [D, 11], F32, IN)
    lamv = dt("lamv", [1, 256], F32, IN)
    grep = dt("grep", [128, 128], F32, IN)
    ropeP = dt("ropeP", [128, NB, 16], F32, IN)
    ropeS = dt("ropeS", [32, 16], F32, IN)
    cst_f = dt("cst_f", [128, 908], F32, IN)
    cst_b = dt("cst_b", [128, 384], BF16, IN)
    yT = dt("yT", [D, NG * 512], F32, OUT)
    ysT = dt("ysT", [D, 32], F32, OUT)
    kp = dt("kp", [NG * 512, D], F32, OUT)
    vp = dt("vp", [NG * 512, D], F32, OUT)
    convp = dt("convp", [D, 2], F32, OUT)
    convs = dt("convs", [D, 8], F32, OUT)
    ks = dt("ks", [32, D], F32, OUT)
    vs = dt("vs", [32, D], F32, OUT)
    s_ci = dt("s_ci", [D, 3 * D], BF16, SCR)
    s_co = dt("s_co", [D, D], BF16, SCR)
    s_qkv = dt("s_qkv", [D, 3 * D], BF16, SCR)
    s_ao = dt("s_ao", [D, D], BF16, SCR)
    s_fi = dt("s_fi", [2, D, 2 * DFF], BF16, SCR)
    s_fo = dt("s_fo", [2, DFF, D], BF16, SCR)
    s_kT = dt("s_kT", [D, NTOK], BF16, SCR)
    s_v = dt("s_v", [NTOK, 8 * 129], BF16, SCR)
    s_x2 = dt("s_x2", [D, NG * 512], F32, SCR)

    with ExitStack() as st:
        fw = FW(nc, st)
        PE, ACT, DVE, POOL, SP = fw.pe, fw.act, fw.dve, fw.pool, fw.sp

        def T(name, shape, dtype):
            return TB(st.enter_context(nc.sbuf_tensor(name, shape, dtype)), name)

        def PS(name, shape, dtype):
            return TB(st.enter_context(nc.psum_tensor(name, shape, dtype)), name)

        c_f = T("c_f", [128, 908], F32)
        c_b = T("c_b", [128, 384], BF16)
        vec = T("vec", [128, KC, 11], F32)
        g_rep = T("g_rep", [128, 128], F32)
        rP = T("rP", [128, NB, 16], F32)
        rS = T("rS", [32, 16], F32)
        lamb = T("lamb", [128, 2], F32)
        Cmat = T("Cmat", [128, 64], F32)
        pt_i = T("pt_i", [128, NPT], I32)
        pt_x = T("pt_x", [128, NPT], I32)
        xf = T("xf", [128, KC, 512], F32)
        xb = T("xb", [128, KC, 512], BF16)
        gbuf = T("gbuf", [128, KC, 512], BF16)
        x1b = T("x1b", [128, KC, 512], BF16)
        hb = [T(f"hb{i}", [128, 512], BF16) for i in range(FC)]
        t1 = T("t1", [128, 512], F32)
        t2 = T("t2", [128, 512], F32)
        lam_sb = t1
        ln_m = t2
        ln_q = t1
        qz = [T(f"qz{i}", [128, 2, 512], BF16) for i in range(2)]
        pt_f = view(t2.t[:, 0:NPT], "pt_f")
        pt_f.b = t2.b
        upad = T("upad", [128, 520], F32)
        uhal = T("uhal", [128, 8], F32)
        cvo = T("cvo", [128, KC, 8], F32)
        ln_r = T("ln_r", [128, 512], F32)
        ln_n = T("ln_n", [128, 512], F32)
        ln_b1 = [T(f"ln_b1_{i}", [128, 512], BF16) for i in range(2)]
        ln_b2 = [T(f"ln_b2_{i}", [128, 512], BF16) for i in range(2)]
        tmf = [T(f"tmf{i}", [128, D], F32) for i in range(2)]
        tmb = [T(f"tmb{i}", [128, D], BF16) for i in range(2)]
        rt = [T(f"rt{i}", [128, 16, 8], F32) for i in range(4)]
        kTt = gbuf
        t3 = T("t3", [128, 512], F32)
        vbt = [T(f"vbt{i}", [128, 8, 129], BF16) for i in range(2)]
        wsl = [T(f"wsl{i}", [128, 4096], BF16) for i in range(5)]
        NKB = 4 * NG
        KTW = max(NKB * 128, 2048)
        kt = [T(f"kt{i}", [128, 2, KTW], BF16) for i in range(2)]
        vt = [T(f"vt{i}", [128, 2 * NKB, 129], BF16) for i in range(2)]
        pT = [T(f"pT{i}", [128, 512], BF16) for i in range(4)]
        sm = T("sm", [128, 16], F32)
        o_t = T("o_t", [128, 128], F32)
        o_o = T("o_o", [128, 128], F32)
        o_n = [T(f"o_n{i}", [128, 128], BF16) for i in range(2)]
        osb1 = T("osb1", [128, 3, 387], F32)
        _ob1 = osb1.t[:].rearrange("p a b -> p (a b)")
        qblk = view(_ob1[:, 0:512].bitcast(BF16).rearrange("p (c n) -> p c n", n=128), "qblk")
        kpg = [view(kt[0].t[:, 0, 0:1024], "kpg0"), view(kt[0].t[:, 0, 1024:2048], "kpg1"), view(kt[0].t[:, 1, 0:1024], "kpg2")]
        vpg = [view(kt[1].t[:, 0, 0:1024], "vpg0"), view(kt[1].t[:, 0, 1024:2048], "vpg1"), view(kt[1].t[:, 1, 0:1024], "vpg2")]
        kTp = [view(kt[0].t[:, 1, 1024:2048], "kTp0"), view(kt[1].t[:, 1, 1024:2048], "kTp1")]
        osb0 = T("osb0", [128, 3, 387], F32)
        osbs = [osb0, osb1]
        o_o4 = T("o_o4", [128, 4, 128], F32)
        on4 = T("on4", [128, 4, 128], BF16)
        xs2 = T("xs2", [128, KC, 32], F32)
        pTs = [T(f"pTs{i}", [128, 128], BF16) for i in range(2)]
        kTs = T("kTs", [128, KC, 32], BF16)
        qTs = T("qTs", [128, KC, 32], BF16)
        vsb = view(_ob1[0:32, 512:1024].bitcast(BF16), "vsb")
        od = tmf[0]
        orr = T("orr", [128, 128], F32)
        attS = T("attS", [128, KC, 32], BF16)
        zer = T("zer", [128, 512], BF16)
        mmb = [PS(f"mm{i}", [128, 512], F32) for i in range(3)]
        sbank = PS("sbank", [128, 512], F32)
        trb = PS("trb", [128, 1024], BF16)
        trbF = view(trb.t[:, 0:8].bitcast(F32), "trbF")
        trbF.b = trb.b
        ob = [PS(f"ob{i}", [128, 512], F32) for i in range(3)]

        state = {"mm": 0, "ws": 0, "k": 0}

        tk = {"gen": None}

        def tick(n=1):
            g = tk["gen"]
            if g is None:
                return
            for _ in range(n):
                try:
                    next(g)
                except StopIteration:
                    tk["gen"] = None
                    return

        def next_mm():
            b = mmb[state["mm"] % 3]
            state["mm"] += 1
            return b

        identF = c_f.t[:, 0:128]
        C1 = c_f.t[:, 128:192]
        C2 = c_f.t[:, 192:256]
        mask3 = c_f.t[:, 256:264]
        flag = c_f.t[:, 265:266]
        eps_ln = c_f.t[:, 266:267]
        eps_sub = c_f.t[:, 267:268]
        masknew = c_f.t[0:32, 268:268 + 512]
        identB = c_b.t[:, 0:128]
        onesB = c_b.t[:, 128:256]
        tri = c_b.t[:, 256:384]

        fw.op(POOL, lambda h: h.memset(zer.t[:], 0.0), writes=[zer.b])
        for vb_ in vbt:
            fw.op(POOL, lambda h, vb_=vb_: h.memset(vb_.t[:], 1.0), writes=[vb_.b])
        for qz_ in qz:
            fw.op(POOL, lambda h, qz_=qz_: h.memset(qz_.t[:], 0.0), writes=[qz_.b])
        def ld(tb, src, q=SP):
            fw.dma(q, lambda h: h.dma_start(out=tb.t[:], in_=src), writes=[tb.b], sembuf=tb.b)
        ld(c_f, cst_f)
        ld(c_b, cst_b)
        fw.dma(SP, lambda h: h.dma_start(out=vec.t[:], in_=vecs.rearrange("(c p) k -> p c k", p=128)), writes=[vec.b], sembuf=vec.b)
        ld(g_rep, grep)
        ld(rP, ropeP)
        ld(rS, ropeS)
        fw.dma(SP, lambda h: h.dma_start(out=lam_sb.t[0:1, 0:256], in_=lamv), writes=[lam_sb.b], sembuf=lam_sb.b)
        fw.dma(SP, lambda h: h.dma_start(out=pt_i.t[:], in_=ptab.partition_broadcast(128)), writes=[pt_i.b], sembuf=pt_i.b)
        wb = {}
        wlist = [("ci", w_ci, s_ci), ("co", w_co, s_co), ("fi0", w_fi[0], s_fi[0]), ("fo0", w_fo[0], s_fo[0]),
                 ("qkv", w_qkv, s_qkv), ("ao", w_ao, s_ao), ("fi1", w_fi[1], s_fi[1]), ("fo1", w_fo[1], s_fo[1])]
        for name, src, dst in wlist:
            wb[name] = Buf("w_" + name)

        def convert(names):
            for name, src, dst in wlist:
                if name not in names:
                    continue
                b = wb[name]
                nk = src.shape[0] // 128
                for k0 in range(0, nk, 4):
                    k1 = min(nk, k0 + 4)
                    fw.dma(POOL, lambda h, src=src, dst=dst, k0=k0, k1=k1: h.dma_start(
                        out=dst[k0 * 128:k1 * 128, :], in_=src[k0 * 128:k1 * 128, :]), writes=[b], sembuf=b)
        convert(["ci", "co", "fi0", "fo0", "qkv", "ao", "fi1", "fo1"])
        fw.op(DVE, lambda h: h.tensor_scalar(out=g_rep.t[:], in0=g_rep.t[:], scalar1=float(1.0 - LAM_INIT), scalar2=None, op0=ALU.mult),
              reads=[g_rep.b], writes=[g_rep.b])
        L = lam_sb.t
        fw.op(DVE, lambda h: h.tensor_tensor(out=L[0:1, 0:64], in0=L[0:1, 0:64], in1=L[0:1, 64:128], op=ALU.mult), reads=[lam_sb.b], writes=[lam_sb.b])
        fw.op(DVE, lambda h: h.tensor_tensor(out=L[0:1, 128:192], in0=L[0:1, 128:192], in1=L[0:1, 192:256], op=ALU.mult), reads=[lam_sb.b], writes=[lam_sb.b])
        fw.op(DVE, lambda h: h.reduce_sum(out=L[0:1, 256:257], in_=L[0:1, 0:64], axis=AX.X), reads=[lam_sb.b], writes=[lam_sb.b])
        fw.op(DVE, lambda h: h.reduce_sum(out=L[0:1, 257:258], in_=L[0:1, 128:192], axis=AX.X), reads=[lam_sb.b], writes=[lam_sb.b])
        fw.op(ACT, lambda h: h.activation(out=L[0:1, 256:258], in_=L[0:1, 256:258], func=AF.Exp), reads=[lam_sb.b], writes=[lam_sb.b])
        fw.op(DVE, lambda h: h.tensor_tensor(out=L[0:1, 258:259], in0=L[0:1, 256:257], in1=L[0:1, 257:258], op=ALU.subtract), reads=[lam_sb.b], writes=[lam_sb.b])
        fw.op(DVE, lambda h: h.tensor_scalar(out=L[0:1, 258:259], in0=L[0:1, 258:259], scalar1=float(LAM_INIT), scalar2=None, op0=ALU.add), reads=[lam_sb.b], writes=[lam_sb.b])
        fw.op(DVE, lambda h: h.tensor_scalar(out=L[0:1, 259:260], in0=L[0:1, 258:259], scalar1=-1.0, scalar2=None, op0=ALU.mult), reads=[lam_sb.b], writes=[lam_sb.b])
        b0 = next_mm()
        fw.op(PE, lambda h: h.matmul(b0.t[:, 0:2], lhsT=c_f.t[0:1, 780:908], rhs=L[0:1, 258:260], start=True, stop=True),
              reads=[c_f.b, lam_sb.b], writes=[b0.b])
        fw.op(DVE, lambda h: h.tensor_copy(out=lamb.t[:], in_=b0.t[:, 0:2]), reads=[b0.b], writes=[lamb.b])
        fw.op(DVE, lambda h: h.scalar_tensor_tensor(out=Cmat.t[:], in0=C2, scalar=lamb.t[:, 1:2], in1=C1, op0=ALU.mult, op1=ALU.add),
              reads=[c_f.b, lamb.b], writes=[Cmat.b])
        fw.op(DVE, lambda h: h.tensor_copy(out=pt_f.t[:], in_=pt_i.t[:]), reads=[pt_i.b], writes=[pt_f.b])
        fw.op(DVE, lambda h: h.tensor_scalar(out=pt_f.t[:], in0=pt_f.t[:], scalar1=128.0, scalar2=c_f.t[:, 264:265], op0=ALU.mult, op1=ALU.add),
              reads=[pt_f.b, c_f.b], writes=[pt_f.b])
        fw.op(DVE, lambda h: h.tensor_copy(out=pt_x.t[:], in_=pt_f.t[:]), reads=[pt_f.b], writes=[pt_x.b])

        plan = []
        pst = {"i": 0, "issued": 0}
        NS, LOOK = len(wsl), 2

        def issue_w(j):
            scr, wbuf, k0, nk, c0, ncols = plan[j]
            slot = wsl[j % NS]
            fw.dma(SP, lambda h: h.dma_start(
                out=slot.t[:, 0:nk * ncols].rearrange("p (k n) -> p k n", n=ncols),
                in_=scr[k0 * 128:(k0 + nk) * 128, c0:c0 + ncols].rearrange("(k p) n -> p k n", p=128)),
                reads=[wbuf], writes=[slot.b], sembuf=slot.b)

        def load_w(scr, wbuf, k0, nk, c0, ncols):
            if fw.dry:
                plan.append((scr, wbuf, k0, nk, c0, ncols))
                return wsl[0]
            i = pst["i"]
            pst["i"] += 1
            assert plan[i][2:] == (k0, nk, c0, ncols)
            while pst["issued"] < min(len(plan), i + 1 + LOOK):
                issue_w(pst["issued"])
                pst["issued"] += 1
            return wsl[i % NS]

        def linear_fm(inp, in_bufs, nkc, scr, wbuf, c0, noc_total, n, evac):
            for og in range(0, noc_total, 3):
                noc = min(3, noc_total - og)
                banks = [next_mm() for _ in range(noc)]
                for k0 in range(0, nkc, 8):
                    nk = min(8, nkc - k0)
                    slot = load_w(scr, wbuf, k0, nk, c0 + og * 128, noc * 128)
                    for j in range(noc):
                        for kk in range(nk):
                            kc = k0 + kk
                            fw.op(PE, lambda h, j=j, kk=kk, kc=kc, slot=slot, noc=noc: h.matmul(
                                banks[j].t[:, 0:n], lhsT=slot.t[:, kk * noc * 128 + j * 128: kk * noc * 128 + (j + 1) * 128],
                                rhs=inp(kc), start=(kc == 0), stop=(kc == nkc - 1)),
                                reads=[slot.b] + in_bufs, writes=[banks[j].b], signal=(kk == nk - 1))
                    tick()
                for j in range(noc):
                    evac(og + j, banks[j])

        def linear_tm(inp, in_bufs, nrows, ntb, scr, wbuf, c0, evac):
            slot = load_w(scr, wbuf, 0, KC, c0, 512)
            for tb in range(ntb):
                bank = next_mm()
                for kc in range(KC):
                    fw.op(PE, lambda h, kc=kc, tb=tb, bank=bank: h.matmul(
                        bank.t[0:nrows, :], lhsT=inp(kc, tb), rhs=slot.t[:, kc * 512:(kc + 1) * 512],
                        start=(kc == 0), stop=(kc == KC - 1)),
                        reads=[slot.b] + in_bufs, writes=[bank.b], signal=(kc == KC - 1))
                evac(tb, bank)

        def layernorm(y, n, gcol, bcol, outb, out_dram=None):
            s1, s2 = next_mm(), next_mm()
            for c in range(KC):
                a = ln_b1[c % 2]
                q = ln_b2[c % 2]
                fw.op(ACT, lambda h, c=c, a=a: h.activation(out=a.t[:, 0:n], in_=y.t[:, c, 0:n], func=AF.Copy), reads=[y.b], writes=[a.b])
                fw.op(ACT, lambda h, c=c, q=q: h.activation(out=q.t[:, 0:n], in_=y.t[:, c, 0:n], func=AF.Square), reads=[y.b], writes=[q.b])
                fw.op(PE, lambda h, c=c, a=a: h.matmul(s1.t[:, 0:n], lhsT=onesB, rhs=a.t[:, 0:n], start=(c == 0), stop=(c == KC - 1)),
                      reads=[a.b, c_b.b], writes=[s1.b])
                fw.op(PE, lambda h, c=c, q=q: h.matmul(s2.t[:, 0:n], lhsT=onesB, rhs=q.t[:, 0:n], start=(c == 0), stop=(c == KC - 1)),
                      reads=[q.b, c_b.b], writes=[s2.b])
            tick()
            m, q, r, nm = ln_m, ln_q, ln_r, ln_n
            fw.op(DVE, lambda h: h.tensor_scalar(out=m.t[:, 0:n], in0=s1.t[:, 0:n], scalar1=1.0 / D, scalar2=None, op0=ALU.mult), reads=[s1.b], writes=[m.b])
            fw.op(DVE, lambda h: h.tensor_tensor(out=q.t[:, 0:n], in0=m.t[:, 0:n], in1=m.t[:, 0:n], op=ALU.mult), reads=[m.b], writes=[q.b])
            fw.op(DVE, lambda h: h.scalar_tensor_tensor(out=q.t[:, 0:n], in0=s2.t[:, 0:n], scalar=1.0 / D, in1=q.t[:, 0:n], op0=ALU.mult, op1=ALU.subtract),
                  reads=[s2.b, q.b], writes=[q.b])
            fw.op(ACT, lambda h: h.activation(out=r.t[:, 0:n], in_=q.t[:, 0:n], func=AF.Sqrt, bias=eps_ln, scale=1.0), reads=[q.b, c_f.b], writes=[r.b])
            fw.op(DVE, lambda h: h.reciprocal(out=r.t[:, 0:n], in_=r.t[:, 0:n]), reads=[r.b], writes=[r.b])
            fw.op(DVE, lambda h: h.scalar_tensor_tensor(out=nm.t[:, 0:n], in0=m.t[:, 0:n], scalar=-1.0, in1=r.t[:, 0:n], op0=ALU.mult, op1=ALU.mult),
                  reads=[m.b, r.b], writes=[nm.b])
            for c in range(KC):
                fw.op(DVE, lambda h, c=c: h.tensor_tensor(out=y.t[:, c, 0:n], in0=y.t[:, c, 0:n], in1=r.t[:, 0:n], op=ALU.mult), reads=[y.b, r.b], writes=[y.b])
                fw.op(DVE, lambda h, c=c: h.tensor_tensor(out=y.t[:, c, 0:n], in0=y.t[:, c, 0:n], in1=nm.t[:, 0:n], op=ALU.add), reads=[y.b, nm.b], writes=[y.b])
                fw.op(ACT, lambda h, c=c: h.activation(out=y.t[:, c, 0:n], in_=y.t[:, c, 0:n], func=AF.Identity,
                                                       scale=vec.t[:, c, gcol:gcol + 1], bias=vec.t[:, c, bcol:bcol + 1]),
                      reads=[y.b, vec.b], writes=[y.b])
                fw.op(ACT, lambda h, c=c: h.activation(out=outb.t[:, c, 0:n], in_=y.t[:, c, 0:n], func=AF.Copy), reads=[y.b], writes=[outb.b])
                if c % 2 == 1:
                    tick()

        def ffn(layer, inb, y, n):
            wfi, wfo = wb[f"fi{layer}"], wb[f"fo{layer}"]
            for j0 in range(0, FC, 2):
                sg = load_w(s_fi[layer], wfi, 0, KC, j0 * 128, 256)
                su = load_w(s_fi[layer], wfi, 0, KC, DFF + j0 * 128, 256)
                for jj in range(2):
                    j = j0 + jj
                    bg, bu = next_mm(), next_mm()
                    for kc in range(KC):
                        fw.op(PE, lambda h, kc=kc, jj=jj, bg=bg: h.matmul(bg.t[:, 0:n], lhsT=sg.t[:, kc * 256 + jj * 128: kc * 256 + (jj + 1) * 128],
                              rhs=inb.t[:, kc, 0:n], start=(kc == 0), stop=(kc == KC - 1)), reads=[sg.b, inb.b], writes=[bg.b], signal=(kc == KC - 1))
                    for kc in range(KC):
                        fw.op(PE, lambda h, kc=kc, jj=jj, bu=bu: h.matmul(bu.t[:, 0:n], lhsT=su.t[:, kc * 256 + jj * 128: kc * 256 + (jj + 1) * 128],
                              rhs=inb.t[:, kc, 0:n], start=(kc == 0), stop=(kc == KC - 1)), reads=[su.b, inb.b], writes=[bu.b], signal=(kc == KC - 1))
                    tt = t1 if j % 2 == 0 else t2
                    fw.op(ACT, lambda h, bg=bg, tt=tt: h.activation(out=tt.t[:, 0:n], in_=bg.t[:, 0:n], func=AF.Silu), reads=[bg.b], writes=[tt.b])
                    fw.op(DVE, lambda h, bu=bu, tt=tt, j=j: h.tensor_tensor(out=hb[j].t[:, 0:n], in0=bu.t[:, 0:n], in1=tt.t[:, 0:n], op=ALU.mult),
                          reads=[bu.b, tt.b], writes=[hb[j].b])
                    tick()

            def ev(oc, bank):
                fw.op(DVE, lambda h: h.scalar_tensor_tensor(out=y.t[:, oc, 0:n], in0=y.t[:, oc, 0:n], scalar=float(ALPHA), in1=bank.t[:, 0:n],
                                                            op0=ALU.mult, op1=ALU.add), reads=[y.b, bank.b], writes=[y.b])
            linear_fm(lambda kc: hb[kc].t[:, 0:n], [h_.b for h_ in hb], FC, s_fo[layer], wfo, 0, KC, n, ev)

        def rope(tm, nrows, cos, sin):
            v = tm.t[0:nrows, :].rearrange("p (h d) -> p h d", d=64)
            x1 = v[:, :, 0:8]
            x2 = v[:, :, 8:16]
            cb = cos.unsqueeze(1).to_broadcast([nrows, 16, 8])
            sb = sin.unsqueeze(1).to_broadcast([nrows, 16, 8])
            a, b, c_, d_ = [r.t[0:nrows] for r in rt]
            rb = [r.b for r in rt]
            fw.op(DVE, lambda h: h.tensor_tensor(out=a, in0=x1, in1=cb, op=ALU.mult), reads=[tm.b, rP.b, rS.b], writes=[rb[0]])
            fw.op(DVE, lambda h: h.tensor_tensor(out=b, in0=x2, in1=sb, op=ALU.mult), reads=[tm.b, rP.b, rS.b], writes=[rb[1]])
            fw.op(DVE, lambda h: h.tensor_tensor(out=c_, in0=x2, in1=cb, op=ALU.mult), reads=[tm.b, rP.b, rS.b], writes=[rb[2]])
            fw.op(DVE, lambda h: h.tensor_tensor(out=d_, in0=x1, in1=sb, op=ALU.mult), reads=[tm.b, rP.b, rS.b], writes=[rb[3]])
            fw.op(DVE, lambda h: h.tensor_tensor(out=x1, in0=a, in1=b, op=ALU.subtract), reads=[rb[0], rb[1]], writes=[tm.b])
            fw.op(DVE, lambda h: h.tensor_tensor(out=x2, in0=c_, in1=d_, op=ALU.add), reads=[rb[2], rb[3]], writes=[tm.b])

        def qk_proj(inb, nrows, ntb, col0, cosf, sinf, dstT, out_dram):
            tms = {}
            deferred = []

            def ev_half(half):
                def ev(tb, bank):
                    tm = tmf[tb % 2]
                    fw.op(ACT, lambda h: h.activation(out=tm.t[0:nrows, half * 512:(half + 1) * 512], in_=bank.t[0:nrows, :], func=AF.Copy),
                          reads=[bank.b], writes=[tm.b])
                    if half == 1:
                        rope(tm, nrows, cosf(tb), sinf(tb))
                        if out_dram is not None:
                            fw.dma(SP, lambda h: h.dma_start(out=out_dram(tb), in_=tm.t[0:nrows, :]), reads=[tm.b], sembuf=tm.b, is_out=True)
                        tbf = tmb[tb % 2]
                        fw.op(DVE, lambda h: h.tensor_copy(out=tbf.t[0:nrows, :], in_=tm.t[0:nrows, :]), reads=[tm.b], writes=[tbf.b])
                        def do_tr(tb=tb, tbf=tbf):
                            for c in range(KC):
                                fw.op(PE, lambda h, c=c: h.transpose(trb.t[:, c * 128:c * 128 + nrows], tbf.t[0:nrows, c * 128:(c + 1) * 128], identB[0:nrows, 0:nrows]),
                                      reads=[tbf.b, c_b.b], writes=[trb.b])
                            fw.op(ACT, lambda h: h.activation(out=dstT.t[:, :, tb * nrows:(tb + 1) * nrows],
                                                              in_=trb.t[:, :].rearrange("p (c t) -> p c t", t=128)[:, :, 0:nrows], func=AF.Copy),
                                  reads=[trb.b], writes=[dstT.b])
                        deferred.append(do_tr)
                return ev
            for tb0 in range(0, ntb, 2):
                nt_ = min(2, ntb - tb0)
                for half in range(2):
                    slot = load_w(s_qkv, wb["qkv"], 0, KC, col0 + half * 512, 512)
                    for tb in range(tb0, tb0 + nt_):
                        bank = next_mm()
                        for kc in range(KC):
                            fw.op(PE, lambda h, kc=kc, tb=tb, bank=bank, slot=slot: h.matmul(
                                bank.t[0:nrows, :], lhsT=inb.t[:, kc, tb * nrows:(tb + 1) * nrows], rhs=slot.t[:, kc * 512:(kc + 1) * 512],
                                start=(kc == 0), stop=(kc == KC - 1)), reads=[slot.b, inb.b], writes=[bank.b], signal=(kc == KC - 1))
                        if half == 0 and tb == tb0:
                            while deferred:
                                deferred.pop(0)()
                        ev_half(half)(tb, bank)
                        tick()
            while deferred:
                deferred.pop(0)()

        def v_proj(inb, nrows, ntb, out_dram, vdst):
            for tb0 in range(0, ntb, 2):
                nt_ = min(2, ntb - tb0)
                for half in range(2):
                    slot = load_w(s_qkv, wb["qkv"], 0, KC, 2048 + half * 512, 512)
                    for tb in range(tb0, tb0 + nt_):
                        bank = next_mm()
                        for kc in range(KC):
                            fw.op(PE, lambda h, kc=kc, tb=tb, bank=bank, slot=slot: h.matmul(
                                bank.t[0:nrows, :], lhsT=inb.t[:, kc, tb * nrows:(tb + 1) * nrows], rhs=slot.t[:, kc * 512:(kc + 1) * 512],
                                start=(kc == 0), stop=(kc == KC - 1)), reads=[slot.b, inb.b], writes=[bank.b], signal=(kc == KC - 1))
                        tm = tmf[tb % 2]
                        fw.op(ACT, lambda h, tm=tm, bank=bank, half=half: h.activation(out=tm.t[0:nrows, half * 512:(half + 1) * 512], in_=bank.t[0:nrows, :], func=AF.Copy),
                              reads=[bank.b], writes=[tm.b])
                        if half == 1:
                            if out_dram is not None:
                                fw.dma(SP, lambda h, tm=tm, tb=tb: h.dma_start(out=out_dram(tb), in_=tm.t[0:nrows, :]), reads=[tm.b], sembuf=tm.b, is_out=True)
                            vdst(tb, tm)
                        tick()

        def layer0(n, nb, seg, x_src, halo_src, state_src, conv_out):
            fw.dma(SP, lambda h: h.dma_start(out=xf.t[:, :, 0:n], in_=x_src), writes=[xf.b], sembuf=xf.b)
            for c in range(KC):
                fw.op(ACT if c % 2 else DVE, (lambda h, c=c: h.activation(out=xb.t[:, c, 0:n], in_=xf.t[:, c, 0:n], func=AF.Copy)) if c % 2 else
                      (lambda h, c=c: h.tensor_copy(out=xb.t[:, c, 0:n], in_=xf.t[:, c, 0:n])), reads=[xf.b], writes=[xb.b])
            if halo_src is not None:
                fw.dma(SP, lambda h: h.dma_start(out=t2.t[:, 0:16].rearrange("p (c t) -> p c t", t=2), in_=halo_src), writes=[t2.b], sembuf=t2.b)
                fw.op(DVE, lambda h: h.tensor_copy(out=ln_b1[0].t[:, 0:16], in_=t2.t[:, 0:16]), reads=[t2.b], writes=[ln_b1[0].b])
            up3 = upad.t[:, 0:nb * (seg + 2)].rearrange("p (b s) -> p b s", s=seg + 2)
            for c in range(KC):
                if c % 4 == 0:
                    sl = [load_w(s_ci, wb["ci"], 0, KC, part * 1024 + c * 128, 512) for part in range(3)]
                j = c % 4
                banks = [next_mm() for _ in range(3)]
                for part in (1, 2, 0):
                    for kc in range(KC):
                        fw.op(PE, lambda h, kc=kc, part=part, j=j, sl=sl, banks=banks: h.matmul(
                            banks[part].t[:, 0:n], lhsT=sl[part].t[:, kc * 512 + j * 128: kc * 512 + (j + 1) * 128], rhs=xb.t[:, kc, 0:n],
                            start=(kc == 0), stop=(kc == KC - 1)), reads=[sl[part].b, xb.b], writes=[banks[part].b], signal=(kc == KC - 1))
                bgb, bgc, bh = banks
                fw.op(ACT, lambda h, bgc=bgc: h.activation(out=t1.t[:, 0:n], in_=bgc.t[:, 0:n], func=AF.Copy), reads=[bgc.b], writes=[t1.b])
                fw.op(ACT, lambda h, bgb=bgb: h.activation(out=t3.t[:, 0:n], in_=bgb.t[:, 0:n], func=AF.Copy), reads=[bgb.b], writes=[t3.b])
                fw.op(DVE, lambda h, bh=bh: h.tensor_tensor(out=up3[:, :, 2:seg + 2], in0=t1.t[:, 0:n].rearrange("p (b s) -> p b s", s=seg),
                                                            in1=bh.t[:, 0:n].rearrange("p (b s) -> p b s", s=seg), op=ALU.mult),
                      reads=[t1.b, bh.b], writes=[upad.b])
                if halo_src is not None:
                    bk = trbF
                    for part in (1, 2):
                        for kc in range(KC):
                            fw.op(PE, lambda h, kc=kc, part=part, j=j, sl=sl, bk=bk: h.matmul(
                                bk.t[:, (part - 1) * 2:(part - 1) * 2 + 2], lhsT=sl[part].t[:, kc * 512 + j * 128: kc * 512 + (j + 1) * 128],
                                rhs=ln_b1[0].t[:, kc * 2:kc * 2 + 2], start=(kc == 0), stop=(kc == KC - 1)),
                                reads=[sl[part].b, ln_b1[0].b], writes=[bk.b], signal=(kc == KC - 1))
                    fw.op(ACT, lambda h, bk=bk: h.activation(out=uhal.t[:, 0:2], in_=bk.t[:, 0:2], func=AF.Copy), reads=[bk.b], writes=[uhal.b])
                    fw.op(DVE, lambda h, bk=bk: h.tensor_tensor(out=upad.t[:, 0:2], in0=uhal.t[:, 0:2], in1=bk.t[:, 2:4], op=ALU.mult),
                          reads=[uhal.b, bk.b], writes=[upad.b])
                else:
                    fw.dma(SP, lambda h, c=c: h.dma_start(out=up3[:, :, 0:2], in_=state_src[c * 128:(c + 1) * 128, :].rearrange("p (b t) -> p b t", t=2)),
                           writes=[upad.b], sembuf=upad.b)
                tv = t2.t[:, 0:n].rearrange("p (b s) -> p b s", s=seg)
                fw.op(DVE, lambda h, c=c: h.tensor_scalar(out=tv, in0=up3[:, :, 0:seg], scalar1=vec.t[:, c, 0:1], scalar2=None, op0=ALU.mult),
                      reads=[upad.b, vec.b], writes=[t2.b])
                fw.op(DVE, lambda h, c=c: h.scalar_tensor_tensor(out=tv, in0=up3[:, :, 1:seg + 1], scalar=vec.t[:, c, 1:2], in1=tv, op0=ALU.mult, op1=ALU.add),
                      reads=[upad.b, vec.b, t2.b], writes=[t2.b])
                fw.op(DVE, lambda h, c=c: h.scalar_tensor_tensor(out=tv, in0=up3[:, :, 2:seg + 2], scalar=vec.t[:, c, 2:3], in1=tv, op0=ALU.mult, op1=ALU.add),
                      reads=[upad.b, vec.b, t2.b], writes=[t2.b])
                fw.op(DVE, lambda h, c=c: h.tensor_tensor(out=gbuf.t[:, c, 0:n], in0=t3.t[:, 0:n], in1=t2.t[:, 0:n], op=ALU.mult),
                      reads=[t3.b, t2.b], writes=[gbuf.b])
                tick()
                if conv_out is not None:
                    fw.op(DVE, lambda h, c=c: h.tensor_copy(out=cvo.t[:, c, 0:nb * 2].rearrange("p (b t) -> p b t", t=2), in_=up3[:, :, seg:seg + 2]),
                          reads=[upad.b], writes=[cvo.b])
            if conv_out is not None:
                fw.dma(SP, lambda h: h.dma_start(out=conv_out.rearrange("(c p) t -> p c t", p=128), in_=cvo.t[:, :, 0:nb * 2]), reads=[cvo.b], sembuf=cvo.b, is_out=True)

            def ev(oc, bank):
                fw.op(DVE, lambda h: h.scalar_tensor_tensor(out=xf.t[:, oc, 0:n], in0=xf.t[:, oc, 0:n], scalar=float(ALPHA), in1=bank.t[:, 0:n],
                                                            op0=ALU.mult, op1=ALU.add), reads=[xf.b, bank.b], writes=[xf.b])
            linear_fm(lambda kc: gbuf.t[:, kc, 0:n], [gbuf.b], KC, s_co, wb["co"], 0, KC, n, ev)
            layernorm(xf, n, 3, 4, x1b)
            ffn(0, x1b, xf, n)
            layernorm(xf, n, 5, 6, xb)

        def layer1_tail(attT, n, out_dram):
            def ev(oc, bank):
                fw.op(DVE, lambda h: h.scalar_tensor_tensor(out=xf.t[:, oc, 0:n], in0=xf.t[:, oc, 0:n], scalar=float(ALPHA), in1=bank.t[:, 0:n],
                                                            op0=ALU.mult, op1=ALU.add), reads=[xf.b, bank.b], writes=[xf.b])
            linear_fm(lambda kc: attT.t[:, kc, 0:n], [attT.b], KC, s_ao, wb["ao"], 0, KC, n, ev)
            layernorm(xf, n, 7, 8, x1b)
            ffn(1, x1b, xf, n)
            layernorm(xf, n, 9, 10, xb)
            fw.dma(SP, lambda h: h.dma_start(out=out_dram, in_=xf.t[:, :, 0:n]), reads=[xf.b], sembuf=xf.b, is_out=True)

        def emit_all():
            def phaseA_tile(ti):
                own = ti < NG
                conv_o = convp if ti == NG - 1 else None
                layer0(512, 1, 512, xT[:, ti * 512:(ti + 1) * 512].rearrange("(c p) t -> p c t", p=128),
                       xh[:, ti * 2:(ti + 1) * 2].rearrange("(c p) t -> p c t", p=128), None, conv_o)
                if own:
                    fw.dma(SP, lambda h, ti=ti: h.dma_start(out=s_x2[:, ti * 512:(ti + 1) * 512].rearrange("(c p) t -> p c t", p=128), in_=xf.t[:]),
                           reads=[xf.b], writes=[Buf("x2s%d" % ti)], sembuf=xf.b)
                qk_proj(xb, 128, 4, 1024, lambda tb, ti=ti: rP.t[:, ti * 4 + tb, 0:8], lambda tb, ti=ti: rP.t[:, ti * 4 + tb, 8:16], kTt,
                        (lambda tb, ti=ti: kp[ti * 512 + tb * 128: ti * 512 + (tb + 1) * 128, :]) if own else None)
                bkT = Buf("skT")
                fw.dma(SP, lambda h, ti=ti: h.dma_start(out=s_kT[:, ti * 512:(ti + 1) * 512].rearrange("(c p) t -> p c t", p=128), in_=kTt.t[:]),
                       reads=[kTt.b], writes=[bkT], sembuf=kTt.b)

                def vdst_p(tb, tm, ti=ti):
                    vb_ = vbt[tb % 2]
                    fw.op(ACT, lambda h: h.activation(out=vb_.t[:, :, 0:128], in_=tm.t[:, :].rearrange("p (g e) -> p g e", e=128), func=AF.Copy), reads=[tm.b], writes=[vb_.b])
                    r0 = ti * 512 + tb * 128
                    fw.dma(SP, lambda h: h.dma_start(out=s_v[r0:r0 + 128, :], in_=vb_.t[:].rearrange("p g e -> p (g e)")), reads=[vb_.b], writes=[Buf("sv")], sembuf=vb_.b)
                v_proj(xb, 128, 4, (lambda tb, ti=ti: vp[ti * 512 + tb * 128: ti * 512 + (tb + 1) * 128, :]) if own else None, vdst_p)
            phaseA_tile(0)
            layer0(32, 4, 8, xsT.rearrange("(c p) t -> p c t", p=128), None, sconvT, convs)
            fw.op(DVE, lambda h: h.tensor_copy(out=xs2.t[:], in_=xf.t[:, :, 0:32]), reads=[xf.b], writes=[xs2.b])
            qk_proj(xb, 32, 1, 0, lambda tb: rS.t[:, 0:8], lambda tb: rS.t[:, 8:16], qTs, None)
            qk_proj(xb, 32, 1, 1024, lambda tb: rS.t[:, 0:8], lambda tb: rS.t[:, 8:16], kTs, lambda tb: ks)

            def vdst_s(tb, tm):
                fw.op(POOL, lambda h: h.tensor_copy(out=vsb.t[:], in_=tm.t[0:32, :]), reads=[tm.b], writes=[vsb.b])
            v_proj(xb, 32, 1, lambda tb: vs, vdst_s)

            osum = ob[2]

            def sample_gen():
                for b in range(4):
                    fw.op(POOL, lambda h: h.memset(qblk.t[:], 0.0), writes=[qblk.b])
                    for c in range(KC):
                        for hh in range(2):
                            col = (2 * c + hh) * 8
                            fw.op(ACT, lambda h, c=c, hh=hh, col=col, b=b: h.activation(out=qblk.t[hh * 64:(hh + 1) * 64, c, col:col + 8],
                                  in_=qTs.t[hh * 64:(hh + 1) * 64, c, b * 8:(b + 1) * 8], func=AF.Copy), reads=[qTs.b], writes=[qblk.b])
                    yield
                    for t in range(-3, NPG + 2):
                        p = t
                        if 0 <= p <= NPG:
                            last = (p == NPG)
                            if not last:
                                kT_ = kTp[p % 2]
                                nk = 128
                                lhs = lambda c, kT_=kT_: kT_.t[:, c * 128:(c + 1) * 128]
                                lb = [kT_.b]
                            else:
                                nk = 32
                                lhs = lambda c: kTs.t[:, c, 0:32]
                                lb = [kTs.b]
                            for c in range(KC):
                                fw.op(PE, lambda h, c=c, lhs=lhs, nk=nk: h.matmul(sbank.t[0:nk, 0:128], lhsT=lhs(c), rhs=qblk.t[:, c, :],
                                      start=(c == 0), stop=(c == KC - 1)), reads=lb + [qblk.b], writes=[sbank.b], signal=(c == KC - 1))
                            pT_ = pTs[p % 2]
                            fw.op(ACT, lambda h, pT_=pT_, nk=nk: h.activation(out=pT_.t[0:nk, :], in_=sbank.t[0:nk, 0:128], func=AF.Exp, scale=float(SCALE)),
                                  reads=[sbank.b], writes=[pT_.b])
                            if last:
                                fw.op(DVE, lambda h, pT_=pT_, b=b: h.tensor_tensor(out=pT_.t[0:32, :], in0=pT_.t[0:32, :], in1=masknew[:, b * 128:(b + 1) * 128], op=ALU.mult),
                                      reads=[pT_.b, c_f.b], writes=[pT_.b])
                        p = t - 1
                        if 0 <= p <= NPG:
                            last = (p == NPG)
                            pT_ = pTs[p % 2]
                            nk = 32 if last else 128
                            vrhs = vsb if last else vpg[p % 3]
                            for half in range(2):
                                fw.op(PE, lambda h, half=half, pT_=pT_, vrhs=vrhs, nk=nk, p=p, last=last: h.matmul(ob[half].t[:, :], lhsT=pT_.t[0:nk, :],
                                      rhs=vrhs.t[0:nk, half * 512:(half + 1) * 512], start=(p == 0), stop=last), reads=[pT_.b, vrhs.b], writes=[ob[half].b], signal=False)
                            fw.op(PE, lambda h, pT_=pT_, nk=nk, p=p, last=last: h.matmul(osum.t[:, 0:1], lhsT=pT_.t[0:nk, :], rhs=onesB[0:nk, 0:1],
                                  start=(p == 0), stop=last), reads=[pT_.b, c_b.b], writes=[osum.b], signal=True)
                        p = t + 1
                        if 0 <= p < NPG:
                            kpg_ = kpg[p % 3]
                            kT_ = kTp[p % 2]
                            for c in range(KC):
                                fw.op(PE, lambda h, c=c, kpg_=kpg_: h.transpose(trb.t[:, c * 128:(c + 1) * 128], kpg_.t[:, c * 128:(c + 1) * 128], identB),
                                      reads=[kpg_.b, c_b.b], writes=[trb.b])
                            fw.op(DVE, lambda h, kT_=kT_: h.tensor_copy(out=kT_.t[:], in_=trb.t[:]), reads=[trb.b], writes=[kT_.b])
                        p = t + 1
                        if 0 <= p < NPG:
                            vpg_ = vpg[p % 3]
                            col = b * NPG + p
                            fw.dma(POOL, lambda h, vpg_=vpg_, col=col: h.indirect_dma_start(out=vpg_.t[:], out_offset=None, in_=cache_v,
                                   in_offset=bass.IndirectOffsetOnAxis(ap=pt_x.t[:, col:col + 1], axis=0)), reads=[pt_x.b], writes=[vpg_.b], sembuf=vpg_.b)
                        p = t + 3
                        if 0 <= p < NPG:
                            kpg_ = kpg[p % 3]
                            col = b * NPG + p
                            fw.dma(POOL, lambda h, kpg_=kpg_, col=col: h.indirect_dma_start(out=kpg_.t[:], out_offset=None, in_=cache_k,
                                   in_offset=bass.IndirectOffsetOnAxis(ap=pt_x.t[:, col:col + 1], axis=0)), reads=[pt_x.b], writes=[kpg_.b], sembuf=kpg_.b)
                        yield
                    for half in range(2):
                        fw.op(DVE, lambda h, half=half: h.tensor_tensor(out=od.t[:, half * 512:(half + 1) * 512].rearrange("p (g e) -> p g e", e=128),
                              in0=ob[half].t[:, :].rearrange("p (g e) -> p g e", e=128),
                              in1=mask3[:, half * 4:(half + 1) * 4].unsqueeze(2).to_broadcast([128, 4, 128]), op=ALU.mult),
                              reads=[ob[half].b, c_f.b], writes=[od.b])
                    fw.op(DVE, lambda h: h.tensor_reduce(out=orr.t[:], in_=od.t[:].rearrange("p (g e) -> p e g", e=128), op=ALU.add, axis=AX.X), reads=[od.b], writes=[orr.b])
                    fw.op(DVE, lambda h: h.reciprocal(out=sm.t[:, 0:1], in_=osum.t[:, 0:1]), reads=[osum.b], writes=[sm.b])
                    fw.op(DVE, lambda h: h.tensor_scalar(out=orr.t[:], in0=orr.t[:], scalar1=sm.t[:, 0:1], scalar2=None, op0=ALU.mult), reads=[orr.b, sm.b], writes=[orr.b])
                    yield
                    fw.op(PE, lambda h: h.matmul(sbank.t[0:64, 0:128], lhsT=Cmat.t[:], rhs=orr.t[:], start=True, stop=True), reads=[Cmat.b, orr.b], writes=[sbank.b])
                    fw.op(ACT, lambda h: h.activation(out=o_t.t[0:64, :], in_=sbank.t[0:64, 0:128], func=AF.Square, accum_out=sm.t[0:64, 1:2]),
                          reads=[sbank.b], writes=[o_t.b, sm.b])
                    fw.op(ACT, lambda h: h.activation(out=sm.t[0:64, 1:2], in_=sm.t[0:64, 1:2], func=AF.Sqrt, bias=eps_sub[0:64], scale=1.0 / 128), reads=[sm.b, c_f.b], writes=[sm.b])
                    fw.op(DVE, lambda h: h.reciprocal(out=sm.t[0:64, 1:2], in_=sm.t[0:64, 1:2]), reads=[sm.b], writes=[sm.b])
                    on_ = o_n[b % 2]
                    fw.op(DVE, lambda h, on_=on_: h.scalar_tensor_tensor(out=on_.t[0:64, :], in0=sbank.t[0:64, 0:128], scalar=sm.t[0:64, 1:2], in1=g_rep.t[0:64, :],
                          op0=ALU.mult, op1=ALU.mult), reads=[sbank.b, sm.b, g_rep.b], writes=[on_.b])
                    yield
                    fw.op(PE, lambda h, on_=on_: h.transpose(trb.t[:, 0:64], on_.t[0:64, :], identB[0:64, 0:64]), reads=[on_.b, c_b.b], writes=[trb.b])
                    fw.op(ACT, lambda h, b=b: h.activation(out=attS.t[:, :, b * 8:(b + 1) * 8], in_=trb.t[:, 0:64].rearrange("p (g q) -> p g q", q=8), func=AF.Copy),
                          reads=[trb.b], writes=[attS.b])
                    yield

            tk["gen"] = sample_gen()
            tick(4)

            for ti in range(1, NT):
                phaseA_tile(ti)
            while tk["gen"] is not None:
                tick()
            fw.op(DVE, lambda h: h.tensor_copy(out=xf.t[:, :, 0:32], in_=xs2.t[:]), reads=[xs2.b], writes=[xf.b])
            layer1_tail(attS, 32, ysT.rearrange("(c p) t -> p c t", p=128))
            if not fw.dry:
                for tb_ in [kTt] + vbt + [xf]:
                    s = tb_.b.dsem
                    if s is not None and SP.seen.get(s, 0) < s.cnt:
                        SP.h.wait_ge(s.h, s.cnt)
                        SP.seen[s] = s.cnt
                merge_bufs(kt[0].b, [x.b for x in (kpg[0], kpg[1], kpg[2], kTp[0])])
                merge_bufs(kt[1].b, [x.b for x in (vpg[0], vpg[1], vpg[2], kTp[1])])
                merge_bufs(osb1.b, [qblk.b, vsb.b])

            QT = kTt
            attT = x1b
            for I in range(NG):
                fw.dma(SP, lambda h, I=I: h.dma_start(out=xf.t[:], in_=s_x2[:, I * 512:(I + 1) * 512].rearrange("(c p) t -> p c t", p=128)), writes=[xf.b], sembuf=xf.b)
                for c in range(KC):
                    fw.op(ACT if c % 2 else DVE, (lambda h, c=c: h.activation(out=xb.t[:, c, :], in_=xf.t[:, c, :], func=AF.Copy)) if c % 2 else
                          (lambda h, c=c: h.tensor_copy(out=xb.t[:, c, :], in_=xf.t[:, c, :])), reads=[xf.b], writes=[xb.b])
                qk_proj(xb, 128, 4, 0, lambda tb, I=I: rP.t[:, I * 4 + tb, 0:8], lambda tb, I=I: rP.t[:, I * 4 + tb, 8:16], QT, None)
                nkb = 4 * (I + 1)
                def load_kv(hd, I=I, nkb=nkb):
                    kt_, vt_ = kt[hd % 2], vt[hd % 2]
                    for side in range(2):
                        t0 = side * NG * 512
                        fw.dma(SP, lambda h, side=side, t0=t0, kt_=kt_, hd=hd: h.dma_start(out=kt_.t[:, side, 0:nkb * 128], in_=s_kT[hd * 128:(hd + 1) * 128, t0:t0 + nkb * 128]),
                               writes=[kt_.b], sembuf=kt_.b)
                        fw.dma(SP, lambda h, side=side, t0=t0, vt_=vt_, hd=hd: h.dma_start(out=vt_.t[:, side * NKB:side * NKB + nkb, :],
                               in_=s_v[t0:t0 + nkb * 128, hd * 129:(hd + 1) * 129].rearrange("(k p) e -> p k e", p=128)), writes=[vt_.b], sembuf=vt_.b)

                def oacc(s, qs):
                    if qs < 3:
                        return ob[s], ob[s].t[:, qs * 129:(qs + 1) * 129]
                    return ob[2], ob[2].t[:, s * 129:(s + 1) * 129]

                def osb_ap(osb, s, qs):
                    if qs < 3:
                        return osb.t[:, s, qs * 129:(qs + 1) * 129]
                    return osb.t[:, 2, s * 129:(s + 1) * 129]

                def blocks_of(hd, I=I):
                    kt_, vt_ = kt[hd % 2], vt[hd % 2]
                    blocks = []
                    for side in range(2):
                        for kb in range(4 * I):
                            blocks.append((side, kb, "full", 0))
                    for kb in range(4):
                        blocks.append((1, 4 * I + kb, "pdiag", 0))
                    for kb in range(4):
                        blocks.append((0, 4 * I + kb, "odiag", kb))
                    for ob_ in ob:
                        fw.op(PE, lambda h, ob_=ob_: h.matmul(ob_.t[:, 0:512], lhsT=zer.t[:, 0:128], rhs=zer.t[:, 0:512], start=True, stop=False),
                              reads=[zer.b], writes=[ob_.b])
                    qz_ = qz[hd % 2]
                    fw.op(POOL, lambda h, qz_=qz_, hd=hd: h.tensor_copy(out=qz_.t[0:64, 0, :], in_=QT.t[0:64, hd, :]), reads=[QT.b], writes=[qz_.b])
                    fw.op(POOL, lambda h, qz_=qz_, hd=hd: h.tensor_copy(out=qz_.t[64:128, 1, :], in_=QT.t[64:128, hd, :]), reads=[QT.b], writes=[qz_.b])
                    steps = [(side, kb, kind, j0, s) for (side, kb, kind, j0) in blocks for s in range(2)]
                    pts = {}
                    SKEW = 2

                    def qk_exp(i):
                        side, kb, kind, j0, s = steps[i]
                        q0 = j0 * 128
                        bank = next_mm()
                        fw.op(PE, lambda h: h.matmul(
                            bank.t[:, q0:512], lhsT=kt_.t[:, side, kb * 128:(kb + 1) * 128], rhs=qz_.t[:, s, q0:512],
                            start=True, stop=True), reads=[kt_.b, qz_.b], writes=[bank.b])
                        pT_ = pT[state["k"] % 4]
                        state["k"] += 1
                        pts[i] = pT_
                        if kind == "pdiag":
                            fw.op(ACT, lambda h: h.activation(out=pT_.t[:, :], in_=bank.t[:, :], func=AF.Exp, scale=float(SCALE), bias=flag),
                                  reads=[bank.b, c_f.b], writes=[pT_.b])
                        else:
                            fw.op(ACT, lambda h: h.activation(out=pT_.t[:, q0:512], in_=bank.t[:, q0:512], func=AF.Exp, scale=float(SCALE)),
                                  reads=[bank.b], writes=[pT_.b])
                        if kind == "odiag":
                            fw.op(POOL, lambda h: h.tensor_tensor(out=pT_.t[:, q0:q0 + 128], in0=pT_.t[:, q0:q0 + 128], in1=tri, op=ALU.mult),
                                  reads=[pT_.b, c_b.b], writes=[pT_.b])

                    def pv(i):
                        side, kb, kind, j0, s = steps[i]
                        pT_ = pts.pop(i)
                        for qs in range(j0, 4):
                            ab, aap = oacc(s, qs)
                            stop_ = (kind == "odiag" and j0 == qs)
                            fw.op(PE, lambda h, aap=aap, qs=qs, stop_=stop_: h.matmul(
                                aap, lhsT=pT_.t[:, qs * 128:(qs + 1) * 128], rhs=vt_.t[:, side * NKB + kb, :], start=False, stop=stop_),
                                reads=[pT_.b, vt_.b], writes=[ab.b], signal=(qs == 3))

                    for i in range(len(steps) + SKEW):
                        if i < len(steps):
                            qk_exp(i)
                        if i - SKEW >= 0:
                            pv(i - SKEW)

                def evac_o(hd):
                    osb = osbs[hd % 2]
                    fw.op(ACT, lambda h: h.activation(out=osb.t[:, 0, :], in_=ob[0].t[:, 0:387], func=AF.Copy), reads=[ob[0].b], writes=[osb.b])
                    fw.op(ACT, lambda h: h.activation(out=osb.t[:, 1, :], in_=ob[1].t[:, 0:387], func=AF.Copy), reads=[ob[1].b], writes=[osb.b])
                    fw.op(ACT, lambda h: h.activation(out=osb.t[:, 2, 0:258], in_=ob[2].t[:, 0:258], func=AF.Copy), reads=[ob[2].b], writes=[osb.b])

                def finalize(hd):
                    osb = osbs[hd % 2]
                    for qs in range(4):
                        a0 = osb_ap(osb, 0, qs)
                        a1 = osb_ap(osb, 1, qs)
                        fw.op(DVE, lambda h, a0=a0: h.reciprocal(out=sm.t[:, 2:3], in_=a0[:, 128:129]), reads=[osb.b], writes=[sm.b])
                        fw.op(DVE, lambda h, a1=a1: h.reciprocal(out=sm.t[:, 3:4], in_=a1[:, 128:129]), reads=[osb.b], writes=[sm.b])
                        fw.op(DVE, lambda h: h.tensor_tensor(out=sm.t[:, 3:4], in0=sm.t[:, 3:4], in1=lamb.t[:, 1:2], op=ALU.mult), reads=[sm.b, lamb.b], writes=[sm.b])
                        fw.op(DVE, lambda h, a0=a0: h.tensor_scalar(out=o_t.t[:], in0=a0[:, 0:128], scalar1=sm.t[:, 2:3], scalar2=None, op0=ALU.mult),
                              reads=[osb.b, sm.b], writes=[o_t.b])
                        fw.op(DVE, lambda h, a1=a1, qs=qs: h.scalar_tensor_tensor(out=o_o4.t[:, qs, :], in0=a1[:, 0:128], scalar=sm.t[:, 3:4], in1=o_t.t[:], op0=ALU.mult, op1=ALU.add),
                              reads=[osb.b, sm.b, o_t.b], writes=[o_o4.b])
                        fw.op(DVE, lambda h, qs=qs: h.tensor_tensor(out=o_t.t[:], in0=o_o4.t[:, qs, :], in1=o_o4.t[:, qs, :], op=ALU.mult), reads=[o_o4.b], writes=[o_t.b])
                        fw.op(DVE, lambda h, qs=qs: h.reduce_sum(out=sm.t[:, 8 + qs:9 + qs], in_=o_t.t[:], axis=AX.X), reads=[o_t.b], writes=[sm.b])
                    fw.op(ACT, lambda h: h.activation(out=sm.t[:, 12:16], in_=sm.t[:, 8:12], func=AF.Sqrt, bias=eps_sub, scale=1.0 / 128), reads=[sm.b, c_f.b], writes=[sm.b])
                    fw.op(DVE, lambda h: h.reciprocal(out=sm.t[:, 12:16], in_=sm.t[:, 12:16]), reads=[sm.b], writes=[sm.b])
                    for qs in range(4):
                        fw.op(DVE, lambda h, qs=qs: h.scalar_tensor_tensor(out=on4.t[:, qs, :], in0=o_o4.t[:, qs, :], scalar=sm.t[:, 12 + qs:13 + qs], in1=g_rep.t[:], op0=ALU.mult, op1=ALU.mult),
                              reads=[o_o4.b, sm.b, g_rep.b], writes=[on4.b])
                    for qs in range(4):
                        fw.op(PE, lambda h, qs=qs: h.transpose(trb.t[:, qs * 128:(qs + 1) * 128], on4.t[:, qs, :], identB), reads=[on4.b, c_b.b], writes=[trb.b])
                    fw.op(DVE, lambda h, hd=hd: h.tensor_copy(out=attT.t[:, hd, :], in_=trb.t[:, 0:512]), reads=[trb.b], writes=[attT.b])

                load_kv(0)
                for hd in range(8):
                    if hd + 1 < 8:
                        load_kv(hd + 1)
                    blocks_of(hd)
                    evac_o(hd)
                    if hd > 0:
                        finalize(hd - 1)
                finalize(7)
                layer1_tail(attT, 512, yT[:, I * 512:(I + 1) * 512].rearrange("(c p) t -> p c t", p=128))

        fw.dry = True
        emit_all()
        fw.dry = False
        state["mm"] = 0
        state["ws"] = 0
        state["k"] = 0
        emit_all()
        assert pst["i"] == len(plan)
        fw.finish()
        print("bass program: n_inst", fw.n_inst, "nsem", fw.nsem)
    return nc


def _consts(parity):
    cf = np.zeros((128, 908), np.float32)
    cf[:, 780:908] = 1.0
    cf[:, 0:128] = np.eye(128, dtype=np.float32)
    for r in range(128):
        hp, q = r // 8, r % 8
        c = (hp // 2) * 8 + q
        if hp % 2 == 0:
            cf[r, 128 + c] = 1.0
        else:
            cf[r, 192 + c] = 1.0
        cf[r, 256 + hp // 2] = 1.0
    cf[:, 264] = np.arange(128, dtype=np.float32)
    cf[:, 265] = NEG if parity == 0 else 0.0
    cf[:, 266] = LN_EPS
    cf[:, 267] = SUB_EPS
    for b in range(4):
        for r in range(32):
            bb, t = r // 8, r % 8
            for c in range(128):
                q = c % 8
                cf[r, 268 + b * 128 + c] = 1.0 if (bb == b and t <= q) else 0.0
    cb = np.zeros((128, 384), np.float32)
    cb[:, 0:128] = np.eye(128)
    cb[:, 128:256] = 1.0
    k = np.arange(128)[:, None]
    q = np.arange(128)[None, :]
    cb[:, 256:384] = (k <= q).astype(np.float32)
    return cf, cb.astype(ml_dtypes.bfloat16)


def _rope_tab(pos):
    inv = np.power(np.float32(500000.0), -np.arange(0, 16, 2, dtype=np.float32) / np.float32(16)).astype(np.float32)
    ang = pos.astype(np.float32)[:, None] * inv[None, :]
    return np.concatenate([np.cos(ang), np.sin(ang)], axis=-1).astype(np.float32)


_CACHE = {}


def run(inputs, NG, NPG):
    f = lambda k: np.ascontiguousarray(np.asarray(inputs[k]))
    x_prompt, x_sample, state_conv = f("x_prompt"), f("x_sample"), f("state_conv")
    cache_k, cache_v, page_table = f("cache_k"), f("cache_v"), f("page_table")
    B, S, _ = x_prompt.shape
    NPOOL = cache_k.shape[0]
    assert S == 2 * NG * 512 and page_table.shape[1] == NPG and B == 4
    key = (NG, NPG, NPOOL)
    if key not in _CACHE:
        _CACHE[key] = build(NG, NPG, NPOOL)
    nc = _CACHE[key]
    ck = cache_k.reshape(NPOOL * 128, D)
    cv = cache_v.reshape(NPOOL * 128, D)
    vecs = np.concatenate([f("w_conv").T,
                           np.stack([f("ln_mix_g")[0], f("ln_mix_b")[0], f("ln_ffn_g")[0], f("ln_ffn_b")[0],
                                     f("ln_mix_g")[1], f("ln_mix_b")[1], f("ln_ffn_g")[1], f("ln_ffn_b")[1]], axis=1)], axis=1).astype(np.float32)
    lamv = np.concatenate([f("lambda_q1"), f("lambda_k1"), f("lambda_q2"), f("lambda_k2")]).reshape(1, 256).astype(np.float32)
    grep = np.broadcast_to(f("subln_g")[None, :], (128, 128)).astype(np.float32).copy()
    past_len = NPG * 128
    ropeS = _rope_tab(past_len + (np.arange(32) % 8))
    in_maps = []
    orders = []
    for core in range(8):
        b, par = core // 2, core % 2
        own = [2 * I + par for I in range(NG)]
        oth = [2 * I + (1 - par) for I in range(NG)]
        gran = own + oth
        orders.append(gran)
        tok = np.concatenate([np.arange(g * 512, (g + 1) * 512) for g in gran])
        xT = np.ascontiguousarray(x_prompt[b][tok].T)
        xh = np.zeros((D, 2 * len(gran)), np.float32)
        for i, g in enumerate(gran):
            if g > 0:
                xh[:, 2 * i:2 * i + 2] = x_prompt[b][g * 512 - 2:g * 512].T
        rp = _rope_tab(tok).reshape(len(gran) * 4, 128, 16).transpose(1, 0, 2)
        cf, cb = _consts(par)
        sb = slice(core * 4, core * 4 + 4)
        in_maps.append({
            "xT": xT, "xh": xh,
            "xsT": np.ascontiguousarray(x_sample[sb].reshape(32, D).T),
            "sconvT": np.ascontiguousarray(state_conv[sb].reshape(8, D).T),
            "cache_k": ck, "cache_v": cv,
            "ptab": np.ascontiguousarray(page_table[sb].reshape(1, 4 * NPG).astype(np.int32)),
            "w_ci": f("w_conv_in"), "w_co": f("w_conv_out"), "w_qkv": f("w_qkv"), "w_ao": f("w_attn_out"),
            "w_fi": f("w_ffn_in"), "w_fo": f("w_ffn_out"), "vecs": vecs, "lamv": lamv, "grep": grep,
            "ropeP": np.ascontiguousarray(rp), "ropeS": ropeS, "cst_f": cf, "cst_b": cb,
        })
    res = run_bass_kernel_spmd(nc, in_maps, core_ids=list(range(8))).results
    y_p = np.zeros((B, S, D), np.float32)
    k_p = np.zeros((B, S, 16, 64), np.float32)
    v_p = np.zeros((B, S, 8, 128), np.float32)
    conv_p = np.zeros((B, 2, D), np.float32)
    y_s = np.zeros((32, 8, D), np.float32)
    conv_s = np.zeros((32, 2, D), np.float32)
    k_s = np.zeros((32, 8, 16, 64), np.float32)
    v_s = np.zeros((32, 8, 8, 128), np.float32)
    for core in range(8):
        b, par = core // 2, core % 2
        r = res[core]
        for I in range(NG):
            g = 2 * I + par
            y_p[b, g * 512:(g + 1) * 512] = r["yT"][:, I * 512:(I + 1) * 512].T
            k_p[b, g * 512:(g + 1) * 512] = r["kp"][I * 512:(I + 1) * 512].reshape(512, 16, 64)
            v_p[b, g * 512:(g + 1) * 512] = r["vp"][I * 512:(I + 1) * 512].reshape(512, 8, 128)
        if par == 1:
            conv_p[b] = r["convp"].T
        sb = slice(core * 4, core * 4 + 4)
        y_s[sb] = r["ysT"].T.reshape(4, 8, D)
        conv_s[sb] = r["convs"].T.reshape(4, 2, D)
        k_s[sb] = r["ks"].reshape(4, 8, 16, 64)
        v_s[sb] = r["vs"].reshape(4, 8, 8, 128)
    return (y_p, y_s, conv_p, k_p, v_p, conv_s, k_s, v_s)


def kernel(**inputs):
    return run(inputs, 4, 64)
```

```python
import math
import numpy as np
import ml_dtypes
from contextlib import ExitStack
import concourse.bass as bass
import concourse.mybir as mybir
from concourse.bass_utils import run_bass_kernel_spmd

F32 = mybir.dt.float32
BF16 = mybir.dt.bfloat16
I32 = mybir.dt.int32
AF = mybir.ActivationFunctionType
ALU = mybir.AluOpType
AX = mybir.AxisListType

D = 1024
KC = 8
DFF = 2816
FC = 22
ALPHA = (2 * 2) ** 0.25
SCALE = 64 ** -0.5
LN_EPS = 1e-5
SUB_EPS = 1e-5
LAM_INIT = 0.8 - 0.6 * math.exp(-0.3 * 1)
NEG = -30000.0


class Buf:
    __slots__ = ("name", "w", "rs", "dsem")

    def __init__(self, name):
        self.name = name
        self.w = None
        self.rs = {}
        self.dsem = None


class Sem:
    __slots__ = ("h", "cnt")

    def __init__(self, h):
        self.h = h
        self.cnt = 0


class Eng:
    def __init__(self, name, h, sem):
        self.name = name
        self.h = h
        self.sem = sem
        self.seen = {}
        self.pending = False


class TB:
    def __init__(self, t, name):
        self.t = t
        self.b = Buf(name)


class FW:
    def __init__(self, nc, stack):
        self.nc = nc
        self.stack = stack
        self.nsem = 0
        self.pe = self._eng("pe", nc.tensor)
        self.act = self._eng("act", nc.scalar)
        self.dve = self._eng("dve", nc.vector)
        self.pool = self._eng("pool", nc.gpsimd)
        self.sp = self._eng("sp", nc.sync)
        self.engs = [self.pe, self.act, self.dve, self.pool, self.sp]
        self.out_events = {}
        self.n_inst = 0
        self.dry = False

    def new_sem(self, name):
        self.nsem += 1
        return Sem(self.stack.enter_context(self.nc.semaphore(name)))

    def _eng(self, name, h):
        return Eng(name, h, self.new_sem("s_" + name))

    def _deps(self, E, reads, writes):
        need = {}
        for b in reads:
            if b.w is not None:
                s, v = b.w
                if need.get(s, 0) < v:
                    need[s] = v
        for b in writes:
            if b.w is not None:
                s, v = b.w
                if need.get(s, 0) < v:
                    need[s] = v
            for s, v in b.rs.items():
                if need.get(s, 0) < v:
                    need[s] = v
        for s, v in need.items():
            if s is E.sem and (v > s.cnt or E is self.pe):
                continue
            if E.seen.get(s, 0) >= v:
                continue
            E.h.wait_ge(s.h, v)
            E.seen[s] = v

    def _record(self, ev, reads, writes):
        s, v = ev
        for b in writes:
            b.w = ev
            b.rs = {}
        for b in reads:
            if b.rs.get(s, 0) < v:
                b.rs[s] = v

    def op(self, E, fn, reads=(), writes=(), signal=True):
        if self.dry:
            return None
        self._deps(E, reads, writes)
        ins = fn(E.h)
        self.n_inst += 1
        if signal:
            E.sem.cnt += 1
            ins.then_inc(E.sem.h, 1)
            ev = (E.sem, E.sem.cnt)
            E.pending = False
        else:
            ev = (E.sem, E.sem.cnt + 1)
            E.pending = True
        self._record(ev, reads, writes)
        return ins

    def dma(self, Q, fn, reads=(), writes=(), sembuf=None, is_out=False):
        if self.dry:
            return None
        self._deps(Q, reads, writes)
        if sembuf.dsem is None:
            sembuf.dsem = self.new_sem("d_" + sembuf.name)
        s = sembuf.dsem
        ins = fn(Q.h)
        self.n_inst += 1
        s.cnt += 16
        ins.then_inc(s.h, 16)
        ev = (s, s.cnt)
        self._record(ev, reads, writes)
        if is_out:
            self.out_events[s] = s.cnt
        return ins

    def finish(self):
        assert not self.pe.pending
        E = self.sp
        for s, v in self.out_events.items():
            if E.seen.get(s, 0) < v:
                E.h.wait_ge(s.h, v)
                E.seen[s] = v
        for X in self.engs:
            if X is E or X.sem.cnt == 0:
                continue
            if E.seen.get(X.sem, 0) < X.sem.cnt:
                E.h.wait_ge(X.sem.h, X.sem.cnt)


def view(ap, name):
    v = TB.__new__(TB)
    v.t = ap
    v.b = Buf(name)
    return v


def merge_bufs(dst, srcs):
    for s_ in srcs:
        evs = dict(s_.rs)
        if s_.w is not None:
            s, v = s_.w
            if evs.get(s, 0) < v:
                evs[s] = v
        for s, v in evs.items():
            if dst.rs.get(s, 0) < v:
                dst.rs[s] = v


def build(NG, NPG, NPOOL):
    NT = 2 * NG
    NTOK = NT * 512
    NB = NT * 4
    NPT = 4 * NPG
    nc = bass.Bass("TRN2", target_bir_lowering=False)
    dt = lambda n, s, d, k: nc.dram_tensor(n, s, d, kind=k).ap()
    IN, OUT, SCR = "ExternalInput", "ExternalOutput", "Internal"
    xT = dt("xT", [D, NTOK], F32, IN)
    xh = dt("xh", [D, NT * 2], F32, IN)
    xsT = dt("xsT", [D, 32], F32, IN)
    sconvT = dt("sconvT", [D, 8], F32, IN)
    cache_k = dt("cache_k", [NPOOL * 128, D], F32, IN)
    cache_v = dt("cache_v", [NPOOL * 128, D], F32, IN)
    ptab = dt("ptab", [1, NPT], I32, IN)
    w_ci = dt("w_ci", [D, 3 * D], F32, IN)
    w_co = dt("w_co", [D, D], F32, IN)
    w_qkv = dt("w_qkv", [D, 3 * D], F32, IN)
    w_ao = dt("w_ao", [D, D], F32, IN)
    w_fi = dt("w_fi", [2, D, 2 * DFF], F32, IN)
    w_fo = dt("w_fo", [2, DFF, D], F32, IN)
    vecs = dt("vecs", [D, 11], F32, IN)
    lamv = dt("lamv", [1, 256], F32, IN)
    grep = dt("grep", [128, 128], F32, IN)
    ropeP = dt("ropeP", [128, NB, 16], F32, IN)
    ropeS = dt("ropeS", [32, 16], F32, IN)
    cst_f = dt("cst_f", [128, 908], F32, IN)
    cst_b = dt("cst_b", [128, 384], BF16, IN)
    yT = dt("yT", [D, NG * 512], F32, OUT)
    ysT = dt("ysT", [D, 32], F32, OUT)
    kp = dt("kp", [NG * 512, D], F32, OUT)
    vp = dt("vp", [NG * 512, D], F32, OUT)
    convp = dt("convp", [D, 2], F32, OUT)
    convs = dt("convs", [D, 8], F32, OUT)
    ks = dt("ks", [32, D], F32, OUT)
    vs = dt("vs", [32, D], F32, OUT)
    s_ci = dt("s_ci", [D, 3 * D], BF16, SCR)
    s_co = dt("s_co", [D, D], BF16, SCR)
    s_qkv = dt("s_qkv", [D, 3 * D], BF16, SCR)
    s_ao = dt("s_ao", [D, D], BF16, SCR)
    s_fi = dt("s_fi", [2, D, 2 * DFF], BF16, SCR)
    s_fo = dt("s_fo", [2, DFF, D], BF16, SCR)
    s_kT = dt("s_kT", [D, NTOK], BF16, SCR)
    s_v = dt("s_v", [NTOK, 8 * 129], BF16, SCR)
    s_x2 = dt("s_x2", [D, NG * 512], F32, SCR)

    with ExitStack() as st:
        fw = FW(nc, st)
        PE, ACT, DVE, POOL, SP = fw.pe, fw.act, fw.dve, fw.pool, fw.sp

        def T(name, shape, dtype):
            return TB(st.enter_context(nc.sbuf_tensor(name, shape, dtype)), name)

        def PS(name, shape, dtype):
            return TB(st.enter_context(nc.psum_tensor(name, shape, dtype)), name)

        c_f = T("c_f", [128, 908], F32)
        c_b = T("c_b", [128, 384], BF16)
        vec = T("vec", [128, KC, 11], F32)
        g_rep = T("g_rep", [128, 128], F32)
        rP = T("rP", [128, NB, 16], F32)
        rS = T("rS", [32, 16], F32)
        lamb = T("lamb", [128, 2], F32)
        Cmat = T("Cmat", [128, 64], F32)
        pt_i = T("pt_i", [128, NPT], I32)
        pt_x = T("pt_x", [128, NPT], I32)
        xf = T("xf", [128, KC, 512], F32)
        xf.bs = [Buf("xf%d" % c) for c in range(KC)]
        xb = T("xb", [128, KC, 512], BF16)
        gbuf = T("gbuf", [128, KC, 512], BF16)
        x1b = T("x1b", [128, KC, 512], BF16)
        hb = [T(f"hb{i}", [128, 512], BF16) for i in range(FC)]
        t1 = T("t1", [128, 512], F32)
        t2 = T("t2", [128, 512], F32)
        lam_sb = t1
        ln_m = t2
        ln_q = t1
        qz = [T(f"qz{i}", [128, 2, 512], BF16) for i in range(2)]
        pt_f = view(t2.t[:, 0:NPT], "pt_f")
        pt_f.b = t2.b
        upad = T("upad", [128, 520], F32)
        uhal = T("uhal", [128, 8], F32)
        cvo = T("cvo", [128, KC, 8], F32)
        ln_r = T("ln_r", [128, 512], F32)
        ln_n = T("ln_n", [128, 512], F32)
        ln_b1 = [T(f"ln_b1_{i}", [128, 512], BF16) for i in range(2)]
        ln_b2 = [T(f"ln_b2_{i}", [128, 512], BF16) for i in range(2)]
        tmf = [T(f"tmf{i}", [128, D], F32) for i in range(2)]
        tmb = [T(f"tmb{i}", [128, D], BF16) for i in range(2)]
        rt = [T(f"rt{i}", [128, 16, 8], F32) for i in range(4)]
        kTt = gbuf
        t3 = T("t3", [128, 512], F32)
        vbt = [T(f"vbt{i}", [128, 8, 129], BF16) for i in range(2)]
        wsl = [T(f"wsl{i}", [128, 4096], BF16) for i in range(5)]
        NKB = 4 * NG
        KTW = max(NKB * 128, 2048)
        kt = [T(f"kt{i}", [128, 2, KTW], BF16) for i in range(2)]
        vt = [T(f"vt{i}", [128, 2 * NKB, 129], BF16) for i in range(2)]
        pT = [T(f"pT{i}", [128, 512], BF16) for i in range(4)]
        sm = T("sm", [128, 16], F32)
        o_t = T("o_t", [128, 128], F32)
        o_o = T("o_o", [128, 128], F32)
        o_n = [T(f"o_n{i}", [128, 128], BF16) for i in range(2)]
        osb1 = T("osb1", [128, 3, 387], F32)
        _ob1 = osb1.t[:].rearrange("p a b -> p (a b)")
        qblk = view(_ob1[:, 0:512].bitcast(BF16).rearrange("p (c n) -> p c n", n=128), "qblk")
        kpg = [view(kt[0].t[:, 0, 0:1024], "kpg0"), view(kt[0].t[:, 0, 1024:2048], "kpg1"), view(kt[0].t[:, 1, 0:1024], "kpg2")]
        vpg = [view(kt[1].t[:, 0, 0:1024], "vpg0"), view(kt[1].t[:, 0, 1024:2048], "vpg1"), view(kt[1].t[:, 1, 0:1024], "vpg2")]
        kTp = [view(kt[0].t[:, 1, 1024:2048], "kTp0"), view(kt[1].t[:, 1, 1024:2048], "kTp1")]
        osb0 = T("osb0", [128, 3, 387], F32)
        osbs = [osb0, osb1]
        o_o4 = T("o_o4", [128, 4, 128], F32)
        on4 = T("on4", [128, 4, 128], BF16)
        xs2 = T("xs2", [128, KC, 32], F32)
        pTs = [T(f"pTs{i}", [128, 128], BF16) for i in range(2)]
        kTs = T("kTs", [128, KC, 32], BF16)
        qTs = T("qTs", [128, KC, 32], BF16)
        vsb = view(_ob1[0:32, 512:1024].bitcast(BF16), "vsb")
        od = tmf[0]
        orr = T("orr", [128, 128], F32)
        attS = T("attS", [128, KC, 32], BF16)
        zer = T("zer", [128, 512], BF16)
        mmb = [PS(f"mm{i}", [128, 512], F32) for i in range(3)]
        sbank = PS("sbank", [128, 512], F32)
        trb = PS("trb", [128, 1024], BF16)
        trbF = view(trb.t[:, 0:8].bitcast(F32), "trbF")
        trbF.b = trb.b
        ob = [PS(f"ob{i}", [128, 512], F32) for i in range(3)]

        state = {"mm": 0, "ws": 0, "k": 0}

        tk = {"gen": None}

        def tick(n=1):
            g = tk["gen"]
            if g is None:
                return
            for _ in range(n):
                try:
                    next(g)
                except StopIteration:
                    tk["gen"] = None
                    return

        def next_mm():
            b = mmb[state["mm"] % 3]
            state["mm"] += 1
            return b

        identF = c_f.t[:, 0:128]
        C1 = c_f.t[:, 128:192]
        C2 = c_f.t[:, 192:256]
        mask3 = c_f.t[:, 256:264]
        flag = c_f.t[:, 265:266]
        eps_ln = c_f.t[:, 266:267]
        eps_sub = c_f.t[:, 267:268]
        masknew = c_f.t[0:32, 268:268 + 512]
        identB = c_b.t[:, 0:128]
        onesB = c_b.t[:, 128:256]
        tri = c_b.t[:, 256:384]

        fw.op(POOL, lambda h: h.memset(zer.t[:], 0.0), writes=[zer.b])
        for vb_ in vbt:
            fw.op(POOL, lambda h, vb_=vb_: h.memset(vb_.t[:], 1.0), writes=[vb_.b])
        for qz_ in qz:
            fw.op(POOL, lambda h, qz_=qz_: h.memset(qz_.t[:], 0.0), writes=[qz_.b])
        def ld(tb, src, q=SP):
            fw.dma(q, lambda h: h.dma_start(out=tb.t[:], in_=src), writes=[tb.b], sembuf=tb.b)
        ld(c_f, cst_f)
        ld(c_b, cst_b)
        fw.dma(SP, lambda h: h.dma_start(out=vec.t[:], in_=vecs.rearrange("(c p) k -> p c k", p=128)), writes=[vec.b], sembuf=vec.b)
        ld(g_rep, grep)
        ld(rP, ropeP)
        ld(rS, ropeS)
        fw.dma(SP, lambda h: h.dma_start(out=lam_sb.t[0:1, 0:256], in_=lamv), writes=[lam_sb.b], sembuf=lam_sb.b)
        fw.dma(SP, lambda h: h.dma_start(out=pt_i.t[:], in_=ptab.partition_broadcast(128)), writes=[pt_i.b], sembuf=pt_i.b)
        wb = {}
        wlist = [("ci", w_ci, s_ci), ("co", w_co, s_co), ("fi0", w_fi[0], s_fi[0]), ("fo0", w_fo[0], s_fo[0]),
                 ("qkv", w_qkv, s_qkv), ("ao", w_ao, s_ao), ("fi1", w_fi[1], s_fi[1]), ("fo1", w_fo[1], s_fo[1])]
        for name, src, dst in wlist:
            wb[name] = Buf("w_" + name)

        def convert(names):
            for name, src, dst in wlist:
                if name not in names:
                    continue
                b = wb[name]
                nk = src.shape[0] // 128
                for k0 in range(0, nk, 4):
                    k1 = min(nk, k0 + 4)
                    fw.dma(POOL, lambda h, src=src, dst=dst, k0=k0, k1=k1: h.dma_start(
                        out=dst[k0 * 128:k1 * 128, :], in_=src[k0 * 128:k1 * 128, :]), writes=[b], sembuf=b)
        convert(["ci", "co", "fi0", "fo0", "qkv", "ao", "fi1", "fo1"])
        fw.op(DVE, lambda h: h.tensor_scalar(out=g_rep.t[:], in0=g_rep.t[:], scalar1=float(1.0 - LAM_INIT), scalar2=None, op0=ALU.mult),
              reads=[g_rep.b], writes=[g_rep.b])
        L = lam_sb.t
        fw.op(DVE, lambda h: h.tensor_tensor(out=L[0:1, 0:64], in0=L[0:1, 0:64], in1=L[0:1, 64:128], op=ALU.mult), reads=[lam_sb.b], writes=[lam_sb.b])
        fw.op(DVE, lambda h: h.tensor_tensor(out=L[0:1, 128:192], in0=L[0:1, 128:192], in1=L[0:1, 192:256], op=ALU.mult), reads=[lam_sb.b], writes=[lam_sb.b])
        fw.op(DVE, lambda h: h.reduce_sum(out=L[0:1, 256:257], in_=L[0:1, 0:64], axis=AX.X), reads=[lam_sb.b], writes=[lam_sb.b])
        fw.op(DVE, lambda h: h.reduce_sum(out=L[0:1, 257:258], in_=L[0:1, 128:192], axis=AX.X), reads=[lam_sb.b], writes=[lam_sb.b])
        fw.op(ACT, lambda h: h.activation(out=L[0:1, 256:258], in_=L[0:1, 256:258], func=AF.Exp), reads=[lam_sb.b], writes=[lam_sb.b])
        fw.op(DVE, lambda h: h.tensor_tensor(out=L[0:1, 258:259], in0=L[0:1, 256:257], in1=L[0:1, 257:258], op=ALU.subtract), reads=[lam_sb.b], writes=[lam_sb.b])
        fw.op(DVE, lambda h: h.tensor_scalar(out=L[0:1, 258:259], in0=L[0:1, 258:259], scalar1=float(LAM_INIT), scalar2=None, op0=ALU.add), reads=[lam_sb.b], writes=[lam_sb.b])
        fw.op(DVE, lambda h: h.tensor_scalar(out=L[0:1, 259:260], in0=L[0:1, 258:259], scalar1=-1.0, scalar2=None, op0=ALU.mult), reads=[lam_sb.b], writes=[lam_sb.b])
        b0 = next_mm()
        fw.op(PE, lambda h: h.matmul(b0.t[:, 0:2], lhsT=c_f.t[0:1, 780:908], rhs=L[0:1, 258:260], start=True, stop=True),
              reads=[c_f.b, lam_sb.b], writes=[b0.b])
        fw.op(DVE, lambda h: h.tensor_copy(out=lamb.t[:], in_=b0.t[:, 0:2]), reads=[b0.b], writes=[lamb.b])
        fw.op(DVE, lambda h: h.scalar_tensor_tensor(out=Cmat.t[:], in0=C2, scalar=lamb.t[:, 1:2], in1=C1, op0=ALU.mult, op1=ALU.add),
              reads=[c_f.b, lamb.b], writes=[Cmat.b])
        fw.op(DVE, lambda h: h.tensor_copy(out=pt_f.t[:], in_=pt_i.t[:]), reads=[pt_i.b], writes=[pt_f.b])
        fw.op(DVE, lambda h: h.tensor_scalar(out=pt_f.t[:], in0=pt_f.t[:], scalar1=128.0, scalar2=c_f.t[:, 264:265], op0=ALU.mult, op1=ALU.add),
              reads=[pt_f.b, c_f.b], writes=[pt_f.b])
        fw.op(DVE, lambda h: h.tensor_copy(out=pt_x.t[:], in_=pt_f.t[:]), reads=[pt_f.b], writes=[pt_x.b])

        plan = []
        pst = {"i": 0, "issued": 0}
        NS, LOOK = len(wsl), 2

        def issue_w(j):
            scr, wbuf, k0, nk, c0, ncols = plan[j]
            slot = wsl[j % NS]
            fw.dma(SP, lambda h: h.dma_start(
                out=slot.t[:, 0:nk * ncols].rearrange("p (k n) -> p k n", n=ncols),
                in_=scr[k0 * 128:(k0 + nk) * 128, c0:c0 + ncols].rearrange("(k p) n -> p k n", p=128)),
                reads=[wbuf], writes=[slot.b], sembuf=slot.b)

        def load_w(scr, wbuf, k0, nk, c0, ncols):
            if fw.dry:
                plan.append((scr, wbuf, k0, nk, c0, ncols))
                return wsl[0]
            i = pst["i"]
            pst["i"] += 1
            assert plan[i][2:] == (k0, nk, c0, ncols)
            while pst["issued"] < min(len(plan), i + 1 + LOOK):
                issue_w(pst["issued"])
                pst["issued"] += 1
            return wsl[i % NS]

        def linear_fm(inp, in_bufs, nkc, scr, wbuf, c0, noc_total, n, evac):
            for og in range(0, noc_total, 3):
                noc = min(3, noc_total - og)
                banks = [next_mm() for _ in range(noc)]
                for k0 in range(0, nkc, 8):
                    nk = min(8, nkc - k0)
                    slot = load_w(scr, wbuf, k0, nk, c0 + og * 128, noc * 128)
                    for j in range(noc):
                        for kk in range(nk):
                            kc = k0 + kk
                            fw.op(PE, lambda h, j=j, kk=kk, kc=kc, slot=slot, noc=noc: h.matmul(
                                banks[j].t[:, 0:n], lhsT=slot.t[:, kk * noc * 128 + j * 128: kk * noc * 128 + (j + 1) * 128],
                                rhs=inp(kc), start=(kc == 0), stop=(kc == nkc - 1)),
                                reads=[slot.b] + in_bufs, writes=[banks[j].b], signal=(kk == nk - 1))
                    tick()
                for j in range(noc):
                    evac(og + j, banks[j])

        def linear_tm(inp, in_bufs, nrows, ntb, scr, wbuf, c0, evac):
            slot = load_w(scr, wbuf, 0, KC, c0, 512)
            for tb in range(ntb):
                bank = next_mm()
                for kc in range(KC):
                    fw.op(PE, lambda h, kc=kc, tb=tb, bank=bank: h.matmul(
                        bank.t[0:nrows, :], lhsT=inp(kc, tb), rhs=slot.t[:, kc * 512:(kc + 1) * 512],
                        start=(kc == 0), stop=(kc == KC - 1)),
                        reads=[slot.b] + in_bufs, writes=[bank.b], signal=(kc == KC - 1))
                evac(tb, bank)

        def layernorm(y, n, gcol, bcol, outb, out_dram=None):
            s1, s2 = next_mm(), next_mm()
            for c in range(KC):
                a = ln_b1[c % 2]
                q = ln_b2[c % 2]
                fw.op(DVE, lambda h, c=c, a=a: h.tensor_copy(out=a.t[:, 0:n], in_=y.t[:, c, 0:n]), reads=[y.bs[c]], writes=[a.b])
                fw.op(ACT, lambda h, c=c, q=q: h.activation(out=q.t[:, 0:n], in_=y.t[:, c, 0:n], func=AF.Square), reads=[y.bs[c]], writes=[q.b])
                fw.op(PE, lambda h, c=c, a=a: h.matmul(s1.t[:, 0:n], lhsT=onesB, rhs=a.t[:, 0:n], start=(c == 0), stop=(c == KC - 1)),
                      reads=[a.b, c_b.b], writes=[s1.b])
                fw.op(PE, lambda h, c=c, q=q: h.matmul(s2.t[:, 0:n], lhsT=onesB, rhs=q.t[:, 0:n], start=(c == 0), stop=(c == KC - 1)),
                      reads=[q.b, c_b.b], writes=[s2.b])
            tick()
            m, q, r, nm = ln_m, ln_q, ln_r, ln_n
            fw.op(DVE, lambda h: h.tensor_scalar(out=m.t[:, 0:n], in0=s1.t[:, 0:n], scalar1=1.0 / D, scalar2=None, op0=ALU.mult), reads=[s1.b], writes=[m.b])
            fw.op(DVE, lambda h: h.tensor_tensor(out=q.t[:, 0:n], in0=m.t[:, 0:n], in1=m.t[:, 0:n], op=ALU.mult), reads=[m.b], writes=[q.b])
            fw.op(DVE, lambda h: h.scalar_tensor_tensor(out=q.t[:, 0:n], in0=s2.t[:, 0:n], scalar=1.0 / D, in1=q.t[:, 0:n], op0=ALU.mult, op1=ALU.subtract),
                  reads=[s2.b, q.b], writes=[q.b])
            fw.op(ACT, lambda h: h.activation(out=r.t[:, 0:n], in_=q.t[:, 0:n], func=AF.Sqrt, bias=eps_ln, scale=1.0), reads=[q.b, c_f.b], writes=[r.b])
            fw.op(DVE, lambda h: h.reciprocal(out=r.t[:, 0:n], in_=r.t[:, 0:n]), reads=[r.b], writes=[r.b])
            fw.op(DVE, lambda h: h.scalar_tensor_tensor(out=nm.t[:, 0:n], in0=m.t[:, 0:n], scalar=-1.0, in1=r.t[:, 0:n], op0=ALU.mult, op1=ALU.mult),
                  reads=[m.b, r.b], writes=[nm.b])
            for c in range(KC):
                fw.op(DVE, lambda h, c=c: h.tensor_tensor(out=y.t[:, c, 0:n], in0=y.t[:, c, 0:n], in1=r.t[:, 0:n], op=ALU.mult), reads=[y.bs[c], r.b], writes=[y.bs[c]])
                fw.op(DVE, lambda h, c=c: h.tensor_tensor(out=y.t[:, c, 0:n], in0=y.t[:, c, 0:n], in1=nm.t[:, 0:n], op=ALU.add), reads=[y.bs[c], nm.b], writes=[y.bs[c]])
                fw.op(ACT, lambda h, c=c: h.activation(out=y.t[:, c, 0:n], in_=y.t[:, c, 0:n], func=AF.Identity,
                                                       scale=vec.t[:, c, gcol:gcol + 1], bias=vec.t[:, c, bcol:bcol + 1]),
                      reads=[y.bs[c], vec.b], writes=[y.bs[c]])
                fw.op(ACT, lambda h, c=c: h.activation(out=outb.t[:, c, 0:n], in_=y.t[:, c, 0:n], func=AF.Copy), reads=[y.bs[c]], writes=[outb.b])
                if c % 2 == 1:
                    tick()

        def ffn(layer, inb, y, n):
            wfi, wfo = wb[f"fi{layer}"], wb[f"fo{layer}"]
            for j0 in range(0, FC, 2):
                sg = load_w(s_fi[layer], wfi, 0, KC, j0 * 128, 256)
                su = load_w(s_fi[layer], wfi, 0, KC, DFF + j0 * 128, 256)
                for jj in range(2):
                    j = j0 + jj
                    bg, bu = next_mm(), next_mm()
                    for kc in range(KC):
                        fw.op(PE, lambda h, kc=kc, jj=jj, bg=bg: h.matmul(bg.t[:, 0:n], lhsT=sg.t[:, kc * 256 + jj * 128: kc * 256 + (jj + 1) * 128],
                              rhs=inb.t[:, kc, 0:n], start=(kc == 0), stop=(kc == KC - 1)), reads=[sg.b, inb.b], writes=[bg.b], signal=(kc == KC - 1))
                    for kc in range(KC):
                        fw.op(PE, lambda h, kc=kc, jj=jj, bu=bu: h.matmul(bu.t[:, 0:n], lhsT=su.t[:, kc * 256 + jj * 128: kc * 256 + (jj + 1) * 128],
                              rhs=inb.t[:, kc, 0:n], start=(kc == 0), stop=(kc == KC - 1)), reads=[su.b, inb.b], writes=[bu.b], signal=(kc == KC - 1))
                    tt = t1 if j % 2 == 0 else t2
                    fw.op(ACT, lambda h, bg=bg, tt=tt: h.activation(out=tt.t[:, 0:n], in_=bg.t[:, 0:n], func=AF.Silu), reads=[bg.b], writes=[tt.b])
                    fw.op(DVE, lambda h, bu=bu, tt=tt, j=j: h.tensor_tensor(out=hb[j].t[:, 0:n], in0=bu.t[:, 0:n], in1=tt.t[:, 0:n], op=ALU.mult),
                          reads=[bu.b, tt.b], writes=[hb[j].b])
                    tick()

            def ev(oc, bank):
                fw.op(DVE, lambda h: h.scalar_tensor_tensor(out=y.t[:, oc, 0:n], in0=y.t[:, oc, 0:n], scalar=float(ALPHA), in1=bank.t[:, 0:n],
                                                            op0=ALU.mult, op1=ALU.add), reads=[y.bs[oc], bank.b], writes=[y.bs[oc]])
            linear_fm(lambda kc: hb[kc].t[:, 0:n], [h_.b for h_ in hb], FC, s_fo[layer], wfo, 0, KC, n, ev)

        def rope(tm, nrows, cos, sin):
            v = tm.t[0:nrows, :].rearrange("p (h d) -> p h d", d=64)
            x1 = v[:, :, 0:8]
            x2 = v[:, :, 8:16]
            cb = cos.unsqueeze(1).to_broadcast([nrows, 16, 8])
            sb = sin.unsqueeze(1).to_broadcast([nrows, 16, 8])
            a, b, c_, d_ = [r.t[0:nrows] for r in rt]
            rb = [r.b for r in rt]
            fw.op(DVE, lambda h: h.tensor_tensor(out=a, in0=x1, in1=cb, op=ALU.mult), reads=[tm.b, rP.b, rS.b], writes=[rb[0]])
            fw.op(DVE, lambda h: h.tensor_tensor(out=b, in0=x2, in1=sb, op=ALU.mult), reads=[tm.b, rP.b, rS.b], writes=[rb[1]])
            fw.op(DVE, lambda h: h.tensor_tensor(out=c_, in0=x2, in1=cb, op=ALU.mult), reads=[tm.b, rP.b, rS.b], writes=[rb[2]])
            fw.op(DVE, lambda h: h.tensor_tensor(out=d_, in0=x1, in1=sb, op=ALU.mult), reads=[tm.b, rP.b, rS.b], writes=[rb[3]])
            fw.op(DVE, lambda h: h.tensor_tensor(out=x1, in0=a, in1=b, op=ALU.subtract), reads=[rb[0], rb[1]], writes=[tm.b])
            fw.op(DVE, lambda h: h.tensor_tensor(out=x2, in0=c_, in1=d_, op=ALU.add), reads=[rb[2], rb[3]], writes=[tm.b])

        def qk_proj(inb, nrows, ntb, col0, cosf, sinf, dstT, out_dram):
            tms = {}
            deferred = []

            def ev_half(half):
                def ev(tb, bank):
                    tm = tmf[tb % 2]
                    fw.op(ACT, lambda h: h.activation(out=tm.t[0:nrows, half * 512:(half + 1) * 512], in_=bank.t[0:nrows, :], func=AF.Copy),
                          reads=[bank.b], writes=[tm.b])
                    if half == 1:
                        rope(tm, nrows, cosf(tb), sinf(tb))
                        if out_dram is not None:
                            fw.dma(SP, lambda h: h.dma_start(out=out_dram(tb), in_=tm.t[0:nrows, :]), reads=[tm.b], sembuf=tm.b, is_out=True)
                        tbf = tmb[tb % 2]
                        fw.op(DVE, lambda h: h.tensor_copy(out=tbf.t[0:nrows, :], in_=tm.t[0:nrows, :]), reads=[tm.b], writes=[tbf.b])
                        def do_tr(tb=tb, tbf=tbf):
                            for c in range(KC):
                                fw.op(PE, lambda h, c=c: h.transpose(trb.t[:, c * 128:c * 128 + nrows], tbf.t[0:nrows, c * 128:(c + 1) * 128], identB[0:nrows, 0:nrows]),
                                      reads=[tbf.b, c_b.b], writes=[trb.b])
                            fw.op(ACT, lambda h: h.activation(out=dstT.t[:, :, tb * nrows:(tb + 1) * nrows],
                                                              in_=trb.t[:, :].rearrange("p (c t) -> p c t", t=128)[:, :, 0:nrows], func=AF.Copy),
                                  reads=[trb.b], writes=[dstT.b])
                        deferred.append(do_tr)
                return ev
            for tb0 in range(0, ntb, 2):
                nt_ = min(2, ntb - tb0)
                for half in range(2):
                    slot = load_w(s_qkv, wb["qkv"], 0, KC, col0 + half * 512, 512)
                    for tb in range(tb0, tb0 + nt_):
                        bank = next_mm()
                        for kc in range(KC):
                            fw.op(PE, lambda h, kc=kc, tb=tb, bank=bank, slot=slot: h.matmul(
                                bank.t[0:nrows, :], lhsT=inb.t[:, kc, tb * nrows:(tb + 1) * nrows], rhs=slot.t[:, kc * 512:(kc + 1) * 512],
                                start=(kc == 0), stop=(kc == KC - 1)), reads=[slot.b, inb.b], writes=[bank.b], signal=(kc == KC - 1))
                        if half == 0 and tb == tb0:
                            while deferred:
                                deferred.pop(0)()
                        ev_half(half)(tb, bank)
                        tick()
            while deferred:
                deferred.pop(0)()

        def v_proj(inb, nrows, ntb, out_dram, vdst):
            for tb0 in range(0, ntb, 2):
                nt_ = min(2, ntb - tb0)
                for half in range(2):
                    slot = load_w(s_qkv, wb["qkv"], 0, KC, 2048 + half * 512, 512)
                    for tb in range(tb0, tb0 + nt_):
                        bank = next_mm()
                        for kc in range(KC):
                            fw.op(PE, lambda h, kc=kc, tb=tb, bank=bank, slot=slot: h.matmul(
                                bank.t[0:nrows, :], lhsT=inb.t[:, kc, tb * nrows:(tb + 1) * nrows], rhs=slot.t[:, kc * 512:(kc + 1) * 512],
                                start=(kc == 0), stop=(kc == KC - 1)), reads=[slot.b, inb.b], writes=[bank.b], signal=(kc == KC - 1))
                        tm = tmf[tb % 2]
                        fw.op(ACT, lambda h, tm=tm, bank=bank, half=half: h.activation(out=tm.t[0:nrows, half * 512:(half + 1) * 512], in_=bank.t[0:nrows, :], func=AF.Copy),
                              reads=[bank.b], writes=[tm.b])
                        if half == 1:
                            if out_dram is not None:
                                fw.dma(SP, lambda h, tm=tm, tb=tb: h.dma_start(out=out_dram(tb), in_=tm.t[0:nrows, :]), reads=[tm.b], sembuf=tm.b, is_out=True)
                            vdst(tb, tm)
                        tick()

        def layer0(n, nb, seg, x_src, halo_src, state_src, conv_out):
            fw.dma(SP, lambda h: h.dma_start(out=xf.t[:, :, 0:n], in_=x_src), writes=xf.bs, sembuf=xf.b)
            for c in range(KC):
                fw.op(ACT if c % 2 else DVE, (lambda h, c=c: h.activation(out=xb.t[:, c, 0:n], in_=xf.t[:, c, 0:n], func=AF.Copy)) if c % 2 else
                      (lambda h, c=c: h.tensor_copy(out=xb.t[:, c, 0:n], in_=xf.t[:, c, 0:n])), reads=[xf.bs[c]], writes=[xb.b])
            if halo_src is not None:
                fw.dma(SP, lambda h: h.dma_start(out=t2.t[:, 0:16].rearrange("p (c t) -> p c t", t=2), in_=halo_src), writes=[t2.b], sembuf=t2.b)
                fw.op(DVE, lambda h: h.tensor_copy(out=ln_b1[0].t[:, 0:16], in_=t2.t[:, 0:16]), reads=[t2.b], writes=[ln_b1[0].b])
            up3 = upad.t[:, 0:nb * (seg + 2)].rearrange("p (b s) -> p b s", s=seg + 2)
            for c in range(KC):
                if c % 4 == 0:
                    sl = [load_w(s_ci, wb["ci"], 0, KC, part * 1024 + c * 128, 512) for part in range(3)]
                j = c % 4
                banks = [next_mm() for _ in range(3)]
                for part in (1, 2, 0):
                    for kc in range(KC):
                        fw.op(PE, lambda h, kc=kc, part=part, j=j, sl=sl, banks=banks: h.matmul(
                            banks[part].t[:, 0:n], lhsT=sl[part].t[:, kc * 512 + j * 128: kc * 512 + (j + 1) * 128], rhs=xb.t[:, kc, 0:n],
                            start=(kc == 0), stop=(kc == KC - 1)), reads=[sl[part].b, xb.b], writes=[banks[part].b], signal=(kc == KC - 1))
                bgb, bgc, bh = banks
                fw.op(ACT, lambda h, bgc=bgc: h.activation(out=t1.t[:, 0:n], in_=bgc.t[:, 0:n], func=AF.Copy), reads=[bgc.b], writes=[t1.b])
                fw.op(ACT, lambda h, bgb=bgb: h.activation(out=t3.t[:, 0:n], in_=bgb.t[:, 0:n], func=AF.Copy), reads=[bgb.b], writes=[t3.b])
                fw.op(DVE, lambda h, bh=bh: h.tensor_tensor(out=up3[:, :, 2:seg + 2], in0=t1.t[:, 0:n].rearrange("p (b s) -> p b s", s=seg),
                                                            in1=bh.t[:, 0:n].rearrange("p (b s) -> p b s", s=seg), op=ALU.mult),
                      reads=[t1.b, bh.b], writes=[upad.b])
                if halo_src is not None:
                    bk = trbF
                    for part in (1, 2):
                        for kc in range(KC):
                            fw.op(PE, lambda h, kc=kc, part=part, j=j, sl=sl, bk=bk: h.matmul(
                                bk.t[:, (part - 1) * 2:(part - 1) * 2 + 2], lhsT=sl[part].t[:, kc * 512 + j * 128: kc * 512 + (j + 1) * 128],
                                rhs=ln_b1[0].t[:, kc * 2:kc * 2 + 2], start=(kc == 0), stop=(kc == KC - 1)),
                                reads=[sl[part].b, ln_b1[0].b], writes=[bk.b], signal=(kc == KC - 1))
                    fw.op(ACT, lambda h, bk=bk: h.activation(out=uhal.t[:, 0:2], in_=bk.t[:, 0:2], func=AF.Copy), reads=[bk.b], writes=[uhal.b])
                    fw.op(DVE, lambda h, bk=bk: h.tensor_tensor(out=upad.t[:, 0:2], in0=uhal.t[:, 0:2], in1=bk.t[:, 2:4], op=ALU.mult),
                          reads=[uhal.b, bk.b], writes=[upad.b])
                else:
                    fw.dma(SP, lambda h, c=c: h.dma_start(out=up3[:, :, 0:2], in_=state_src[c * 128:(c + 1) * 128, :].rearrange("p (b t) -> p b t", t=2)),
                           writes=[upad.b], sembuf=upad.b)
                tv = t2.t[:, 0:n].rearrange("p (b s) -> p b s", s=seg)
                fw.op(DVE, lambda h, c=c: h.tensor_scalar(out=tv, in0=up3[:, :, 0:seg], scalar1=vec.t[:, c, 0:1], scalar2=None, op0=ALU.mult),
                      reads=[upad.b, vec.b], writes=[t2.b])
                fw.op(DVE, lambda h, c=c: h.scalar_tensor_tensor(out=tv, in0=up3[:, :, 1:seg + 1], scalar=vec.t[:, c, 1:2], in1=tv, op0=ALU.mult, op1=ALU.add),
                      reads=[upad.b, vec.b, t2.b], writes=[t2.b])
                fw.op(DVE, lambda h, c=c: h.scalar_tensor_tensor(out=tv, in0=up3[:, :, 2:seg + 2], scalar=vec.t[:, c, 2:3], in1=tv, op0=ALU.mult, op1=ALU.add),
                      reads=[upad.b, vec.b, t2.b], writes=[t2.b])
                fw.op(DVE, lambda h, c=c: h.tensor_tensor(out=gbuf.t[:, c, 0:n], in0=t3.t[:, 0:n], in1=t2.t[:, 0:n], op=ALU.mult),
                      reads=[t3.b, t2.b], writes=[gbuf.b])
                tick()
                if conv_out is not None:
                    fw.op(DVE, lambda h, c=c: h.tensor_copy(out=cvo.t[:, c, 0:nb * 2].rearrange("p (b t) -> p b t", t=2), in_=up3[:, :, seg:seg + 2]),
                          reads=[upad.b], writes=[cvo.b])
            if conv_out is not None:
                fw.dma(SP, lambda h: h.dma_start(out=conv_out.rearrange("(c p) t -> p c t", p=128), in_=cvo.t[:, :, 0:nb * 2]), reads=[cvo.b], sembuf=cvo.b, is_out=True)

            def ev(oc, bank):
                fw.op(DVE, lambda h: h.scalar_tensor_tensor(out=xf.t[:, oc, 0:n], in0=xf.t[:, oc, 0:n], scalar=float(ALPHA), in1=bank.t[:, 0:n],
                                                            op0=ALU.mult, op1=ALU.add), reads=[xf.bs[oc], bank.b], writes=[xf.bs[oc]])
            linear_fm(lambda kc: gbuf.t[:, kc, 0:n], [gbuf.b], KC, s_co, wb["co"], 0, KC, n, ev)
            layernorm(xf, n, 3, 4, x1b)
            ffn(0, x1b, xf, n)
            layernorm(xf, n, 5, 6, xb)

        def layer1_tail(attT, n, out_dram):
            def ev(oc, bank):
                fw.op(DVE, lambda h: h.scalar_tensor_tensor(out=xf.t[:, oc, 0:n], in0=xf.t[:, oc, 0:n], scalar=float(ALPHA), in1=bank.t[:, 0:n],
                                                            op0=ALU.mult, op1=ALU.add), reads=[xf.bs[oc], bank.b], writes=[xf.bs[oc]])
            linear_fm(lambda kc: attT.t[:, kc, 0:n], [attT.b], KC, s_ao, wb["ao"], 0, KC, n, ev)
            layernorm(xf, n, 7, 8, x1b)
            ffn(1, x1b, xf, n)
            layernorm(xf, n, 9, 10, xb)
            fw.dma(SP, lambda h: h.dma_start(out=out_dram, in_=xf.t[:, :, 0:n]), reads=xf.bs, sembuf=xf.b, is_out=True)

        def emit_all():
            def phaseA_tile(ti):
                own = ti < NG
                conv_o = convp if ti == NG - 1 else None
                layer0(512, 1, 512, xT[:, ti * 512:(ti + 1) * 512].rearrange("(c p) t -> p c t", p=128),
                       xh[:, ti * 2:(ti + 1) * 2].rearrange("(c p) t -> p c t", p=128), None, conv_o)
                if own:
                    fw.dma(SP, lambda h, ti=ti: h.dma_start(out=s_x2[:, ti * 512:(ti + 1) * 512].rearrange("(c p) t -> p c t", p=128), in_=xf.t[:]),
                           reads=xf.bs, writes=[Buf("x2s%d" % ti)], sembuf=xf.b)
                qk_proj(xb, 128, 4, 1024, lambda tb, ti=ti: rP.t[:, ti * 4 + tb, 0:8], lambda tb, ti=ti: rP.t[:, ti * 4 + tb, 8:16], kTt,
                        (lambda tb, ti=ti: kp[ti * 512 + tb * 128: ti * 512 + (tb + 1) * 128, :]) if own else None)
                bkT = Buf("skT")
                fw.dma(SP, lambda h, ti=ti: h.dma_start(out=s_kT[:, ti * 512:(ti + 1) * 512].rearrange("(c p) t -> p c t", p=128), in_=kTt.t[:]),
                       reads=[kTt.b], writes=[bkT], sembuf=kTt.b)

                def vdst_p(tb, tm, ti=ti):
                    vb_ = vbt[tb % 2]
                    fw.op(ACT, lambda h: h.activation(out=vb_.t[:, :, 0:128], in_=tm.t[:, :].rearrange("p (g e) -> p g e", e=128), func=AF.Copy), reads=[tm.b], writes=[vb_.b])
                    r0 = ti * 512 + tb * 128
                    fw.dma(SP, lambda h: h.dma_start(out=s_v[r0:r0 + 128, :], in_=vb_.t[:].rearrange("p g e -> p (g e)")), reads=[vb_.b], writes=[Buf("sv")], sembuf=vb_.b)
                v_proj(xb, 128, 4, (lambda tb, ti=ti: vp[ti * 512 + tb * 128: ti * 512 + (tb + 1) * 128, :]) if own else None, vdst_p)
            phaseA_tile(0)
            layer0(32, 4, 8, xsT.rearrange("(c p) t -> p c t", p=128), None, sconvT, convs)
            fw.op(DVE, lambda h: h.tensor_copy(out=xs2.t[:], in_=xf.t[:, :, 0:32]), reads=xf.bs, writes=[xs2.b])
            qk_proj(xb, 32, 1, 0, lambda tb: rS.t[:, 0:8], lambda tb: rS.t[:, 8:16], qTs, None)
            qk_proj(xb, 32, 1, 1024, lambda tb: rS.t[:, 0:8], lambda tb: rS.t[:, 8:16], kTs, lambda tb: ks)

            def vdst_s(tb, tm):
                fw.op(POOL, lambda h: h.tensor_copy(out=vsb.t[:], in_=tm.t[0:32, :]), reads=[tm.b], writes=[vsb.b])
            v_proj(xb, 32, 1, lambda tb: vs, vdst_s)

            osum = ob[2]

            def sample_gen():
                for b in range(4):
                    fw.op(POOL, lambda h: h.memset(qblk.t[:], 0.0), writes=[qblk.b])
                    for c in range(KC):
                        for hh in range(2):
                            col = (2 * c + hh) * 8
                            fw.op(ACT, lambda h, c=c, hh=hh, col=col, b=b: h.activation(out=qblk.t[hh * 64:(hh + 1) * 64, c, col:col + 8],
                                  in_=qTs.t[hh * 64:(hh + 1) * 64, c, b * 8:(b + 1) * 8], func=AF.Copy), reads=[qTs.b], writes=[qblk.b])
                    yield
                    for t in range(-3, NPG + 2):
                        p = t
                        if 0 <= p <= NPG:
                            last = (p == NPG)
                            if not last:
                                kT_ = kTp[p % 2]
                                nk = 128
                                lhs = lambda c, kT_=kT_: kT_.t[:, c * 128:(c + 1) * 128]
                                lb = [kT_.b]
                            else:
                                nk = 32
                                lhs = lambda c: kTs.t[:, c, 0:32]
                                lb = [kTs.b]
                            for c in range(KC):
                                fw.op(PE, lambda h, c=c, lhs=lhs, nk=nk: h.matmul(sbank.t[0:nk, 0:128], lhsT=lhs(c), rhs=qblk.t[:, c, :],
                                      start=(c == 0), stop=(c == KC - 1)), reads=lb + [qblk.b], writes=[sbank.b], signal=(c == KC - 1))
                            pT_ = pTs[p % 2]
                            fw.op(ACT, lambda h, pT_=pT_, nk=nk: h.activation(out=pT_.t[0:nk, :], in_=sbank.t[0:nk, 0:128], func=AF.Exp, scale=float(SCALE)),
                                  reads=[sbank.b], writes=[pT_.b])
                            if last:
                                fw.op(DVE, lambda h, pT_=pT_, b=b: h.tensor_tensor(out=pT_.t[0:32, :], in0=pT_.t[0:32, :], in1=masknew[:, b * 128:(b + 1) * 128], op=ALU.mult),
                                      reads=[pT_.b, c_f.b], writes=[pT_.b])
                        p = t - 1
                        if 0 <= p <= NPG:
                            last = (p == NPG)
                            pT_ = pTs[p % 2]
                            nk = 32 if last else 128
                            vrhs = vsb if last else vpg[p % 3]
                            for half in range(2):
                                fw.op(PE, lambda h, half=half, pT_=pT_, vrhs=vrhs, nk=nk, p=p, last=last: h.matmul(ob[half].t[:, :], lhsT=pT_.t[0:nk, :],
                                      rhs=vrhs.t[0:nk, half * 512:(half + 1) * 512], start=(p == 0), stop=last), reads=[pT_.b, vrhs.b], writes=[ob[half].b], signal=False)
                            fw.op(PE, lambda h, pT_=pT_, nk=nk, p=p, last=last: h.matmul(osum.t[:, 0:1], lhsT=pT_.t[0:nk, :], rhs=onesB[0:nk, 0:1],
                                  start=(p == 0), stop=last), reads=[pT_.b, c_b.b], writes=[osum.b], signal=True)
                        p = t + 1
                        if 0 <= p < NPG:
                            kpg_ = kpg[p % 3]
                            kT_ = kTp[p % 2]
                            for c in range(KC):
                                fw.op(PE, lambda h, c=c, kpg_=kpg_: h.transpose(trb.t[:, c * 128:(c + 1) * 128], kpg_.t[:, c * 128:(c + 1) * 128], identB),
                                      reads=[kpg_.b, c_b.b], writes=[trb.b])
                            fw.op(DVE, lambda h, kT_=kT_: h.tensor_copy(out=kT_.t[:], in_=trb.t[:]), reads=[trb.b], writes=[kT_.b])
                        p = t + 1
                        if 0 <= p < NPG:
                            vpg_ = vpg[p % 3]
                            col = b * NPG + p
                            fw.dma(POOL, lambda h, vpg_=vpg_, col=col: h.indirect_dma_start(out=vpg_.t[:], out_offset=None, in_=cache_v,
                                   in_offset=bass.IndirectOffsetOnAxis(ap=pt_x.t[:, col:col + 1], axis=0)), reads=[pt_x.b], writes=[vpg_.b], sembuf=vpg_.b)
                        p = t + 3
                        if 0 <= p < NPG:
                            kpg_ = kpg[p % 3]
                            col = b * NPG + p
                            fw.dma(POOL, lambda h, kpg_=kpg_, col=col: h.indirect_dma_start(out=kpg_.t[:], out_offset=None, in_=cache_k,
                                   in_offset=bass.IndirectOffsetOnAxis(ap=pt_x.t[:, col:col + 1], axis=0)), reads=[pt_x.b], writes=[kpg_.b], sembuf=kpg_.b)
                        yield
                    for half in range(2):
                        fw.op(DVE, lambda h, half=half: h.tensor_tensor(out=od.t[:, half * 512:(half + 1) * 512].rearrange("p (g e) -> p g e", e=128),
                              in0=ob[half].t[:, :].rearrange("p (g e) -> p g e", e=128),
                              in1=mask3[:, half * 4:(half + 1) * 4].unsqueeze(2).to_broadcast([128, 4, 128]), op=ALU.mult),
                              reads=[ob[half].b, c_f.b], writes=[od.b])
                    fw.op(DVE, lambda h: h.tensor_reduce(out=orr.t[:], in_=od.t[:].rearrange("p (g e) -> p e g", e=128), op=ALU.add, axis=AX.X), reads=[od.b], writes=[orr.b])
                    fw.op(DVE, lambda h: h.reciprocal(out=sm.t[:, 0:1], in_=osum.t[:, 0:1]), reads=[osum.b], writes=[sm.b])
                    fw.op(DVE, lambda h: h.tensor_scalar(out=orr.t[:], in0=orr.t[:], scalar1=sm.t[:, 0:1], scalar2=None, op0=ALU.mult), reads=[orr.b, sm.b], writes=[orr.b])
                    yield
                    fw.op(PE, lambda h: h.matmul(sbank.t[0:64, 0:128], lhsT=Cmat.t[:], rhs=orr.t[:], start=True, stop=True), reads=[Cmat.b, orr.b], writes=[sbank.b])
                    fw.op(ACT, lambda h: h.activation(out=o_t.t[0:64, :], in_=sbank.t[0:64, 0:128], func=AF.Square, accum_out=sm.t[0:64, 1:2]),
                          reads=[sbank.b], writes=[o_t.b, sm.b])
                    fw.op(ACT, lambda h: h.activation(out=sm.t[0:64, 1:2], in_=sm.t[0:64, 1:2], func=AF.Sqrt, bias=eps_sub[0:64], scale=1.0 / 128), reads=[sm.b, c_f.b], writes=[sm.b])
                    fw.op(DVE, lambda h: h.reciprocal(out=sm.t[0:64, 1:2], in_=sm.t[0:64, 1:2]), reads=[sm.b], writes=[sm.b])
                    on_ = o_n[b % 2]
                    fw.op(DVE, lambda h, on_=on_: h.scalar_tensor_tensor(out=on_.t[0:64, :], in0=sbank.t[0:64, 0:128], scalar=sm.t[0:64, 1:2], in1=g_rep.t[0:64, :],
                          op0=ALU.mult, op1=ALU.mult), reads=[sbank.b, sm.b, g_rep.b], writes=[on_.b])
                    yield
                    fw.op(PE, lambda h, on_=on_: h.transpose(trb.t[:, 0:64], on_.t[0:64, :], identB[0:64, 0:64]), reads=[on_.b, c_b.b], writes=[trb.b])
                    fw.op(ACT, lambda h, b=b: h.activation(out=attS.t[:, :, b * 8:(b + 1) * 8], in_=trb.t[:, 0:64].rearrange("p (g q) -> p g q", q=8), func=AF.Copy),
                          reads=[trb.b], writes=[attS.b])
                    yield

            tk["gen"] = sample_gen()
            tick(4)

            for ti in range(1, NT):
                phaseA_tile(ti)
            while tk["gen"] is not None:
                tick()
            fw.op(DVE, lambda h: h.tensor_copy(out=xf.t[:, :, 0:32], in_=xs2.t[:]), reads=[xs2.b], writes=xf.bs)
            layer1_tail(attS, 32, ysT.rearrange("(c p) t -> p c t", p=128))
            if not fw.dry:
                for tb_ in [kTt] + vbt + [xf]:
                    s = tb_.b.dsem
                    if s is not None and SP.seen.get(s, 0) < s.cnt:
                        SP.h.wait_ge(s.h, s.cnt)
                        SP.seen[s] = s.cnt
                merge_bufs(kt[0].b, [x.b for x in (kpg[0], kpg[1], kpg[2], kTp[0])])
                merge_bufs(kt[1].b, [x.b for x in (vpg[0], vpg[1], vpg[2], kTp[1])])
                merge_bufs(osb1.b, [qblk.b, vsb.b])

            QT = kTt
            attT = x1b
            for I in range(NG):
                fw.dma(SP, lambda h, I=I: h.dma_start(out=xf.t[:], in_=s_x2[:, I * 512:(I + 1) * 512].rearrange("(c p) t -> p c t", p=128)), writes=xf.bs, sembuf=xf.b)
                for c in range(KC):
                    fw.op(ACT if c % 2 else DVE, (lambda h, c=c: h.activation(out=xb.t[:, c, :], in_=xf.t[:, c, :], func=AF.Copy)) if c % 2 else
                          (lambda h, c=c: h.tensor_copy(out=xb.t[:, c, :], in_=xf.t[:, c, :])), reads=[xf.bs[c]], writes=[xb.b])
                qk_proj(xb, 128, 4, 0, lambda tb, I=I: rP.t[:, I * 4 + tb, 0:8], lambda tb, I=I: rP.t[:, I * 4 + tb, 8:16], QT, None)
                nkb = 4 * (I + 1)
                def load_kv(hd, I=I, nkb=nkb):
                    kt_, vt_ = kt[hd % 2], vt[hd % 2]
                    for side in range(2):
                        t0 = side * NG * 512
                        fw.dma(SP, lambda h, side=side, t0=t0, kt_=kt_, hd=hd: h.dma_start(out=kt_.t[:, side, 0:nkb * 128], in_=s_kT[hd * 128:(hd + 1) * 128, t0:t0 + nkb * 128]),
                               writes=[kt_.b], sembuf=kt_.b)
                        fw.dma(SP, lambda h, side=side, t0=t0, vt_=vt_, hd=hd: h.dma_start(out=vt_.t[:, side * NKB:side * NKB + nkb, :],
                               in_=s_v[t0:t0 + nkb * 128, hd * 129:(hd + 1) * 129].rearrange("(k p) e -> p k e", p=128)), writes=[vt_.b], sembuf=vt_.b)

                def oacc(s, qs):
                    if qs < 3:
                        return ob[s], ob[s].t[:, qs * 129:(qs + 1) * 129]
                    return ob[2], ob[2].t[:, s * 129:(s + 1) * 129]

                def osb_ap(osb, s, qs):
                    if qs < 3:
                        return osb.t[:, s, qs * 129:(qs + 1) * 129]
                    return osb.t[:, 2, s * 129:(s + 1) * 129]

                def blocks_of(hd, I=I):
                    kt_, vt_ = kt[hd % 2], vt[hd % 2]
                    blocks = []
                    for side in range(2):
                        for kb in range(4 * I):
                            blocks.append((side, kb, "full", 0))
                    for kb in range(4):
                        blocks.append((1, 4 * I + kb, "pdiag", 0))
                    for kb in range(4):
                        blocks.append((0, 4 * I + kb, "odiag", kb))
                    for ob_ in ob:
                        fw.op(PE, lambda h, ob_=ob_: h.matmul(ob_.t[:, 0:512], lhsT=zer.t[:, 0:128], rhs=zer.t[:, 0:512], start=True, stop=False),
                              reads=[zer.b], writes=[ob_.b])
                    qz_ = qz[hd % 2]
                    fw.op(POOL, lambda h, qz_=qz_, hd=hd: h.tensor_copy(out=qz_.t[0:64, 0, :], in_=QT.t[0:64, hd, :]), reads=[QT.b], writes=[qz_.b])
                    fw.op(POOL, lambda h, qz_=qz_, hd=hd: h.tensor_copy(out=qz_.t[64:128, 1, :], in_=QT.t[64:128, hd, :]), reads=[QT.b], writes=[qz_.b])
                    steps = [(side, kb, kind, j0, s) for (side, kb, kind, j0) in blocks for s in range(2)]
                    pts = {}
                    SKEW = 2

                    def qk_exp(i):
                        side, kb, kind, j0, s = steps[i]
                        q0 = j0 * 128
                        bank = next_mm()
                        fw.op(PE, lambda h: h.matmul(
                            bank.t[:, q0:512], lhsT=kt_.t[:, side, kb * 128:(kb + 1) * 128], rhs=qz_.t[:, s, q0:512],
                            start=True, stop=True), reads=[kt_.b, qz_.b], writes=[bank.b])
                        pT_ = pT[state["k"] % 4]
                        state["k"] += 1
                        pts[i] = pT_
                        if kind == "pdiag":
                            fw.op(ACT, lambda h: h.activation(out=pT_.t[:, :], in_=bank.t[:, :], func=AF.Exp, scale=float(SCALE), bias=flag),
                                  reads=[bank.b, c_f.b], writes=[pT_.b])
                        else:
                            fw.op(ACT, lambda h: h.activation(out=pT_.t[:, q0:512], in_=bank.t[:, q0:512], func=AF.Exp, scale=float(SCALE)),
                                  reads=[bank.b], writes=[pT_.b])
                        if kind == "odiag":
                            fw.op(POOL, lambda h: h.tensor_tensor(out=pT_.t[:, q0:q0 + 128], in0=pT_.t[:, q0:q0 + 128], in1=tri, op=ALU.mult),
                                  reads=[pT_.b, c_b.b], writes=[pT_.b])

                    def pv(i):
                        side, kb, kind, j0, s = steps[i]
                        pT_ = pts.pop(i)
                        for qs in range(j0, 4):
                            ab, aap = oacc(s, qs)
                            stop_ = (kind == "odiag" and j0 == qs)
                            fw.op(PE, lambda h, aap=aap, qs=qs, stop_=stop_: h.matmul(
                                aap, lhsT=pT_.t[:, qs * 128:(qs + 1) * 128], rhs=vt_.t[:, side * NKB + kb, :], start=False, stop=stop_),
                                reads=[pT_.b, vt_.b], writes=[ab.b], signal=(qs == 3))

                    for i in range(len(steps) + SKEW):
                        if i < len(steps):
                            qk_exp(i)
                        if i - SKEW >= 0:
                            pv(i - SKEW)

                def evac_o(hd):
                    osb = osbs[hd % 2]
                    fw.op(ACT, lambda h: h.activation(out=osb.t[:, 0, :], in_=ob[0].t[:, 0:387], func=AF.Copy), reads=[ob[0].b], writes=[osb.b])
                    fw.op(ACT, lambda h: h.activation(out=osb.t[:, 1, :], in_=ob[1].t[:, 0:387], func=AF.Copy), reads=[ob[1].b], writes=[osb.b])
                    fw.op(ACT, lambda h: h.activation(out=osb.t[:, 2, 0:258], in_=ob[2].t[:, 0:258], func=AF.Copy), reads=[ob[2].b], writes=[osb.b])

                def finalize(hd):
                    osb = osbs[hd % 2]
                    for qs in range(4):
                        a0 = osb_ap(osb, 0, qs)
                        a1 = osb_ap(osb, 1, qs)
                        fw.op(DVE, lambda h, a0=a0: h.reciprocal(out=sm.t[:, 2:3], in_=a0[:, 128:129]), reads=[osb.b], writes=[sm.b])
                        fw.op(DVE, lambda h, a1=a1: h.reciprocal(out=sm.t[:, 3:4], in_=a1[:, 128:129]), reads=[osb.b], writes=[sm.b])
                        fw.op(DVE, lambda h: h.tensor_tensor(out=sm.t[:, 3:4], in0=sm.t[:, 3:4], in1=lamb.t[:, 1:2], op=ALU.mult), reads=[sm.b, lamb.b], writes=[sm.b])
                        fw.op(DVE, lambda h, a0=a0: h.tensor_scalar(out=o_t.t[:], in0=a0[:, 0:128], scalar1=sm.t[:, 2:3], scalar2=None, op0=ALU.mult),
                              reads=[osb.b, sm.b], writes=[o_t.b])
                        fw.op(DVE, lambda h, a1=a1, qs=qs: h.scalar_tensor_tensor(out=o_o4.t[:, qs, :], in0=a1[:, 0:128], scalar=sm.t[:, 3:4], in1=o_t.t[:], op0=ALU.mult, op1=ALU.add),
                              reads=[osb.b, sm.b, o_t.b], writes=[o_o4.b])
                        fw.op(DVE, lambda h, qs=qs: h.tensor_tensor(out=o_t.t[:], in0=o_o4.t[:, qs, :], in1=o_o4.t[:, qs, :], op=ALU.mult), reads=[o_o4.b], writes=[o_t.b])
                        fw.op(DVE, lambda h, qs=qs: h.reduce_sum(out=sm.t[:, 8 + qs:9 + qs], in_=o_t.t[:], axis=AX.X), reads=[o_t.b], writes=[sm.b])
                    fw.op(ACT, lambda h: h.activation(out=sm.t[:, 12:16], in_=sm.t[:, 8:12], func=AF.Sqrt, bias=eps_sub, scale=1.0 / 128), reads=[sm.b, c_f.b], writes=[sm.b])
                    fw.op(DVE, lambda h: h.reciprocal(out=sm.t[:, 12:16], in_=sm.t[:, 12:16]), reads=[sm.b], writes=[sm.b])
                    for qs in range(4):
                        fw.op(DVE, lambda h, qs=qs: h.scalar_tensor_tensor(out=on4.t[:, qs, :], in0=o_o4.t[:, qs, :], scalar=sm.t[:, 12 + qs:13 + qs], in1=g_rep.t[:], op0=ALU.mult, op1=ALU.mult),
                              reads=[o_o4.b, sm.b, g_rep.b], writes=[on4.b])
                    for qs in range(4):
                        fw.op(PE, lambda h, qs=qs: h.transpose(trb.t[:, qs * 128:(qs + 1) * 128], on4.t[:, qs, :], identB), reads=[on4.b, c_b.b], writes=[trb.b])
                    fw.op(DVE, lambda h, hd=hd: h.tensor_copy(out=attT.t[:, hd, :], in_=trb.t[:, 0:512]), reads=[trb.b], writes=[attT.b])

                load_kv(0)
                for hd in range(8):
                    if hd + 1 < 8:
                        load_kv(hd + 1)
                    blocks_of(hd)
                    evac_o(hd)
                    if hd > 0:
                        finalize(hd - 1)
                finalize(7)
                layer1_tail(attT, 512, yT[:, I * 512:(I + 1) * 512].rearrange("(c p) t -> p c t", p=128))

        fw.dry = True
        emit_all()
        fw.dry = False
        state["mm"] = 0
        state["ws"] = 0
        state["k"] = 0
        emit_all()
        assert pst["i"] == len(plan)
        fw.finish()
        print("bass program: n_inst", fw.n_inst, "nsem", fw.nsem)
    return nc


def _consts(parity):
    cf = np.zeros((128, 908), np.float32)
    cf[:, 780:908] = 1.0
    cf[:, 0:128] = np.eye(128, dtype=np.float32)
    for r in range(128):
        hp, q = r // 8, r % 8
        c = (hp // 2) * 8 + q
        if hp % 2 == 0:
            cf[r, 128 + c] = 1.0
        else:
            cf[r, 192 + c] = 1.0
        cf[r, 256 + hp // 2] = 1.0
    cf[:, 264] = np.arange(128, dtype=np.float32)
    cf[:, 265] = NEG if parity == 0 else 0.0
    cf[:, 266] = LN_EPS
    cf[:, 267] = SUB_EPS
    for b in range(4):
        for r in range(32):
            bb, t = r // 8, r % 8
            for c in range(128):
                q = c % 8
                cf[r, 268 + b * 128 + c] = 1.0 if (bb == b and t <= q) else 0.0
    cb = np.zeros((128, 384), np.float32)
    cb[:, 0:128] = np.eye(128)
    cb[:, 128:256] = 1.0
    k = np.arange(128)[:, None]
    q = np.arange(128)[None, :]
    cb[:, 256:384] = (k <= q).astype(np.float32)
    return cf, cb.astype(ml_dtypes.bfloat16)


def _rope_tab(pos):
    inv = np.power(np.float32(500000.0), -np.arange(0, 16, 2, dtype=np.float32) / np.float32(16)).astype(np.float32)
    ang = pos.astype(np.float32)[:, None] * inv[None, :]
    return np.concatenate([np.cos(ang), np.sin(ang)], axis=-1).astype(np.float32)


_CACHE = {}


def run(inputs, NG, NPG):
    f = lambda k: np.ascontiguousarray(np.asarray(inputs[k]))
    x_prompt, x_sample, state_conv = f("x_prompt"), f("x_sample"), f("state_conv")
    cache_k, cache_v, page_table = f("cache_k"), f("cache_v"), f("page_table")
    B, S, _ = x_prompt.shape
    NPOOL = cache_k.shape[0]
    assert S == 2 * NG * 512 and page_table.shape[1] == NPG and B == 4
    key = (NG, NPG, NPOOL)
    if key not in _CACHE:
        _CACHE[key] = build(NG, NPG, NPOOL)
    nc = _CACHE[key]
    ck = cache_k.reshape(NPOOL * 128, D)
    cv = cache_v.reshape(NPOOL * 128, D)
    vecs = np.concatenate([f("w_conv").T,
                           np.stack([f("ln_mix_g")[0], f("ln_mix_b")[0], f("ln_ffn_g")[0], f("ln_ffn_b")[0],
                                     f("ln_mix_g")[1], f("ln_mix_b")[1], f("ln_ffn_g")[1], f("ln_ffn_b")[1]], axis=1)], axis=1).astype(np.float32)
    lamv = np.concatenate([f("lambda_q1"), f("lambda_k1"), f("lambda_q2"), f("lambda_k2")]).reshape(1, 256).astype(np.float32)
    grep = np.broadcast_to(f("subln_g")[None, :], (128, 128)).astype(np.float32).copy()
    past_len = NPG * 128
    ropeS = _rope_tab(past_len + (np.arange(32) % 8))
    in_maps = []
    orders = []
    for core in range(8):
        b, par = core // 2, core % 2
        own = [2 * I + par for I in range(NG)]
        oth = [2 * I + (1 - par) for I in range(NG)]
        gran = own + oth
        orders.append(gran)
        tok = np.concatenate([np.arange(g * 512, (g + 1) * 512) for g in gran])
        xT = np.ascontiguousarray(x_prompt[b][tok].T)
        xh = np.zeros((D, 2 * len(gran)), np.float32)
        for i, g in enumerate(gran):
            if g > 0:
                xh[:, 2 * i:2 * i + 2] = x_prompt[b][g * 512 - 2:g * 512].T
        rp = _rope_tab(tok).reshape(len(gran) * 4, 128, 16).transpose(1, 0, 2)
        cf, cb = _consts(par)
        sb = slice(core * 4, core * 4 + 4)
        in_maps.append({
            "xT": xT, "xh": xh,
            "xsT": np.ascontiguousarray(x_sample[sb].reshape(32, D).T),
            "sconvT": np.ascontiguousarray(state_conv[sb].reshape(8, D).T),
            "cache_k": ck, "cache_v": cv,
            "ptab": np.ascontiguousarray(page_table[sb].reshape(1, 4 * NPG).astype(np.int32)),
            "w_ci": f("w_conv_in"), "w_co": f("w_conv_out"), "w_qkv": f("w_qkv"), "w_ao": f("w_attn_out"),
            "w_fi": f("w_ffn_in"), "w_fo": f("w_ffn_out"), "vecs": vecs, "lamv": lamv, "grep": grep,
            "ropeP": np.ascontiguousarray(rp), "ropeS": ropeS, "cst_f": cf, "cst_b": cb,
        })
    res = run_bass_kernel_spmd(nc, in_maps, core_ids=list(range(8))).results
    y_p = np.zeros((B, S, D), np.float32)
    k_p = np.zeros((B, S, 16, 64), np.float32)
    v_p = np.zeros((B, S, 8, 128), np.float32)
    conv_p = np.zeros((B, 2, D), np.float32)
    y_s = np.zeros((32, 8, D), np.float32)
    conv_s = np.zeros((32, 2, D), np.float32)
    k_s = np.zeros((32, 8, 16, 64), np.float32)
    v_s = np.zeros((32, 8, 8, 128), np.float32)
    for core in range(8):
        b, par = core // 2, core % 2
        r = res[core]
        for I in range(NG):
            g = 2 * I + par
            y_p[b, g * 512:(g + 1) * 512] = r["yT"][:, I * 512:(I + 1) * 512].T
            k_p[b, g * 512:(g + 1) * 512] = r["kp"][I * 512:(I + 1) * 512].reshape(512, 16, 64)
            v_p[b, g * 512:(g + 1) * 512] = r["vp"][I * 512:(I + 1) * 512].reshape(512, 8, 128)
        if par == 1:
            conv_p[b] = r["convp"].T
        sb = slice(core * 4, core * 4 + 4)
        y_s[sb] = r["ysT"].T.reshape(4, 8, D)
        conv_s[sb] = r["convs"].T.reshape(4, 2, D)
        k_s[sb] = r["ks"].reshape(4, 8, 16, 64)
        v_s[sb] = r["vs"].reshape(4, 8, 8, 128)
    return (y_p, y_s, conv_p, k_p, v_p, conv_s, k_s, v_s)


def kernel(**inputs):
    return run(inputs, 4, 64)
```
